# Optimizing a Trainium2 kernel written in Bass

```python
import jax, jax.numpy as jnp
from jax import lax
import numpy as np

D_MODEL = 1024
BATCH = 4
SEQ = 4096
DEPTH = 1

HEAD_DIM = 64
FOX_HEADS = 8
MOBA_HEADS = 8
FOX_WIDTH = FOX_HEADS * HEAD_DIM
MOBA_WIDTH = MOBA_HEADS * HEAD_DIM
ROPE_DIM = HEAD_DIM // 4
ROPE_THETA = 500000.0
FOX_Q_BLOCK = 128
FOX_FORGET_BIAS = 2.0
MOBA_BLOCK = 256
MOBA_TOPK = 3
MOBA_Q_CHUNK = 32
RMS_EPS = 1e-6
IN_SPLITS = (FOX_WIDTH, FOX_WIDTH, FOX_WIDTH, FOX_WIDTH,
             MOBA_WIDTH, MOBA_WIDTH, MOBA_WIDTH, MOBA_WIDTH,
             D_MODEL, D_MODEL, FOX_HEADS)
IN_WIDTH = 4 * FOX_WIDTH + 4 * MOBA_WIDTH + 2 * D_MODEL + FOX_HEADS

kernel_name = "fox_moba_gated_hybrid"


def rms_norm(x, g):
    xf = x.astype(jnp.float32)
    y = xf * lax.rsqrt(jnp.mean(xf * xf, axis=-1, keepdims=True) + RMS_EPS)
    return (y * g.astype(jnp.float32)).astype(x.dtype)


def split_cols(t, sizes):
    outs, off = [], 0
    for size in sizes:
        outs.append(t[..., off:off + size])
        off += size
    return outs


def to_heads(t, n_heads):
    b, s, _ = t.shape
    return t.reshape(b, s, n_heads, HEAD_DIM).transpose(0, 2, 1, 3)


def from_heads(t):
    b, h, s, d = t.shape
    return t.transpose(0, 2, 1, 3).reshape(b, s, h * d)


def partial_rope(t, positions):
    half = ROPE_DIM // 2
    inv_freq = ROPE_THETA ** (-jnp.arange(0, half, dtype=jnp.float32) * 2.0 / ROPE_DIM)
    ang = positions.astype(jnp.float32)[:, None] * inv_freq[None, :]
    cos, sin = jnp.cos(ang), jnp.sin(ang)
    tf = t.astype(jnp.float32)
    x1, x2, rest = tf[..., :half], tf[..., half:ROPE_DIM], tf[..., ROPE_DIM:]
    rot = jnp.concatenate([x1 * cos - x2 * sin, x2 * cos + x1 * sin, rest], axis=-1)
    return rot.astype(t.dtype)


def fox_attention(q, k, v, log_f):
    b, h, s, d = q.shape
    scale = d ** -0.5
    c = jnp.cumsum(log_f, axis=-1)
    kpos = jnp.arange(s)
    n_blocks = s // FOX_Q_BLOCK

    def one_block(i):
        start = i * FOX_Q_BLOCK
        qb = lax.dynamic_slice_in_dim(q, start, FOX_Q_BLOCK, axis=2)
        cb = lax.dynamic_slice_in_dim(c, start, FOX_Q_BLOCK, axis=2)
        logits = jnp.einsum("bhqd,bhkd->bhqk", qb, k).astype(jnp.float32) * scale
        logits = logits + cb[..., :, None] - c[..., None, :]
        qpos = start + jnp.arange(FOX_Q_BLOCK)
        logits = jnp.where(kpos[None, :] <= qpos[:, None], logits, -jnp.inf)
        p = jax.nn.softmax(logits, axis=-1)
        return jnp.einsum("bhqk,bhkd->bhqd", p.astype(v.dtype), v)

    out = lax.map(one_block, jnp.arange(n_blocks))
    return out.transpose(1, 2, 0, 3, 4).reshape(b, h, s, d)


def moba_attention(q, k, v):
    b, h, s, d = q.shape
    scale = d ** -0.5
    n_kb = -(-s // MOBA_BLOCK)
    s_pad = n_kb * MOBA_BLOCK
    pad = ((0, 0), (0, 0), (0, s_pad - s), (0, 0))
    q, k, v = jnp.pad(q, pad), jnp.pad(k, pad), jnp.pad(v, pad)
    kb = k.reshape(b, h, n_kb, MOBA_BLOCK, d)
    vb = v.reshape(b, h, n_kb, MOBA_BLOCK, d)
    k_mean = jnp.mean(kb.astype(jnp.float32), axis=3)
    n_sel = min(MOBA_TOPK, n_kb)
    blk_ids = jnp.arange(n_kb)
    b_ix = jnp.arange(b)[:, None, None, None]
    h_ix = jnp.arange(h)[None, :, None, None]
    n_chunks = s_pad // MOBA_Q_CHUNK

    def one_chunk(ci):
        start = ci * MOBA_Q_CHUNK
        own = start // MOBA_BLOCK
        qc = lax.dynamic_slice_in_dim(q, start, MOBA_Q_CHUNK, axis=2)
        gate = jnp.einsum("bhtd,bhnd->bhtn", qc.astype(jnp.float32), k_mean)
        gate = jnp.where(blk_ids < own, gate, -jnp.inf)
        _, sel = lax.top_k(gate, n_sel)
        valid = sel < own
        k_sel = kb[b_ix, h_ix, sel]
        v_sel = vb[b_ix, h_ix, sel]
        s_sel = jnp.einsum("bhtd,bhtnld->bhtnl", qc, k_sel).astype(jnp.float32) * scale
        s_sel = jnp.where(valid[..., None], s_sel, -jnp.inf)
        s_sel = s_sel.reshape(b, h, MOBA_Q_CHUNK, n_sel * MOBA_BLOCK)
        k_own = lax.dynamic_index_in_dim(kb, own, axis=2, keepdims=False)
        v_own = lax.dynamic_index_in_dim(vb, own, axis=2, keepdims=False)
        s_own = jnp.einsum("bhtd,bhld->bhtl", qc, k_own).astype(jnp.float32) * scale
        qpos = start + jnp.arange(MOBA_Q_CHUNK)
        kpos = own * MOBA_BLOCK + jnp.arange(MOBA_BLOCK)
        s_own = jnp.where(kpos[None, :] <= qpos[:, None], s_own, -jnp.inf)
        p = jax.nn.softmax(jnp.concatenate([s_sel, s_own], axis=-1), axis=-1).astype(v.dtype)
        p_sel = p[..., :n_sel * MOBA_BLOCK].reshape(b, h, MOBA_Q_CHUNK, n_sel, MOBA_BLOCK)
        p_own = p[..., n_sel * MOBA_BLOCK:]
        return (jnp.einsum("bhtnl,bhtnld->bhtd", p_sel, v_sel)
                + jnp.einsum("bhtl,bhld->bhtd", p_own, v_own))

    out = lax.map(one_chunk, jnp.arange(n_chunks))
    out = out.transpose(1, 2, 0, 3, 4).reshape(b, h, s_pad, d)
    return out[:, :, :s]


def setup_inputs(seed: int = 0) -> dict:
    key = jax.random.key(seed)
    ks = jax.random.split(key, 12)
    f32 = jnp.float32
    x = jax.random.normal(ks[0], (BATCH, SEQ, D_MODEL), f32)
    norm_g = 1.0 + 0.02 * jax.random.normal(ks[1], (DEPTH, D_MODEL), f32)
    w_in = jax.random.normal(ks[2], (DEPTH, D_MODEL, IN_WIDTH), f32) * D_MODEL ** -0.5
    b_f = FOX_FORGET_BIAS + 0.1 * jax.random.normal(ks[3], (DEPTH, FOX_HEADS), f32)
    b_gate = 0.02 * jax.random.normal(ks[4], (DEPTH, 2, D_MODEL), f32)
    fox_q_g = 1.0 + 0.02 * jax.random.normal(ks[5], (DEPTH, HEAD_DIM), f32)
    fox_k_g = 1.0 + 0.02 * jax.random.normal(ks[6], (DEPTH, HEAD_DIM), f32)
    moba_q_g = 1.0 + 0.02 * jax.random.normal(ks[7], (DEPTH, HEAD_DIM), f32)
    moba_k_g = 1.0 + 0.02 * jax.random.normal(ks[8], (DEPTH, HEAD_DIM), f32)
    w_fox = jax.random.normal(ks[9], (DEPTH, FOX_WIDTH, D_MODEL), f32) * FOX_WIDTH ** -0.5
    w_moba = jax.random.normal(ks[10], (DEPTH, MOBA_WIDTH, D_MODEL), f32) * MOBA_WIDTH ** -0.5
    w_out = jax.random.normal(ks[11], (DEPTH, D_MODEL, D_MODEL), f32) * D_MODEL ** -0.5
    return {"x": x, "norm_g": norm_g, "w_in": w_in, "b_f": b_f, "b_gate": b_gate,
            "fox_q_g": fox_q_g, "fox_k_g": fox_k_g, "moba_q_g": moba_q_g,
            "moba_k_g": moba_k_g, "w_fox": w_fox, "w_moba": w_moba, "w_out": w_out}


def reference(x, norm_g, w_in, b_f, b_gate, fox_q_g, fox_k_g, moba_q_g, moba_k_g,
              w_fox, w_moba, w_out):
    seq = x.shape[1]
    positions = jnp.arange(seq, dtype=jnp.int32)
    for layer in range(DEPTH):
        h = rms_norm(x, norm_g[layer])
        proj = h @ w_in[layer]
        fq, fk, fv, fz, mq, mk, mv, mz, ga, gb, fl = split_cols(proj, IN_SPLITS)
        q = rms_norm(to_heads(fq, FOX_HEADS), fox_q_g[layer])
        k = rms_norm(to_heads(fk, FOX_HEADS), fox_k_g[layer])
        v = to_heads(fv, FOX_HEADS)
        log_f = jax.nn.log_sigmoid((fl + b_f[layer]).astype(jnp.float32)).transpose(0, 2, 1)
        y_fox = from_heads(fox_attention(q, k, v, log_f)) * jax.nn.silu(fz)
        q = partial_rope(rms_norm(to_heads(mq, MOBA_HEADS), moba_q_g[layer]), positions)
        k = partial_rope(rms_norm(to_heads(mk, MOBA_HEADS), moba_k_g[layer]), positions)
        v = to_heads(mv, MOBA_HEADS)
        y_moba = from_heads(moba_attention(q, k, v)) * jax.nn.silu(mz)
        merged = (jax.nn.sigmoid(ga + b_gate[layer, 0]) * (y_fox @ w_fox[layer])
                  + jax.nn.sigmoid(gb + b_gate[layer, 1]) * (y_moba @ w_moba[layer]))
        x = x + merged @ w_out[layer]
    return x
```

```python
import numpy as np
import ml_dtypes
import concourse.bass as bass
import concourse.mybir as mybir
from concourse.bass_utils import run_bass_kernel_spmd

F32 = mybir.dt.float32
BF16 = mybir.dt.bfloat16
AF = mybir.ActivationFunctionType
ALU = mybir.AluOpType
AX = mybir.AxisListType

D = 1024
SEQ = 4096
NBATCH = 4
TS = 512
NT = 8
NOWN = 4
HD = 64
INW = 6152
BIG = 30000.0
EPS = 1e-6
PI = [[0, 3, 4, 7, 1, 2, 5, 6], [1, 2, 5, 6, 0, 3, 4, 7]]
ROPE_DIM = 16
ROPE_THETA = 500000.0


def past_lists():
    out = []
    for j in range(NOWN):
        s = set()
        for p in range(2):
            for i in range(NT):
                if PI[p][i] < PI[p][j]:
                    s.add(i)
        out.append(sorted(s))
    return out


class Rec:
    ENGS = ("pe", "act", "dve", "pool", "sp")

    def __init__(self):
        self.ops = []
        self.lastw = {}
        self.readers = {}

    def add(self, eng, fn, reads=(), writes=(), dma=False):
        oid = len(self.ops)
        deps = set()
        for k in reads:
            if k in self.lastw:
                deps.add(self.lastw[k])
        for k in writes:
            if k in self.lastw:
                deps.add(self.lastw[k])
            for r in self.readers.get(k, {}).values():
                deps.update(r)
        for k in reads:
            d = self.readers.setdefault(k, {})
            if dma:
                d.setdefault("dma", []).append(oid)
            else:
                d[eng] = [oid]
        for k in writes:
            self.lastw[k] = oid
            self.readers[k] = {}
        self.ops.append(dict(eng=eng, fn=fn, deps=deps, dma=dma, inc=False))
        return oid

    def barrier(self):
        last = {}
        dmas = []
        for oid, op in enumerate(self.ops):
            if op.get("bar"):
                continue
            if op["dma"]:
                dmas.append(oid)
            else:
                last[op["eng"]] = oid
        deps = set(last.values()) | set(dmas)
        sp_id = len(self.ops)
        self.ops.append(dict(eng="sp", fn=(lambda e: e.sem_inc(self._sp_sem, 1)), deps=set(deps), dma=False, inc=True,
                             bar=True, selfinc=True))
        for e in self.ENGS:
            if e != "sp":
                self.ops.append(dict(eng=e, fn=None, deps={sp_id}, dma=False, inc=False, bar=True))

    def emit(self, nc, nsem_sp=24, nsem_pool=12):
        ops = self.ops
        for op in ops:
            for d in op["deps"]:
                if not ops[d]["dma"]:
                    ops[d]["inc"] = True
        cnt = {e: 0 for e in self.ENGS}
        dcount = {"sp": 0, "pool": 0, "act": 0}
        nsem = {"sp": nsem_sp, "pool": nsem_pool, "act": 4}
        for op in ops:
            e = op["eng"]
            if op["dma"]:
                k = dcount[e]
                dcount[e] += 1
                op["dsem"] = (e, k % nsem[e])
                op["dtarget"] = 16 * (k // nsem[e] + 1)
            elif op["inc"]:
                cnt[e] += 1
                op["val"] = cnt[e]
        import contextlib
        with contextlib.ExitStack() as es:
            sems = {e: es.enter_context(nc.semaphore("s_" + e)) for e in ("pe", "act", "dve", "pool", "sp")}
            self._sp_sem = sems["sp"]
            dsems = {}
            for q in ("sp", "pool"):
                if dcount[q]:
                    for i in range(min(nsem[q], dcount[q])):
                        dsems[(q, i)] = es.enter_context(nc.semaphore("d_%s%d" % (q, i)))
            block = es.enter_context(nc.Block())
            handles = {"pe": block.tensor, "act": block.scalar, "dve": block.vector,
                       "pool": block.gpsimd, "sp": block.sync}
            final_d = {}
            for op in ops:
                if op["dma"]:
                    final_d[op["dsem"]] = max(final_d.get(op["dsem"], 0), op["dtarget"])

            def run_engine(e):
                def body(eng):
                    seen = {}

                    def wait(sem_key, sem, val):
                        if seen.get(sem_key, 0) >= val:
                            return
                        seen[sem_key] = val
                        eng.wait_ge(sem, val)

                    for op in ops:
                        if op["eng"] != e:
                            continue
                        for d in sorted(op["deps"]):
                            dop = ops[d]
                            if dop["dma"]:
                                wait(dop["dsem"], dsems[dop["dsem"]], dop["dtarget"])
                            else:
                                if dop["eng"] == "pe" and e == "pe" and not op["dma"]:
                                    continue
                                wait(dop["eng"], sems[dop["eng"]], dop["val"])
                        if op["fn"] is None:
                            continue
                        if op["dma"]:
                            if op["dtarget"] > 16:
                                wait(op["dsem"], dsems[op["dsem"]], op["dtarget"] - 16)
                            inst = op["fn"](eng)
                            inst.then_inc(dsems[op["dsem"]], 16)
                        else:
                            inst = op["fn"](eng)
                            if op["inc"] and not op.get("selfinc"):
                                inst.then_inc(sems[e], 1)
                    if e == "sp":
                        for k, v in final_d.items():
                            wait(k, dsems[k], v)
                        for x in ("pe", "act", "dve", "pool"):
                            if cnt[x]:
                                wait(x, sems[x], cnt[x])
                return body

            for e in self.ENGS:
                handles[e](run_engine(e))


def storage_pos(p):
    return np.concatenate([np.arange(TS) + PI[p][i] * TS for i in range(NT)])


def const_tables(p):
    bf = ml_dtypes.bfloat16
    pos = storage_pos(p)
    t = {}
    tile_of = np.arange(SEQ) // TS
    kac = np.zeros((16, SEQ), np.float32)
    kac[3:6] = 1.0
    for j in range(NOWN):
        kac[6 + j] = np.where(np.array(PI[p])[tile_of] <= PI[p][j], 0.0, -BIG)
    qac = np.zeros((16, NOWN * TS), np.float32)
    qac[0:3] = 1.0
    for j in range(NOWN):
        qac[6 + j] = (tile_of[:NOWN * TS] == j).astype(np.float32)
    kam = np.zeros((16, SEQ), np.float32)
    blk = np.arange(SEQ) // 256
    for j in range(16):
        kam[j] = (blk == j).astype(np.float32)
    t["kac"] = kac.astype(bf)
    t["qac"] = qac.astype(bf)
    t["kam"] = kam.astype(bf)
    act_blk = np.array([PI[p][j // 2] * 2 + j % 2 for j in range(16)])
    mt = np.zeros((3, 16, 2, 16), np.float32)
    for st in range(16):
        own = st // 2
        valid = (act_blk < act_blk[own]).astype(np.float32)
        mt[0, st, :, :] = (valid - 1.0) * BIG
        mt[1, st, :, :] = valid
        mt[2, st, :, own] = 1.0
    t["mtab"] = np.ascontiguousarray(np.broadcast_to(mt.reshape(1, 3, 512), (128, 3, 512))).astype(bf)
    half = ROPE_DIM // 2
    inv_freq = (np.float32(ROPE_THETA) ** (-np.arange(0, half, dtype=np.float32) * np.float32(2.0) / np.float32(ROPE_DIM))).astype(np.float32)
    ang = (pos.astype(np.float32)[:, None] * inv_freq[None, :]).astype(np.float32)
    cos = np.ones((128, SEQ), np.float32)
    sin = np.zeros((128, SEQ), np.float32)
    for r in range(128):
        d = r % HD
        if d < ROPE_DIM:
            cos[r] = np.cos(ang[:, d % half].astype(np.float64)).astype(np.float32)
            sin[r] = np.sin(ang[:, d % half].astype(np.float64)).astype(np.float32)
    t["cos"] = cos
    t["sin"] = sin
    cm = np.zeros((128, 8, 128), np.float32)
    cm[:, 0, :] = np.eye(128)
    s_idx = np.arange(128)[:, None]
    t_idx = np.arange(128)[None, :]
    cm[:, 1, :] = np.where(s_idx > t_idx, -BIG, 0.0)
    cm[:, 2, :] = 1.0 / 1024.0
    bd = (s_idx // HD == t_idx // HD).astype(np.float32)
    cm[:, 3, :] = bd / 64.0
    rt = np.zeros((128, 128), np.float32)
    for m in range(128):
        d = m % HD
        if d < half:
            rt[m + half, m] = -1.0
        elif d < ROPE_DIM:
            rt[m - half, m] = 1.0
    cm[:, 4, :] = rt
    cm[:, 5, :] = bd
    cm[:, 6, :] = 1.0
    cm[64, 7, :] = 1.0
    t["cmat"] = cm.astype(bf)
    dm = np.zeros((128, 4, 512), np.float32)
    for s4 in range(4):
        dm[:, s4, :] = np.where((s4 * 128 + np.arange(128))[:, None] > np.arange(512)[None, :], -BIG, 0.0)
    t["dmask"] = dm.astype(bf)
    ao = np.zeros((8, 8, 8), np.float32)
    for i in range(8):
        for j in range(8):
            ao[:, i, j] = 1.0 if PI[p][j] < PI[p][i] else 0.0
    t["aoff"] = ao
    return t


def build_program(stop_after=None, dbg=()):
    nc = bass.Bass("TRN2", target_bir_lowering=False)
    R = Rec()

    def din(name, shape, dt=F32):
        return nc.dram_tensor(name, list(shape), dt, kind="ExternalInput").ap()

    xT = din("xT", [D, SEQ])
    xo = din("xo", [NOWN * TS, D])
    w_in = din("w_in", [D, INW])
    w_fox = din("w_fox", [512, D])
    w_moba = din("w_moba", [512, D])
    w_out = din("w_out", [D, D])
    gng_d = din("gng", [128, 8])
    gains_d = din("gains", [128, 4])
    bgate_d = din("bgate", [128, 16])
    bf_d = din("bf", [8, 1])
    kac_d = din("kac", [16, SEQ], BF16)
    qac_d = din("qac", [16, NOWN * TS], BF16)
    kam_d = din("kam", [16, SEQ], BF16)
    mtab_d = din("mtab", [128, 3, 512], BF16)
    cos_d = din("cos", [128, SEQ])
    sin_d = din("sin", [128, SEQ])
    cmat_d = din("cmat", [128, 8, 128], BF16)
    aoff_d = din("aoff", [8, 8, 8])
    dmask_d = din("dmask", [128, 4, 512], BF16)
    out_d = nc.dram_tensor("out", [NOWN * TS, D], F32, kind="ExternalOutput").ap()
    cscr = nc.dram_tensor("cscr", [8, 6, SEQ], BF16).ap()
    dbg_out = {}
    for name, shape, dt in dbg:
        dbg_out[name] = nc.dram_tensor(name, list(shape), dt, kind="ExternalOutput").ap()

    cur = [16640]
    OFFS = {}

    def alloc(name, shape, dt):
        sz = int(np.prod(shape[1:])) * (2 if dt == BF16 else 4)
        off = (cur[0] + 31) // 32 * 32
        t = nc.alloc_sbuf_tensor_at(name, list(shape), dt, offset=off)
        cur[0] = off + sz
        OFFS[name] = (off, off + sz)
        return t

    HT = alloc("HT", [128, 8, SEQ], BF16)
    YT = alloc("YT", [128, 8, NOWN * TS], BF16)
    CM = alloc("CM", [128, 8, 128], BF16)
    GNG = alloc("GNG", [128, 8], F32)
    GAINS = alloc("GAINS", [128, 4], F32)
    BGATE = alloc("BGATE", [128, 16], F32)
    BFT = alloc("BFT", [128, 2], F32)
    EPSC = alloc("EPSC", [128, 2], F32)
    phase_base = cur[0]
    IDENT = CM[:, 0, :]
    TRI = CM[:, 1, :]
    ONESM = CM[:, 2, :]
    BD = CM[:, 3, :]
    RT = CM[:, 4, :]
    BDQ = CM[:, 5, :]
    ONES1 = CM[:, 6, :]
    ONESEL = CM[:, 7, :]

    PS = [nc.alloc_psum_tensor("ps%d" % i, [128, 1024], F32) for i in range(4)]

    def bank(b):
        return PS[b // 2][:, (b % 2) * 512:(b % 2 + 1) * 512]

    def dma(q, out, in_, reads=(), writes=()):
        return R.add(q, lambda e: e.dma_start(out=out, in_=in_), reads=reads, writes=writes, dma=True)

    dma("sp", CM[:], cmat_d, writes=["CM"])
    dma("sp", GNG[:], gng_d, writes=["GNG"])
    dma("sp", GAINS[:], gains_d, writes=["GAINS"])
    dma("sp", BGATE[:], bgate_d, writes=["BGATE"])
    dma("sp", BFT[0:8, 0:1], bf_d, writes=["BFT"])
    R.add("dve", lambda e: e.memset(EPSC[:, 0:1], EPS), writes=["EPSC"])
    R.add("dve", lambda e: e.memset(EPSC[:, 1:2], EPS * 64.0), writes=["EPSC"])

    if stop_after == "c0":
        R.add("dve", lambda e: e.memset(HT[:, 0, 0:512], 1.0), writes=[("HT", 0)])
        R.barrier()
        dma("sp", dbg_out["HT"][:, 0, 0:512], HT[:, 0, 0:512], reads=[("HT", 0)])
        R.emit(nc)
        return nc
    cur[0] = phase_base
    KT = [alloc("KA", [128, SEQ], BF16), alloc("KB", [128, SEQ], BF16)]
    QT = [alloc("QA", [128, NOWN * TS], BF16), alloc("QB", [128, NOWN * TS], BF16)]
    VT = [alloc("VA", [128, 32, 66], BF16), alloc("VB", [128, 32, 128], BF16)]
    ZS = alloc("ZS", [128, NOWN * TS], BF16)
    WQ = alloc("WQ", [128, 8, 128], BF16)
    WK = alloc("WK", [128, 8, 128], BF16)
    WZ = alloc("WZ", [128, 8, 128], BF16)
    WV = alloc("WV", [128, 8, 128], BF16)
    PT = [alloc("PT%d" % i, [128, 1024], BF16) for i in range(3)]
    SQ1 = [alloc("SQ1_%d" % i, [128, TS], BF16) for i in range(2)]
    RS = [alloc("RS%d" % i, [128, TS], F32) for i in range(2)]
    T1 = [alloc("T1_%d" % i, [128, TS], F32) for i in range(2)]
    _rec = alloc("REC", [128, TS], F32)
    REC = [_rec, _rec]
    RH = [alloc("RHA", [128, TS], BF16), alloc("RHB", [128, TS], BF16)]
    RL = [alloc("RLA", [128, TS], BF16), alloc("RLB", [128, TS], BF16)]
    _ytmp = alloc("YTMP", [128, TS], F32)
    YTMP = [_ytmp, _ytmp]
    DMASK = alloc("DMASK", [128, 4, 512], BF16)
    moba_base = cur[0]
    ABT = [[alloc("ABT%d_%d" % (i, k), [128, TS], BF16) for k in range(2)] for i in range(2)]
    RCT = [[alloc("RCT%d_%d" % (i, k), [128, TS], F32) for k in range(2)] for i in range(2)]
    COST = [alloc("COST%d" % i, [128, TS], F32) for i in range(2)]
    SINT = [alloc("SINT%d" % i, [128, TS], F32) for i in range(2)]
    MTAB = alloc("MTAB", [128, 3, 512], BF16)
    GM = alloc("GM", [128, 512], F32)
    T8 = alloc("T8", [128, 32, 8], F32)
    SEL = alloc("SEL", [128, 512], F32)
    PEN = [alloc("PENA", [128, 16, 80], BF16), alloc("PENB", [128, 16, 16], BF16)]
    KM = alloc("KM", [128, 16], F32)
    KMR = alloc("KMR", [128, 16], F32)
    KMH = alloc("KMH", [128, 16], BF16)
    KML = alloc("KML", [128, 16], BF16)
    KMH2 = alloc("KMH2", [128, 2, 16], BF16)
    KML2 = alloc("KML2", [128, 2, 16], BF16)
    SB_LIMIT = 16512 + 212863
    print("phase1 sbuf end", cur[0], "moba_base", moba_base, "limit", SB_LIMIT)
    assert cur[0] <= SB_LIMIT, cur[0]
    cur[0] = phase_base
    WO = alloc("WO", [128, 8, D], BF16)
    SA = [alloc("SA%d" % i, [128, TS], F32) for i in range(2)]
    SB = [alloc("SB%d" % i, [128, TS], F32) for i in range(2)]
    TT = [alloc("TT%d" % i, [128, TS], F32) for i in range(2)]
    MTT = alloc("MTT", [128, 8, TS], BF16)
    assert cur[0] <= OFFS["WQ"][0], (cur[0], OFFS["WQ"])
    cur[0] = OFFS["WQ"][0]
    WF = alloc("WF", [128, 4, D], BF16)
    assert cur[0] <= OFFS["WV"][1]
    cur[0] = OFFS["WV"][1]
    WM = alloc("WM", [128, 4, D], BF16)
    XO = [alloc("XO%d" % i, [128, D], F32) for i in range(2)]
    OT = [alloc("OT%d" % i, [128, D], F32) for i in range(2)]
    assert cur[0] <= moba_base, (cur[0], moba_base)
    cur[0] = moba_base
    WG = alloc("WG", [128, 8, 2048], BF16)
    assert cur[0] <= SB_LIMIT, cur[0]
    MOBA_KEYS = ([("abt", a_, k_) for a_ in range(2) for k_ in range(2)] + [("rct", a_, k_) for a_ in range(2) for k_ in range(2)]
                 + [("cost", a_) for a_ in range(2)] + [("sint", a_) for a_ in range(2)] + ["MTAB", "GM", "SEL", "PENc", "KMH", "KMR", "KML", "KMH2c"]
                 + [("T8", g_) for g_ in range(32)] + [("PEN", h_) for h_ in range(2)] + [("KM", i_) for i_ in range(NT)]
                 + [("KMH2", h_) for h_ in range(2)] + [("KML2", h_) for h_ in range(2)])

    w3 = w_in.rearrange("(c p) n -> p c n", p=128)

    def A(eng, fn, reads=(), writes=()):
        return R.add(eng, fn, reads=reads, writes=writes)

    cur[0] = moba_base
    WFL = alloc("WFL", [128, 8, 8], BF16)
    FLE = alloc("FLE", [8, TS], F32)
    CL = alloc("CL", [8, SEQ], F32)
    ONES8 = alloc("ONES8", [8, TS], F32)
    TOT = alloc("TOT", [8, 8], F32)
    OFF = alloc("OFF", [8, 8], F32)
    TMP8 = alloc("TMP8", [8, 8], F32)
    AOFF = alloc("AOFF", [8, 8, 8], F32)
    CC = alloc("CC", [8, TS], F32)
    R1 = alloc("R1", [8, TS], F32)
    _pie = alloc("PIE", [8, 6, TS], BF16)
    PIE = [_pie, _pie]
    assert cur[0] <= SB_LIMIT, cur[0]
    dma("pool", WFL[:], w3[:, :, 6144:6152], writes=["WFL"])
    dma("sp", AOFF[:], aoff_d, writes=["AOFF"])
    A("dve", lambda e: e.memset(ONES8[:], 1.0), writes=["ONES8"])
    A("dve", lambda e: e.tensor_scalar(out=BFT[0:8, 1:2], in0=BFT[0:8, 0:1], scalar1=-1.0, scalar2=None, op0=ALU.mult),
      reads=["BFT"], writes=["NBF"])

    import os as _os
    pairs = [(0, hp) for hp in range(4)] + [(1, hp) for hp in range(4)]
    if _os.environ.get("K_PAIRS"):
        pairs = [tuple(int(v) for v in t.split(":")) for t in _os.environ["K_PAIRS"].split(",")]

    def load_pair_weights(br, hp):
        base = br * 2048
        dma("pool", WK[:], w3[:, :, base + 512 + hp * 128: base + 512 + (hp + 1) * 128], writes=["WK"])
        dma("pool", WV[:], w3[:, :, base + 1024 + hp * 128: base + 1024 + (hp + 1) * 128], writes=["WV"])
        dma("pool", WQ[:], w3[:, :, base + hp * 128: base + (hp + 1) * 128], writes=["WQ"])
        dma("pool", WZ[:], w3[:, :, base + 1536 + hp * 128: base + 1536 + (hp + 1) * 128], writes=["WZ"])
    load_pair_weights(*pairs[0])

    def phase_c_front_pe(i):
        pb = 2 + i % 2
        for c in range(8):
            A("pe", lambda e, c=c: e.matmul(bank(pb)[0:8, :], lhsT=WFL[:, c, :], rhs=HT[:, c, i * TS:(i + 1) * TS],
                                            start=(c == 0), stop=(c == 7)),
              reads=["WFL", ("HT", i)], writes=[("bank", pb)])

    def phase_c_front_rest(i):
        pb = 2 + i % 2
        A("act", lambda e: e.activation(out=FLE[:], in_=bank(pb)[0:8, :], func=AF.Exp, bias=BFT[0:8, 1:2], scale=-1.0),
          reads=[("bank", pb), "NBF"], writes=["fle"])
        A("act", lambda e: e.activation(out=FLE[:], in_=FLE[:], func=AF.Ln, bias=1.0, scale=1.0),
          reads=["fle"], writes=["fle"])
        A("dve", lambda e: e.tensor_tensor_scan(out=CL[:, i * TS:(i + 1) * TS], data0=ONES8[:], data1=FLE[:],
                                                initial=0.0, op0=ALU.mult, op1=ALU.add),
          reads=["fle", "ONES8"], writes=[("cl", i)])
        A("dve", lambda e: e.tensor_copy(out=TOT[:, i:i + 1], in_=CL[:, i * TS + TS - 1:i * TS + TS]),
          reads=[("cl", i)], writes=["TOT"])

    def phase_c_tail_job():
        for i in range(NT):
            A("dve", lambda e, i=i: e.tensor_tensor(out=TMP8[:], in0=AOFF[:, i, :], in1=TOT[:], op=ALU.mult),
              reads=["AOFF", "TOT"], writes=["TMP8"])
            A("dve", lambda e, i=i: e.reduce_sum(out=OFF[:, i:i + 1], in_=TMP8[:], axis=AX.X),
              reads=["TMP8"], writes=["OFF"])
        yield
        for i in range(NT):
            r = 0
            A("dve", lambda e, r=r, i=i: e.tensor_scalar(out=CC[:], in0=CL[:, i * TS:(i + 1) * TS], scalar1=OFF[:, i:i + 1],
                                                         scalar2=-1.0, op0=ALU.add, op1=ALU.mult),
              reads=[("cl", i), "OFF"], writes=["cc"])
            A("dve", lambda e, r=r: e.tensor_copy(out=PIE[r][:, 3, :], in_=CC[:]), reads=["cc"], writes=[("pie", r)])
            A("dve", lambda e, r=r: e.tensor_tensor(out=R1[:], in0=CC[:], in1=PIE[r][:, 3, :], op=ALU.subtract),
              reads=["cc", ("pie", r)], writes=["r1"])
            A("dve", lambda e, r=r: e.tensor_copy(out=PIE[r][:, 4, :], in_=R1[:]), reads=["r1"], writes=[("pie", r)])
            yield
            A("dve", lambda e, r=r: e.tensor_tensor(out=CC[:], in0=R1[:], in1=PIE[r][:, 4, :], op=ALU.subtract),
              reads=["r1", ("pie", r)], writes=["cc"])
            A("dve", lambda e, r=r: e.tensor_copy(out=PIE[r][:, 5, :], in_=CC[:]), reads=["cc"], writes=[("pie", r)])
            A("dve", lambda e, r=r: e.tensor_scalar(out=PIE[r][:, 0:3, :], in0=PIE[r][:, 3:6, :], scalar1=-1.0, scalar2=None, op0=ALU.mult),
              reads=[("pie", r)], writes=[("pie", r)])
            dma("sp", cscr[:, :, i * TS:(i + 1) * TS], PIE[r][:], reads=[("pie", r)], writes=["cscr"])
            yield

    cur[0] = phase_base
    XT8 = [alloc("XT8_%d" % i, [128, 8, TS], F32) for i in range(2)]
    SQ0 = [alloc("SQ0_%d" % i, [128, TS], BF16) for i in range(2)]
    RSTD = [alloc("RSTD_%d" % i, [128, TS], F32) for i in range(2)]
    cur[0] = OFFS["PT0"][0]
    PTMP = [alloc("PTMP%d" % i, [128, TS], F32) for i in range(2)]
    assert cur[0] <= OFFS["REC"][0], (cur[0], OFFS["REC"])
    import os
    P0T = int(os.environ.get("P0_TILES", NT))
    SKIP = set(os.environ.get("P0_SKIP", "").split(","))
    for i in range(P0T):
        b = i % 2
        rb = i % 2
        for c in range(8):
            dma("sp", XT8[b][:, c, :], xT[c * 128:(c + 1) * 128, i * TS:(i + 1) * TS],
                writes=[("xt8", b, c)])
        pb = i % 2
        if i >= 2:
            phase_c_front_pe(i - 2)
        for c in range(8):
            R.add("act", lambda e, b=b, c=c: e.activation(out=SQ0[c % 2][:], in_=XT8[b][:, c, :], func=AF.Square),
                  reads=[("xt8", b, c)], writes=[("sq0", c % 2)])
            R.add("pe", lambda e, c=c, pb=pb: e.matmul(bank(pb), lhsT=ONESM, rhs=SQ0[c % 2][:], start=(c == 0), stop=(c == 7)),
                  reads=[("sq0", c % 2), "CM"], writes=[("bank", pb)])
        R.add("act", lambda e, rb=rb, pb=pb: e.activation(out=RSTD[rb][:], in_=bank(pb), func=AF.Ln, bias=EPSC[:, 0:1], scale=1.0),
              reads=[("bank", pb), "EPSC"], writes=[("rstd", rb)])
        R.add("act", lambda e, rb=rb: e.activation(out=RSTD[rb][:], in_=RSTD[rb][:], func=AF.Exp, scale=-0.5),
              reads=[("rstd", rb)], writes=[("rstd", rb)])
        for c in range(8):
            if c in (3, 7):
                pt_ = (c // 4) % 2
                R.add("pool", lambda e, b=b, rb=rb, c=c, pt_=pt_: e.tensor_tensor(
                    out=PTMP[pt_][:], in0=XT8[b][:, c, :], in1=RSTD[rb][:], op=ALU.mult),
                    reads=[("xt8", b, c), ("rstd", rb)], writes=[("ptmp", pt_)])
                R.add("pool", lambda e, c=c, i=i, pt_=pt_: e.tensor_scalar(
                    out=HT[:, c, i * TS:(i + 1) * TS], in0=PTMP[pt_][:], scalar1=GNG[:, c:c + 1], scalar2=None, op0=ALU.mult),
                    reads=[("ptmp", pt_), "GNG"], writes=[("HT", i)])
                continue
            R.add("dve", lambda e, b=b, rb=rb, c=c, i=i: e.scalar_tensor_tensor(
                out=HT[:, c, i * TS:(i + 1) * TS], in0=XT8[b][:, c, :], scalar=GNG[:, c:c + 1], in1=RSTD[rb][:],
                op0=ALU.mult, op1=ALU.mult),
                reads=[("xt8", b, c), ("rstd", rb), "GNG"], writes=[("HT", i)])
        if i >= 2:
            phase_c_front_rest(i - 2)
    for i in range(max(P0T - 2, 0), P0T):
        phase_c_front_pe(i)
        phase_c_front_rest(i)
    R.barrier()

    if "HT" in dbg_out:
        for c in range(8):
            dma("sp", dbg_out["HT"][:, c, :], HT[:, c, :], reads=[("HT", i) for i in range(NT)])

    if stop_after == "p0":
        R.emit(nc)
        return nc

    HROWS = [slice(0, 80), slice(0, 128)]
    MROWS = [slice(0, 64), slice(64, 128)]
    AUG0 = [64, 0]
    VCOLS = [66, 128]
    DENROW = [slice(64, 65), slice(0, 1)]
    plists = past_lists()

    A("pool", lambda e: e.memset(KT[1][0:64, :], 0.0), writes=[("Kaug", 1)])
    A("pool", lambda e: e.memset(QT[1][0:64, :], 0.0), writes=[("Qaug", 1)])
    A("pool", lambda e: e.memset(QT[0][64:80, :], 0.0), writes=[("Qaug", 0)])
    A("pool", lambda e: e.memset(RH[0][0:64, :], 0.0), writes=["RHc"])
    A("pool", lambda e: e.memset(RL[0][0:64, :], 0.0), writes=["RHc"])
    A("pool", lambda e: e.memset(VT[0][:, :, 64:66], 1.0), writes=[("Vc", 0)])
    A("pool", lambda e: e.memset(VT[1][:, :, 0:2], 1.0), writes=[("Vc", 1)])
    A("pool", lambda e: e.memset(VT[1][:, :, 2:64], 0.0), writes=[("Vc", 1)])
    dma("sp", DMASK[:], dmask_d, writes=["DMASK"])

    bank_rot = [0]

    NROT = [8]

    def next_bank():
        b = bank_rot[0] % NROT[0]
        bank_rot[0] = (b + 1) % NROT[0]
        return b

    free_banks = list(range(8))

    def acquire():
        assert free_banks, "out of PSUM banks"
        return free_banks.pop(0)

    def release(b):
        assert b not in free_banks
        free_banks.append(b)

    rotn = {"sq": (0, 2), "rs": (0, 2), "t": (0, 2), "cs": (0, 2), "rc": (0, 2), "ab": (0, 2)}

    def nxt(k):
        v, n = rotn[k]
        rotn[k] = ((v + 1) % n, n)
        return v

    def proj_feat(W, wkey, i, bk):
        for c in range(8):
            A("pe", lambda e, c=c: e.matmul(bank(bk), lhsT=W[:, c, :], rhs=HT[:, c, i * TS:(i + 1) * TS],
                                            start=(c == 0), stop=(c == 7)),
              reads=[wkey, ("HT", i)], writes=[("bank", bk)])

    def qk_job(br, isq, W, wkey, i):
        gcol = br * 2 + (0 if isq else 1)
        dst = QT if isq else KT
        dkey = "Q" if isq else "K"
        cols = slice(i * TS, (i + 1) * TS)
        bk = acquire()
        proj_feat(W, wkey, i, bk)
        yield
        sq = nxt("sq")
        A("act", lambda e: e.activation(out=SQ1[sq][:], in_=bank(bk), func=AF.Square),
          reads=[("bank", bk)], writes=[("sq1", sq)])
        by = acquire()
        A("pe", lambda e: e.matmul(bank(by), lhsT=(BDQ if isq else BD), rhs=SQ1[sq][:], start=True, stop=True),
          reads=[("sq1", sq), "CM"], writes=[("bank", by)])
        if br == 1:
            cs = nxt("cs")
            dma("sp", COST[cs][:], cos_d[:, cols], writes=[("cost", cs)])
            dma("sp", SINT[cs][:], sin_d[:, cols], writes=[("sint", cs)])
        yield
        r = nxt("rs")
        ec = 1 if isq else 0
        A("act", lambda e: e.activation(out=RS[r][:], in_=bank(by), func=AF.Ln, bias=EPSC[:, ec:ec + 1], scale=1.0),
          reads=[("bank", by), "EPSC"], writes=[("rs", r)])
        A("act", lambda e: e.activation(out=RS[r][:], in_=RS[r][:], func=AF.Exp, scale=-0.5),
          reads=[("rs", r)], writes=[("rs", r)])
        release(by)
        if br == 0:
            for hd in range(2):
                mr = MROWS[hd]
                A("dve", lambda e, hd=hd, mr=mr: e.scalar_tensor_tensor(
                    out=dst[hd][mr, cols], in0=bank(bk)[mr, :], scalar=GAINS[mr, gcol:gcol + 1], in1=RS[r][mr, :],
                    op0=ALU.mult, op1=ALU.mult),
                  reads=[("bank", bk), ("rs", r), "GAINS"], writes=[(dkey, hd, i)])
            release(bk)
            return
        rc = nxt("rc")
        ab = nxt("ab")
        A("pool", lambda e: e.tensor_tensor(out=RCT[rc][0][:], in0=RS[r][:], in1=COST[cs][:], op=ALU.mult),
          reads=[("rs", r), ("cost", cs)], writes=[("rct", rc, 0)])
        A("pool", lambda e: e.tensor_tensor(out=RCT[rc][1][:], in0=RS[r][:], in1=SINT[cs][:], op=ALU.mult),
          reads=[("rs", r), ("sint", cs)], writes=[("rct", rc, 1)])
        for k2 in range(2):
            A("dve", lambda e, k2=k2: e.scalar_tensor_tensor(
                out=ABT[ab][k2][:], in0=bank(bk), scalar=GAINS[:, gcol:gcol + 1], in1=RCT[rc][k2][:], op0=ALU.mult, op1=ALU.mult),
              reads=[("bank", bk), ("rct", rc, k2), "GAINS"], writes=[("abt", ab, k2)])
        release(bk)
        yield
        bz = acquire()
        A("pe", lambda e: e.matmul(bank(bz), lhsT=IDENT, rhs=ABT[ab][0][:], start=True, stop=False),
          reads=[("abt", ab, 0), "CM"], writes=[("bank", bz)])
        A("pe", lambda e: e.matmul(bank(bz), lhsT=RT, rhs=ABT[ab][1][:], start=False, stop=True),
          reads=[("abt", ab, 1), "CM"], writes=[("bank", bz)])
        yield
        for hd in range(2):
            mr = MROWS[hd]
            A("act", lambda e, hd=hd, mr=mr: e.activation(out=dst[hd][mr, cols], in_=bank(bz)[mr, :], func=AF.Copy),
              reads=[("bank", bz)], writes=[(dkey, hd, i)])
        if not isq:
            for hd in range(2):
                mr = MROWS[hd]
                A("dve", lambda e, hd=hd, mr=mr: e.reduce_sum(out=KM[mr, 2 * i:2 * i + 2],
                                                             in_=bank(bz)[mr, :].rearrange("p (b l) -> p b l", l=256), axis=AX.X),
                  reads=[("bank", bz)], writes=[("KM", i)])
        release(bz)

    def z_job(i):
        bk = acquire()
        proj_feat(WZ, "WZ", i, bk)
        yield
        t = nxt("t")
        A("act", lambda e: e.activation(out=T1[t][:], in_=bank(bk), func=AF.Exp, scale=-1.0),
          reads=[("bank", bk)], writes=[("t1", t)])
        A("act", lambda e: e.activation(out=T1[t][:], in_=T1[t][:], func=AF.Ln, bias=1.0, scale=1.0),
          reads=[("t1", t)], writes=[("t1", t)])
        A("act", lambda e: e.activation(out=T1[t][:], in_=T1[t][:], func=AF.Exp, scale=-1.0),
          reads=[("t1", t)], writes=[("t1", t)])
        A("dve", lambda e: e.tensor_tensor(out=ZS[:, i * TS:(i + 1) * TS], in0=T1[t][:], in1=bank(bk), op=ALU.mult),
          reads=[("t1", t), ("bank", bk)], writes=[("ZS", i)])
        release(bk)

    def v_job(g):
        bk = acquire()
        for s4 in range(4):
            st = 4 * g + s4
            for c in range(8):
                A("pe", lambda e, c=c, st=st, s4=s4: e.matmul(bank(bk)[:, s4 * 128:(s4 + 1) * 128],
                                                            lhsT=HT[:, c, st * 128:(st + 1) * 128], rhs=WV[:, c, :],
                                                            start=(c == 0), stop=(c == 7)),
                  reads=["WV", ("HT", st // 4)], writes=[("bank", bk)])
        yield
        src = bank(bk).rearrange("p (s n) -> p s n", n=128)
        A("act", lambda e: e.activation(out=VT[0][:, 4 * g:4 * g + 4, 0:64], in_=src[:, :, 0:64], func=AF.Copy),
          reads=[("bank", bk)], writes=[("V", 0, g)])
        A("act", lambda e: e.activation(out=VT[1][:, 4 * g:4 * g + 4, 64:128], in_=src[:, :, 64:128], func=AF.Copy),
          reads=[("bank", bk)], writes=[("V", 1, g)])
        release(bk)

    def run_pipeline(jobs):
        pend = list(jobs)
        active = []
        done = set()
        nstep = [0]
        held = []
        if pending:
            for b_ in (6, 7):
                free_banks.remove(b_)
                held.append(b_)
        while pend or active:
            for k, (gnr, after) in enumerate(pend):
                if all(id(a_) in done for a_ in after):
                    active.append(gnr)
                    pend.pop(k)
                    break
            for gnr in reversed(list(active)):
                try:
                    next(gnr)
                except StopIteration:
                    active.remove(gnr)
                    done.add(id(gnr))
            nstep[0] += 1
            if nstep[0] == 3:
                flush_pending()
                while held:
                    release(held.pop())
        assert not held

    def attention_pair(br, hp):
        chunk = br * 4 + hp
        flat = []
        for (j, hd) in [(0, 0), (3, 1), (1, 0), (2, 1), (2, 0), (1, 1), (3, 0), (0, 1)]:
            past = []
            for i in plists[j]:
                past += [dict(ks=i * 4 + s4, mk=None, q0=0) for s4 in range(4)]
            dg = [dict(ks=j * 4 + s4, mk=s4, q0=s4 * 128) for s4 in range(4)]
            g1 = [dict(dg[0], b=0, c=0), dict(dg[1], b=1, c=0), dict(dg[3], b=1, c=384)]
            g2 = [dict(dg[2], b=0, c=0), dict(past[0], b=1, c=0)]
            grs = [g1, g2]
            rest = past[1:]
            for k in range(0, len(rest), 2):
                grs.append([dict(e_, b=bi, c=0) for bi, e_ in enumerate(rest[k:k + 2])])
            for gi, g in enumerate(grs):
                flat.append(dict(hd=hd, j=j, ents=g, first=(gi == 0), last=(gi == len(grs) - 1)))
        ngr = len(flat)

        def qk(n):
            G = flat[n]
            hd, j = G["hd"], G["j"]
            Kt, Qt, rows = KT[hd], QT[hd], HROWS[hd]
            sb = n % 3
            for E in G["ents"]:
                ks, mk, q0 = E["ks"], E["mk"], E["q0"]
                c0 = E["b"] * 512 + E["c"]
                c1 = c0 + (512 - q0)
                kreads = [("K", hd, ks // 4), ("Kaug", hd), ("Q", hd, j), ("Qaug", hd)]
                A("pe", lambda e, ks=ks, c0=c0, c1=c1, mk=mk, q0=q0: e.matmul(
                    PS[sb][:, c0:c1], lhsT=Kt[rows, ks * 128:(ks + 1) * 128], rhs=Qt[rows, j * TS + q0:(j + 1) * TS],
                    start=True, stop=(mk is None)),
                  reads=kreads, writes=[("bank", 2 * sb + E["b"])])
                if mk is not None:
                    A("pe", lambda e, c0=c0, c1=c1, mk=mk, q0=q0: e.matmul(PS[sb][:, c0:c1], lhsT=IDENT, rhs=DMASK[:, mk, q0:512],
                                                                         start=False, stop=True),
                      reads=["CM", "DMASK"], writes=[("bank", 2 * sb + E["b"])])

        def ex(n):
            sb = n % 3
            pb = n % 3
            banks = sorted(set(E["b"] for E in flat[n]["ents"]))
            rngs = sorted((E["b"] * 512 + E["c"], E["b"] * 512 + E["c"] + 512 - E["q0"]) for E in flat[n]["ents"])
            merged = []
            for (r0, r1) in rngs:
                if merged and merged[-1][1] == r0:
                    merged[-1][1] = r1
                else:
                    merged.append([r0, r1])
            for (r0, r1) in merged:
                A("act", lambda e, r0=r0, r1=r1: e.activation(out=PT[pb][:, r0:r1], in_=PS[sb][:, r0:r1], func=AF.Exp),
                  reads=[("bank", 2 * sb + b_) for b_ in banks], writes=[("pt", pb)])

        def pv(n):
            G = flat[n]
            hd, j = G["hd"], G["j"]
            Vt, vc, ob = VT[hd], VCOLS[hd], 6 + hd
            pb = n % 3
            ne = len(G["ents"])
            for e_i, E in enumerate(G["ents"]):
                ks, q0 = E["ks"], E["q0"]
                c0 = E["b"] * 512 + E["c"]
                c1 = c0 + (512 - q0)
                first = G["first"] and e_i == 0
                last = G["last"] and e_i == ne - 1
                A("pe", lambda e, ks=ks, c0=c0, c1=c1, q0=q0, first=first, last=last: e.matmul(
                    bank(ob)[0:vc, q0:512], lhsT=Vt[:, ks, 0:vc], rhs=PT[pb][:, c0:c1], start=first, stop=last,
                    skip_group_check=True),
                  reads=[("pt", pb), ("V", hd, ks // 4), ("Vc", hd)], writes=[("bank", ob)])
            if G["last"]:
                finalize1(hd, j, ob)

        def finalize1(hd, j, ob):
            dr = DENROW[hd]
            mr = MROWS[hd]
            A("dve", lambda e: e.reciprocal(out=REC[hd][dr, :], in_=bank(ob)[dr, :]), reads=[("bank", ob)], writes=[("REC", hd)])
            A("dve", lambda e: e.tensor_copy(out=RH[hd][dr, :], in_=REC[hd][dr, :]), reads=[("REC", hd)], writes=[("RH", hd)])
            A("dve", lambda e: e.tensor_tensor(out=RL[hd][dr, :], in0=REC[hd][dr, :], in1=RH[hd][dr, :], op=ALU.subtract),
              reads=[("REC", hd), ("RH", hd)], writes=[("RL", hd)])
            A("dve", lambda e: e.tensor_tensor(out=YTMP[hd][mr, :], in0=bank(ob)[mr, :], in1=ZS[mr, j * TS:(j + 1) * TS], op=ALU.mult),
              reads=[("bank", ob), ("ZS", j)], writes=[("YTMP", hd)])

            def stage2():
                if hd == 0:
                    bl, br_ = ONESEL[0:65, :], slice(0, 65)
                else:
                    bl, br_ = ONES1[0:1, :], slice(0, 1)
                A("pe", lambda e: e.matmul(bank(ob), lhsT=bl, rhs=RH[hd][br_, :], start=True, stop=False),
                  reads=[("RH", hd), "RHc", "CM"], writes=[("bank", ob)])
                A("pe", lambda e: e.matmul(bank(ob), lhsT=bl, rhs=RL[hd][br_, :], start=False, stop=True),
                  reads=[("RL", hd), "RHc", "CM"], writes=[("bank", ob)])
                A("dve", lambda e: e.tensor_tensor(out=YT[mr, chunk, j * TS:(j + 1) * TS], in0=YTMP[hd][mr, :], in1=bank(ob)[mr, :], op=ALU.mult),
                  reads=[("YTMP", hd), ("bank", ob)], writes=[("YT", chunk, j)])
            pending.append([8, stage2])

        LOOK = 2
        for n in range(min(LOOK, ngr)):
            qk(n)
            ex(n)
        for n in range(ngr):
            if n + LOOK < ngr:
                qk(n + LOOK)
                ex(n + LOOK)
            for it in list(pending):
                it[0] -= 1
                if it[0] <= 0:
                    pending.remove(it)
                    it[1]()
            pv(n)

    pending = []

    def flush_pending():
        while pending:
            pending.pop(0)[1]()

    def gating_job():
        A("dve", lambda e: e.tensor_copy(out=KMH[:], in_=KM[:]), reads=[("KM", i) for i in range(NT)], writes=["KMH"])
        A("dve", lambda e: e.tensor_tensor(out=KMR[:], in0=KM[:], in1=KMH[:], op=ALU.subtract),
          reads=[("KM", i) for i in range(NT)] + ["KMH"], writes=["KMR"])
        A("dve", lambda e: e.tensor_copy(out=KML[:], in_=KMR[:]), reads=["KMR"], writes=["KML"])
        for hd in range(2):
            mr = MROWS[hd]
            A("dve", lambda e, hd=hd, mr=mr: e.tensor_copy(out=KMH2[mr, hd, :], in_=KMH[mr, :]), reads=["KMH", "KMH2c"], writes=[("KMH2", hd)])
            A("dve", lambda e, hd=hd, mr=mr: e.tensor_copy(out=KML2[mr, hd, :], in_=KML[mr, :]), reads=["KML", "KMH2c"], writes=[("KML2", hd)])
        yield
        gb_ = acquire()
        g4 = bank(gb_).rearrange("p (s h j) -> p s h j", h=2, j=16)
        for st in range(16):
            for hd in range(2):
                hr = HROWS[hd]
                A("pe", lambda e, st=st, hd=hd, hr=hr: e.matmul(g4[:, st, hd, :], lhsT=QT[hd][hr, st * 128:(st + 1) * 128], rhs=KMH2[hr, hd, :],
                                                                start=True, stop=False),
                  reads=[("Q", hd, st // 4), ("Qaug", hd), ("KMH2", hd), "KMH2c"], writes=[("bank", gb_)])
                A("pe", lambda e, st=st, hd=hd, hr=hr: e.matmul(g4[:, st, hd, :], lhsT=QT[hd][hr, st * 128:(st + 1) * 128], rhs=KML2[hr, hd, :],
                                                                start=False, stop=True),
                  reads=[("Q", hd, st // 4), ("Qaug", hd), ("KML2", hd), "KMH2c"], writes=[("bank", gb_)])
        yield
        A("dve", lambda e: e.tensor_tensor(out=GM[:], in0=bank(gb_), in1=MTAB[:, 0, :], op=ALU.add),
          reads=[("bank", gb_), "MTAB"], writes=["GM"])
        release(gb_)
        for grp in range(32):
            A("dve", lambda e, grp=grp: e.max(out=T8[:, grp, :], in_=GM[:, grp * 16:(grp + 1) * 16]),
              reads=["GM"], writes=[("T8", grp)])
        yield
        gm3 = GM[:].rearrange("p (g j) -> p g j", j=16)
        sel3 = SEL[:].rearrange("p (g j) -> p g j", j=16)
        A("dve", lambda e: e.tensor_tensor(out=sel3, in0=gm3, in1=T8[:, :, 2:3].to_broadcast([128, 32, 16]), op=ALU.is_ge),
          reads=["GM"] + [("T8", grp) for grp in range(32)], writes=["SEL"])
        A("dve", lambda e: e.tensor_tensor(out=SEL[:], in0=SEL[:], in1=MTAB[:, 1, :], op=ALU.mult), reads=["SEL", "MTAB"], writes=["SEL"])
        A("dve", lambda e: e.tensor_tensor(out=SEL[:], in0=SEL[:], in1=MTAB[:, 2, :], op=ALU.add), reads=["SEL", "MTAB"], writes=["SEL"])
        sel4 = SEL[:].rearrange("p (s h j) -> p s h j", h=2, j=16)
        for hd in range(2):
            a0 = AUG0[hd]
            A("dve", lambda e, hd=hd, a0=a0: e.tensor_scalar(out=PEN[hd][:, :, a0:a0 + 16], in0=sel4[:, :, hd, :], scalar1=-1.0, scalar2=BIG,
                                                             op0=ALU.add, op1=ALU.mult),
              reads=["SEL", "PENc"], writes=[("PEN", hd)])
        yield
        for g in range(4):
            for hd in range(2):
                M = 80 if hd == 0 else 16
                cr = slice(64, 80) if hd == 0 else slice(0, 16)
                tb = acquire()
                for s4 in range(4):
                    st = 4 * g + s4
                    A("pe", lambda e, st=st, s4=s4, hd=hd, M=M, tb=tb: e.matmul(bank(tb)[0:M, s4 * 128:(s4 + 1) * 128], lhsT=PEN[hd][:, st, :], rhs=IDENT,
                                                                              start=True, stop=True),
                      reads=[("PEN", hd), "PENc", "CM"], writes=[("bank", tb)])
                A("dve", lambda e, hd=hd, g=g, cr=cr, tb=tb: e.tensor_copy(out=QT[hd][cr, g * TS:(g + 1) * TS], in_=bank(tb)[cr, :]),
                  reads=[("bank", tb)], writes=[("Qaug", hd)])
                release(tb)
            yield

    def do_pair(br, hp):
        if (br, hp) != pairs[0]:
            load_pair_weights(br, hp)

        def c_rows():
            for hd in range(2):
                h = 2 * hp + hd
                a0 = AUG0[hd]
                dma("sp", KT[hd][a0:a0 + 3, :], cscr[h, 0:3, :], reads=["cscr"], writes=[("Kaug", hd)])
                dma("sp", QT[hd][a0 + 3:a0 + 6, :], cscr[h, 3:6, 0:NOWN * TS], reads=["cscr"], writes=[("Qaug", hd)])
        first = not state0["tail_done"]
        if br == 0 and not first:
            c_rows()
        jobs = []
        if first:
            jobs.append((phase_c_tail_job(), []))
            state0["tail_done"] = True
        if br == 0:
            for i in range(NT):
                jobs.append((qk_job(br, False, WK, "WK", i), []))
                jobs.append((v_job(i), []))
            for i in range(NOWN):
                jobs.append((qk_job(br, True, WQ, "WQ", i), []))
                jobs.append((z_job(i), []))
        else:
            kq = [qk_job(br, False, WK, "WK", i) for i in range(NT)] + [qk_job(br, True, WQ, "WQ", i) for i in range(NOWN)]
            jobs += [(g_, []) for g_ in kq]
            jobs.append((gating_job(), kq))
            for i in range(NT):
                jobs.append((v_job(i), []))
                if i < NOWN:
                    jobs.append((z_job(i), []))
        run_pipeline(jobs)
        if br == 0 and first:
            c_rows()
        if first:
            R.barrier()
            A("pool", lambda e: e.memset(KMH2[:], 0.0), writes=["KMH2c"])
            A("pool", lambda e: e.memset(KML2[:], 0.0), writes=["KMH2c"])
            A("pool", lambda e: e.memset(PEN[0][:, :, 0:64], 0.0), writes=["PENc"])
            dma("sp", MTAB[:], mtab_d, writes=["MTAB"])
        if (br, hp) == pairs[-1] and stop_after is None:
            dma("pool", WG[:, :, 0:1024], w3[:, :, 4096:5120], reads=[], writes=["WG0"] + MOBA_KEYS)
            dma("pool", WG[:, :, 1024:2048], w3[:, :, 5120:6144], reads=[], writes=["WG1"] + MOBA_KEYS)
            dma("pool", WF[:], w_fox.rearrange("(c p) n -> p c n", p=128), writes=["WF", "WQ", "WK", "WZ", "WV"])
        attention_pair(br, hp)

    last_br = None
    state0 = {"tail_done": False}
    if pairs[0][0] != 0:
        run_pipeline([(phase_c_tail_job(), [])])
        state0["tail_done"] = True
    for (br, hp) in pairs:
        if br != last_br:
            for hd in range(2):
                a0 = AUG0[hd]
                if br == 0:
                    dma("sp", KT[hd][a0 + 3:a0 + 16, :], kac_d[3:16, :], writes=[("Kaug", hd)])
                    dma("sp", QT[hd][a0:a0 + 3, :], qac_d[0:3, :], writes=[("Qaug", hd)])
                    dma("sp", QT[hd][a0 + 6:a0 + 16, :], qac_d[6:16, :], writes=[("Qaug", hd)])
                else:
                    dma("sp", KT[hd][a0:a0 + 16, :], kam_d[:, :], writes=[("Kaug", hd)])
            last_br = br
        do_pair(br, hp)
    flush_pending()
    R.barrier()
    if "YT" in dbg_out:
        for c in sorted(set(b_ * 4 + h_ for (b_, h_) in pairs)):
            dma("sp", dbg_out["YT"][:, c, :], YT[:, c, :], reads=[("YT", c, j) for j in range(NOWN)])
    if "KA" in dbg_out:
        dma("sp", dbg_out["KA"][0:80, :], KT[0][0:80, :], reads=[("K", 0, i) for i in range(NT)] + [("Kaug", 0)])
        dma("sp", dbg_out["KB"], KT[1][:], reads=[("K", 1, i) for i in range(NT)] + [("Kaug", 1)])
        dma("sp", dbg_out["QA"][0:80, :], QT[0][0:80, :], reads=[("Q", 0, i) for i in range(NOWN)] + [("Qaug", 0)])
        dma("sp", dbg_out["QB"], QT[1][:], reads=[("Q", 1, i) for i in range(NOWN)] + [("Qaug", 1)])
        dma("sp", dbg_out["VB"], VT[1][:], reads=[("V", 1, g) for g in range(8)] + [("Vc", 1)])
        dma("sp", dbg_out["ZS"], ZS[:], reads=[("ZS", i) for i in range(NOWN)])
    if stop_after == "p1":
        R.emit(nc)
        return nc

    dma("pool", WM[:], w_moba.rearrange("(c p) n -> p c n", p=128), writes=["WM"])
    dma("pool", WO[:], w_out.rearrange("(c p) n -> p c n", p=128), writes=["WO"])
    rr = [0]
    xr = [0]
    for j in range(NOWN):
        for n in range(8):
            r = rr[0]
            rr[0] = 1 - r
            ba, bb, bf_, bm = next_bank(), next_bank(), next_bank(), next_bank()
            for gi, (bk, dst) in enumerate([(ba, SA), (bb, SB)]):
                for c in range(8):
                    A("pe", lambda e, c=c, gi=gi, bk=bk, n=n, j=j: e.matmul(
                        bank(bk), lhsT=WG[:, c, gi * 1024 + n * 128: gi * 1024 + (n + 1) * 128], rhs=HT[:, c, j * TS:(j + 1) * TS],
                        start=(c == 0), stop=(c == 7)),
                      reads=["WG%d" % gi, ("HT", j)], writes=[("bank", bk)])
                A("act", lambda e, gi=gi, bk=bk, n=n, dst=dst, r=r: e.activation(
                    out=dst[r][:], in_=bank(bk), func=AF.Sigmoid, bias=BGATE[:, gi * 8 + n: gi * 8 + n + 1], scale=1.0),
                  reads=[("bank", bk), "BGATE"], writes=[("sg", gi, r)])
            for (bk, W, wk, c0) in [(bf_, WF, "WF", 0), (bm, WM, "WM", 4)]:
                for c in range(4):
                    A("pe", lambda e, c=c, bk=bk, W=W, c0=c0, n=n, j=j: e.matmul(
                        bank(bk), lhsT=W[:, c, n * 128:(n + 1) * 128], rhs=YT[:, c0 + c, j * TS:(j + 1) * TS],
                        start=(c == 0), stop=(c == 3)),
                      reads=[wk] + [("YT", c0 + cc, j) for cc in range(4)], writes=[("bank", bk)])
            A("dve", lambda e, r=r, bf_=bf_: e.tensor_tensor(out=TT[r][:], in0=bank(bf_), in1=SA[r][:], op=ALU.mult),
              reads=[("bank", bf_), ("sg", 0, r)], writes=[("tt", r)])
            A("dve", lambda e, r=r, bm=bm: e.tensor_tensor(out=SB[r][:], in0=bank(bm), in1=SB[r][:], op=ALU.mult),
              reads=[("bank", bm), ("sg", 1, r)], writes=[("sg", 1, r)])
            A("dve", lambda e, r=r, n=n: e.tensor_tensor(out=MTT[:, n, :], in0=TT[r][:], in1=SB[r][:], op=ALU.add),
              reads=[("tt", r), ("sg", 1, r)], writes=[("mtt", n)])
        for ts4 in range(4):
            x = xr[0]
            xr[0] = 1 - x
            row0 = (j * 4 + ts4) * 128
            dma("sp", XO[x][:], xo[row0:row0 + 128, :], writes=[("xo", x)])
            for half in range(2):
                bo = next_bank()
                for c in range(8):
                    A("pe", lambda e, c=c, bo=bo, half=half, ts4=ts4: e.matmul(
                        bank(bo), lhsT=MTT[:, c, ts4 * 128:(ts4 + 1) * 128], rhs=WO[:, c, half * 512:(half + 1) * 512],
                        start=(c == 0), stop=(c == 7)),
                      reads=["WO"] + [("mtt", cc) for cc in range(8)], writes=[("bank", bo)])
                A("dve", lambda e, x=x, bo=bo, half=half: e.tensor_tensor(
                    out=OT[x][:, half * 512:(half + 1) * 512], in0=bank(bo), in1=XO[x][:, half * 512:(half + 1) * 512], op=ALU.add),
                  reads=[("bank", bo), ("xo", x)], writes=[("ot", x, half)])
            dma("sp", out_d[row0:row0 + 128, :], OT[x][:], reads=[("ot", x, 0), ("ot", x, 1)], writes=[("outd", row0)])
    R.emit(nc)
    return nc


def make_in_maps(inputs):
    x = np.asarray(inputs["x"], np.float32)
    w_in = np.ascontiguousarray(np.asarray(inputs["w_in"], np.float32)[0])
    w_fox = np.ascontiguousarray(np.asarray(inputs["w_fox"], np.float32)[0])
    w_moba = np.ascontiguousarray(np.asarray(inputs["w_moba"], np.float32)[0])
    w_out = np.ascontiguousarray(np.asarray(inputs["w_out"], np.float32)[0])
    gng = np.ascontiguousarray(np.asarray(inputs["norm_g"], np.float32)[0].reshape(8, 128).T)
    gains = np.ascontiguousarray(np.stack([
        np.tile(np.asarray(inputs["fox_q_g"], np.float32)[0], 2),
        np.tile(np.asarray(inputs["fox_k_g"], np.float32)[0], 2),
        np.tile(np.asarray(inputs["moba_q_g"], np.float32)[0], 2),
        np.tile(np.asarray(inputs["moba_k_g"], np.float32)[0], 2)], axis=1))
    bg = np.asarray(inputs["b_gate"], np.float32)[0]
    bgate = np.ascontiguousarray(bg.reshape(2, 8, 128).transpose(2, 0, 1).reshape(128, 16))
    bf = np.ascontiguousarray(np.asarray(inputs["b_f"], np.float32)[0].reshape(8, 1))
    tabs = [const_tables(p) for p in range(2)]
    maps = []
    for core in range(8):
        b, p = core // 2, core % 2
        pos = storage_pos(p)
        xs = x[b][pos]
        m = dict(xT=np.ascontiguousarray(xs.T), xo=np.ascontiguousarray(xs[:NOWN * TS]),
                 w_in=w_in, w_fox=w_fox, w_moba=w_moba, w_out=w_out, gng=gng, gains=gains,
                 bgate=bgate, bf=bf)
        m.update(tabs[p])
        maps.append(m)
    return maps


def kernel(**inputs):
    maps = make_in_maps(inputs)
    nc = build_program()
    res = run_bass_kernel_spmd(nc, maps, core_ids=list(range(8)))
    out = np.zeros((NBATCH, SEQ, D), np.float32)
    for core in range(8):
        b, p = core // 2, core % 2
        pos = storage_pos(p)
        out[b, pos[:NOWN * TS]] = res.results[core]["out"]
    return out
```

```python
import numpy as np
import ml_dtypes
import concourse.bass as bass
import concourse.mybir as mybir
from concourse.bass_utils import run_bass_kernel_spmd

F32 = mybir.dt.float32
BF16 = mybir.dt.bfloat16
AF = mybir.ActivationFunctionType
ALU = mybir.AluOpType
AX = mybir.AxisListType

D = 1024
SEQ = 4096
NBATCH = 4
TS = 512
NT = 8
NOWN = 4
HD = 64
INW = 6152
BIG = 30000.0
EPS = 1e-6
PI = [[0, 3, 4, 7, 1, 2, 5, 6], [1, 2, 5, 6, 0, 3, 4, 7]]
ROPE_DIM = 16
ROPE_THETA = 500000.0


def past_lists():
    out = []
    for j in range(NOWN):
        s = set()
        for p in range(2):
            for i in range(NT):
                if PI[p][i] < PI[p][j]:
                    s.add(i)
        out.append(sorted(s))
    return out


class Rec:
    ENGS = ("pe", "act", "dve", "pool", "sp")

    def __init__(self):
        self.ops = []
        self.lastw = {}
        self.readers = {}

    def add(self, eng, fn, reads=(), writes=(), dma=False):
        oid = len(self.ops)
        deps = set()
        for k in reads:
            if k in self.lastw:
                deps.add(self.lastw[k])
        for k in writes:
            if k in self.lastw:
                deps.add(self.lastw[k])
            for r in self.readers.get(k, {}).values():
                deps.update(r)
        for k in reads:
            d = self.readers.setdefault(k, {})
            if dma:
                d.setdefault("dma", []).append(oid)
            else:
                d[eng] = [oid]
        for k in writes:
            self.lastw[k] = oid
            self.readers[k] = {}
        self.ops.append(dict(eng=eng, fn=fn, deps=deps, dma=dma, inc=False))
        return oid

    def barrier(self):
        last = {}
        dmas = []
        for oid, op in enumerate(self.ops):
            if op.get("bar"):
                continue
            if op["dma"]:
                dmas.append(oid)
            else:
                last[op["eng"]] = oid
        deps = set(last.values()) | set(dmas)
        sp_id = len(self.ops)
        self.ops.append(dict(eng="sp", fn=(lambda e: e.sem_inc(self._sp_sem, 1)), deps=set(deps), dma=False, inc=True,
                             bar=True, selfinc=True))
        for e in self.ENGS:
            if e != "sp":
                self.ops.append(dict(eng=e, fn=None, deps={sp_id}, dma=False, inc=False, bar=True))

    def emit(self, nc, nsem_sp=24, nsem_pool=12):
        ops = self.ops
        for op in ops:
            for d in op["deps"]:
                if not ops[d]["dma"]:
                    ops[d]["inc"] = True
        cnt = {e: 0 for e in self.ENGS}
        dcount = {"sp": 0, "pool": 0, "act": 0}
        nsem = {"sp": nsem_sp, "pool": nsem_pool, "act": 4}
        for op in ops:
            e = op["eng"]
            if op["dma"]:
                k = dcount[e]
                dcount[e] += 1
                op["dsem"] = (e, k % nsem[e])
                op["dtarget"] = 16 * (k // nsem[e] + 1)
            elif op["inc"]:
                cnt[e] += 1
                op["val"] = cnt[e]
        import contextlib
        with contextlib.ExitStack() as es:
            sems = {e: es.enter_context(nc.semaphore("s_" + e)) for e in ("pe", "act", "dve", "pool", "sp")}
            self._sp_sem = sems["sp"]
            dsems = {}
            for q in ("sp", "pool"):
                if dcount[q]:
                    for i in range(min(nsem[q], dcount[q])):
                        dsems[(q, i)] = es.enter_context(nc.semaphore("d_%s%d" % (q, i)))
            block = es.enter_context(nc.Block())
            handles = {"pe": block.tensor, "act": block.scalar, "dve": block.vector,
                       "pool": block.gpsimd, "sp": block.sync}
            final_d = {}
            for op in ops:
                if op["dma"]:
                    final_d[op["dsem"]] = max(final_d.get(op["dsem"], 0), op["dtarget"])

            def run_engine(e):
                def body(eng):
                    seen = {}

                    def wait(sem_key, sem, val):
                        if seen.get(sem_key, 0) >= val:
                            return
                        seen[sem_key] = val
                        eng.wait_ge(sem, val)

                    for op in ops:
                        if op["eng"] != e:
                            continue
                        for d in sorted(op["deps"]):
                            dop = ops[d]
                            if dop["dma"]:
                                wait(dop["dsem"], dsems[dop["dsem"]], dop["dtarget"])
                            else:
                                if dop["eng"] == "pe" and e == "pe" and not op["dma"]:
                                    continue
                                wait(dop["eng"], sems[dop["eng"]], dop["val"])
                        if op["fn"] is None:
                            continue
                        if op["dma"]:
                            if op["dtarget"] > 16:
                                wait(op["dsem"], dsems[op["dsem"]], op["dtarget"] - 16)
                            inst = op["fn"](eng)
                            inst.then_inc(dsems[op["dsem"]], 16)
                        else:
                            inst = op["fn"](eng)
                            if op["inc"] and not op.get("selfinc"):
                                inst.then_inc(sems[e], 1)
                    if e == "sp":
                        for k, v in final_d.items():
                            wait(k, dsems[k], v)
                        for x in ("pe", "act", "dve", "pool"):
                            if cnt[x]:
                                wait(x, sems[x], cnt[x])
                return body

            for e in self.ENGS:
                handles[e](run_engine(e))


def storage_pos(p):
    return np.concatenate([np.arange(TS) + PI[p][i] * TS for i in range(NT)])


def const_tables(p):
    bf = ml_dtypes.bfloat16
    pos = storage_pos(p)
    t = {}
    tile_of = np.arange(SEQ) // TS
    kac = np.zeros((16, SEQ), np.float32)
    kac[3:6] = 1.0
    for j in range(NOWN):
        kac[6 + j] = np.where(np.array(PI[p])[tile_of] <= PI[p][j], 0.0, -BIG)
    qac = np.zeros((16, NOWN * TS), np.float32)
    qac[0:3] = 1.0
    for j in range(NOWN):
        qac[6 + j] = (tile_of[:NOWN * TS] == j).astype(np.float32)
    kam = np.zeros((16, SEQ), np.float32)
    blk = np.arange(SEQ) // 256
    for j in range(16):
        kam[j] = (blk == j).astype(np.float32)
    t["kac"] = kac.astype(bf)
    t["qac"] = qac.astype(bf)
    t["kam"] = kam.astype(bf)
    act_blk = np.array([PI[p][j // 2] * 2 + j % 2 for j in range(16)])
    mt = np.zeros((3, 16, 2, 16), np.float32)
    for st in range(16):
        own = st // 2
        valid = (act_blk < act_blk[own]).astype(np.float32)
        mt[0, st, :, :] = (valid - 1.0) * BIG
        mt[1, st, :, :] = valid
        mt[2, st, :, own] = 1.0
    t["mtab"] = np.ascontiguousarray(np.broadcast_to(mt.reshape(1, 3, 512), (128, 3, 512))).astype(bf)
    half = ROPE_DIM // 2
    inv_freq = (np.float32(ROPE_THETA) ** (-np.arange(0, half, dtype=np.float32) * np.float32(2.0) / np.float32(ROPE_DIM))).astype(np.float32)
    ang = (pos.astype(np.float32)[:, None] * inv_freq[None, :]).astype(np.float32)
    cos = np.ones((128, SEQ), np.float32)
    sin = np.zeros((128, SEQ), np.float32)
    for r in range(128):
        d = r % HD
        if d < ROPE_DIM:
            cos[r] = np.cos(ang[:, d % half].astype(np.float64)).astype(np.float32)
            sin[r] = np.sin(ang[:, d % half].astype(np.float64)).astype(np.float32)
    t["cos"] = cos
    t["sin"] = sin
    cm = np.zeros((128, 8, 128), np.float32)
    cm[:, 0, :] = np.eye(128)
    s_idx = np.arange(128)[:, None]
    t_idx = np.arange(128)[None, :]
    cm[:, 1, :] = np.where(s_idx > t_idx, -BIG, 0.0)
    cm[:, 2, :] = 1.0 / 1024.0
    bd = (s_idx // HD == t_idx // HD).astype(np.float32)
    cm[:, 3, :] = bd / 64.0
    rt = np.zeros((128, 128), np.float32)
    for m in range(128):
        d = m % HD
        if d < half:
            rt[m + half, m] = -1.0
        elif d < ROPE_DIM:
            rt[m - half, m] = 1.0
    cm[:, 4, :] = rt
    cm[:, 5, :] = bd
    cm[:, 6, :] = 1.0
    cm[64, 7, :] = 1.0
    t["cmat"] = cm.astype(bf)
    dm = np.zeros((128, 4, 512), np.float32)
    for s4 in range(4):
        dm[:, s4, :] = np.where((s4 * 128 + np.arange(128))[:, None] > np.arange(512)[None, :], -BIG, 0.0)
    t["dmask"] = dm.astype(bf)
    ao = np.zeros((8, 8, 8), np.float32)
    for i in range(8):
        for j in range(8):
            ao[:, i, j] = 1.0 if PI[p][j] < PI[p][i] else 0.0
    t["aoff"] = ao
    return t


def build_program(stop_after=None, dbg=()):
    nc = bass.Bass("TRN2", target_bir_lowering=False)
    R = Rec()

    def din(name, shape, dt=F32):
        return nc.dram_tensor(name, list(shape), dt, kind="ExternalInput").ap()

    xT = din("xT", [D, SEQ])
    xo = din("xo", [NOWN * TS, D])
    w_in = din("w_in", [D, INW])
    w_fox = din("w_fox", [512, D])
    w_moba = din("w_moba", [512, D])
    w_out = din("w_out", [D, D])
    gng_d = din("gng", [128, 8])
    gains_d = din("gains", [128, 4])
    bgate_d = din("bgate", [128, 16])
    bf_d = din("bf", [8, 1])
    kac_d = din("kac", [16, SEQ], BF16)
    qac_d = din("qac", [16, NOWN * TS], BF16)
    kam_d = din("kam", [16, SEQ], BF16)
    mtab_d = din("mtab", [128, 3, 512], BF16)
    cos_d = din("cos", [128, SEQ])
    sin_d = din("sin", [128, SEQ])
    cmat_d = din("cmat", [128, 8, 128], BF16)
    aoff_d = din("aoff", [8, 8, 8])
    dmask_d = din("dmask", [128, 4, 512], BF16)
    out_d = nc.dram_tensor("out", [NOWN * TS, D], F32, kind="ExternalOutput").ap()
    cscr = nc.dram_tensor("cscr", [8, 6, SEQ], BF16).ap()
    dbg_out = {}
    for name, shape, dt in dbg:
        dbg_out[name] = nc.dram_tensor(name, list(shape), dt, kind="ExternalOutput").ap()

    cur = [16640]
    OFFS = {}

    def alloc(name, shape, dt):
        sz = int(np.prod(shape[1:])) * (2 if dt == BF16 else 4)
        off = (cur[0] + 31) // 32 * 32
        t = nc.alloc_sbuf_tensor_at(name, list(shape), dt, offset=off)
        cur[0] = off + sz
        OFFS[name] = (off, off + sz)
        return t

    HT = alloc("HT", [128, 8, SEQ], BF16)
    YT = alloc("YT", [128, 8, NOWN * TS], BF16)
    CM = alloc("CM", [128, 8, 128], BF16)
    GNG = alloc("GNG", [128, 8], F32)
    GAINS = alloc("GAINS", [128, 4], F32)
    BGATE = alloc("BGATE", [128, 16], F32)
    BFT = alloc("BFT", [128, 2], F32)
    EPSC = alloc("EPSC", [128, 2], F32)
    phase_base = cur[0]
    IDENT = CM[:, 0, :]
    TRI = CM[:, 1, :]
    ONESM = CM[:, 2, :]
    BD = CM[:, 3, :]
    RT = CM[:, 4, :]
    BDQ = CM[:, 5, :]
    ONES1 = CM[:, 6, :]
    ONESEL = CM[:, 7, :]

    PS = [nc.alloc_psum_tensor("ps%d" % i, [128, 1024], F32) for i in range(4)]

    def bank(b):
        return PS[b // 2][:, (b % 2) * 512:(b % 2 + 1) * 512]

    def dma(q, out, in_, reads=(), writes=()):
        return R.add(q, lambda e: e.dma_start(out=out, in_=in_), reads=reads, writes=writes, dma=True)

    dma("sp", CM[:], cmat_d, writes=["CM"])
    dma("sp", GNG[:], gng_d, writes=["GNG"])
    dma("sp", GAINS[:], gains_d, writes=["GAINS"])
    dma("sp", BGATE[:], bgate_d, writes=["BGATE"])
    dma("sp", BFT[0:8, 0:1], bf_d, writes=["BFT"])
    R.add("dve", lambda e: e.memset(EPSC[:, 0:1], EPS), writes=["EPSC"])
    R.add("dve", lambda e: e.memset(EPSC[:, 1:2], EPS * 64.0), writes=["EPSC"])

    if stop_after == "c0":
        R.add("dve", lambda e: e.memset(HT[:, 0, 0:512], 1.0), writes=[("HT", 0)])
        R.barrier()
        dma("sp", dbg_out["HT"][:, 0, 0:512], HT[:, 0, 0:512], reads=[("HT", 0)])
        R.emit(nc)
        return nc
    cur[0] = phase_base
    KT = [alloc("KA", [128, SEQ], BF16), alloc("KB", [128, SEQ], BF16)]
    QT = [alloc("QA", [128, NOWN * TS], BF16), alloc("QB", [128, NOWN * TS], BF16)]
    VT = [alloc("VA", [128, 32, 66], BF16), alloc("VB", [128, 32, 128], BF16)]
    ZS = alloc("ZS", [128, NOWN * TS], BF16)
    WQ = alloc("WQ", [128, 8, 128], BF16)
    WK = alloc("WK", [128, 8, 128], BF16)
    WZ = alloc("WZ", [128, 8, 128], BF16)
    WV = alloc("WV", [128, 8, 128], BF16)
    PT = [alloc("PT%d" % i, [128, 1024], BF16) for i in range(3)]
    SQ1 = [alloc("SQ1_%d" % i, [128, TS], BF16) for i in range(2)]
    RS = [alloc("RS%d" % i, [128, TS], F32) for i in range(2)]
    T1 = [alloc("T1_%d" % i, [128, TS], F32) for i in range(2)]
    _rec = alloc("REC", [128, TS], F32)
    REC = [_rec, _rec]
    RH = [alloc("RHA", [128, TS], BF16), alloc("RHB", [128, TS], BF16)]
    RL = [alloc("RLA", [128, TS], BF16), alloc("RLB", [128, TS], BF16)]
    _ytmp = alloc("YTMP", [128, TS], F32)
    YTMP = [_ytmp, _ytmp]
    DMASK = alloc("DMASK", [128, 4, 512], BF16)
    moba_base = cur[0]
    ABT = [[alloc("ABT%d_%d" % (i, k), [128, TS], BF16) for k in range(2)] for i in range(2)]
    RCT = [[alloc("RCT%d_%d" % (i, k), [128, TS], F32) for k in range(2)] for i in range(2)]
    COST = [alloc("COST%d" % i, [128, TS], F32) for i in range(2)]
    SINT = [alloc("SINT%d" % i, [128, TS], F32) for i in range(2)]
    MTAB = alloc("MTAB", [128, 3, 512], BF16)
    GM = alloc("GM", [128, 512], F32)
    T8 = alloc("T8", [128, 32, 8], F32)
    SEL = alloc("SEL", [128, 512], F32)
    PEN = [alloc("PENA", [128, 16, 80], BF16), alloc("PENB", [128, 16, 16], BF16)]
    KM = alloc("KM", [128, 16], F32)
    KMR = alloc("KMR", [128, 16], F32)
    KMH = alloc("KMH", [128, 16], BF16)
    KML = alloc("KML", [128, 16], BF16)
    KMH2 = alloc("KMH2", [128, 2, 16], BF16)
    KML2 = alloc("KML2", [128, 2, 16], BF16)
    SB_LIMIT = 16512 + 212863
    print("phase1 sbuf end", cur[0], "moba_base", moba_base, "limit", SB_LIMIT)
    assert cur[0] <= SB_LIMIT, cur[0]
    cur[0] = phase_base
    WO = alloc("WO", [128, 8, D], BF16)
    SA = [alloc("SA%d" % i, [128, TS], F32) for i in range(2)]
    SB = [alloc("SB%d" % i, [128, TS], F32) for i in range(2)]
    TT = [alloc("TT%d" % i, [128, TS], F32) for i in range(2)]
    MTT = alloc("MTT", [128, 8, TS], BF16)
    assert cur[0] <= OFFS["WQ"][0], (cur[0], OFFS["WQ"])
    cur[0] = OFFS["WQ"][0]
    WF = alloc("WF", [128, 4, D], BF16)
    assert cur[0] <= OFFS["WV"][1]
    cur[0] = OFFS["WV"][1]
    WM = alloc("WM", [128, 4, D], BF16)
    XO = [alloc("XO%d" % i, [128, D], F32) for i in range(2)]
    OT = [alloc("OT%d" % i, [128, D], F32) for i in range(2)]
    assert cur[0] <= moba_base, (cur[0], moba_base)
    cur[0] = moba_base
    WG = alloc("WG", [128, 8, 2048], BF16)
    assert cur[0] <= SB_LIMIT, cur[0]
    MOBA_KEYS = ([("abt", a_, k_) for a_ in range(2) for k_ in range(2)] + [("rct", a_, k_) for a_ in range(2) for k_ in range(2)]
                 + [("cost", a_) for a_ in range(2)] + [("sint", a_) for a_ in range(2)] + ["MTAB", "GM", "SEL", "PENc", "KMH", "KMR", "KML", "KMH2c"]
                 + [("T8", g_) for g_ in range(32)] + [("PEN", h_) for h_ in range(2)] + [("KM", i_) for i_ in range(NT)]
                 + [("KMH2", h_) for h_ in range(2)] + [("KML2", h_) for h_ in range(2)])

    w3 = w_in.rearrange("(c p) n -> p c n", p=128)

    def A(eng, fn, reads=(), writes=()):
        return R.add(eng, fn, reads=reads, writes=writes)

    cur[0] = moba_base
    WFL = alloc("WFL", [128, 8, 8], BF16)
    FLE = alloc("FLE", [8, TS], F32)
    CL = alloc("CL", [8, SEQ], F32)
    ONES8 = alloc("ONES8", [8, TS], F32)
    TOT = alloc("TOT", [8, 8], F32)
    OFF = alloc("OFF", [8, 8], F32)
    TMP8 = alloc("TMP8", [8, 8], F32)
    AOFF = alloc("AOFF", [8, 8, 8], F32)
    CC = alloc("CC", [8, TS], F32)
    R1 = alloc("R1", [8, TS], F32)
    _pie = alloc("PIE", [8, 6, TS], BF16)
    PIE = [_pie, _pie]
    assert cur[0] <= SB_LIMIT, cur[0]
    dma("pool", WFL[:], w3[:, :, 6144:6152], writes=["WFL"])
    dma("sp", AOFF[:], aoff_d, writes=["AOFF"])
    A("dve", lambda e: e.memset(ONES8[:], 1.0), writes=["ONES8"])
    A("dve", lambda e: e.tensor_scalar(out=BFT[0:8, 1:2], in0=BFT[0:8, 0:1], scalar1=-1.0, scalar2=None, op0=ALU.mult),
      reads=["BFT"], writes=["NBF"])

    import os as _os
    pairs = [(0, hp) for hp in range(4)] + [(1, hp) for hp in range(4)]
    if _os.environ.get("K_PAIRS"):
        pairs = [tuple(int(v) for v in t.split(":")) for t in _os.environ["K_PAIRS"].split(",")]

    def load_pair_weights(br, hp):
        base = br * 2048
        dma("pool", WK[:], w3[:, :, base + 512 + hp * 128: base + 512 + (hp + 1) * 128], writes=["WK"])
        dma("pool", WV[:], w3[:, :, base + 1024 + hp * 128: base + 1024 + (hp + 1) * 128], writes=["WV"])
        dma("pool", WQ[:], w3[:, :, base + hp * 128: base + (hp + 1) * 128], writes=["WQ"])
        dma("pool", WZ[:], w3[:, :, base + 1536 + hp * 128: base + 1536 + (hp + 1) * 128], writes=["WZ"])
    load_pair_weights(*pairs[0])

    def phase_c_front_pe(i):
        pb = 2 + i % 2
        for c in range(8):
            A("pe", lambda e, c=c: e.matmul(bank(pb)[0:8, :], lhsT=WFL[:, c, :], rhs=HT[:, c, i * TS:(i + 1) * TS],
                                            start=(c == 0), stop=(c == 7)),
              reads=["WFL", ("HT", i, c)], writes=[("bank", pb)])

    def phase_c_front_rest(i):
        pb = 2 + i % 2
        A("act", lambda e: e.activation(out=FLE[:], in_=bank(pb)[0:8, :], func=AF.Exp, bias=BFT[0:8, 1:2], scale=-1.0),
          reads=[("bank", pb), "NBF"], writes=["fle"])
        A("act", lambda e: e.activation(out=FLE[:], in_=FLE[:], func=AF.Ln, bias=1.0, scale=1.0),
          reads=["fle"], writes=["fle"])
        A("dve", lambda e: e.tensor_tensor_scan(out=CL[:, i * TS:(i + 1) * TS], data0=ONES8[:], data1=FLE[:],
                                                initial=0.0, op0=ALU.mult, op1=ALU.add),
          reads=["fle", "ONES8"], writes=[("cl", i)])
        A("dve", lambda e: e.tensor_copy(out=TOT[:, i:i + 1], in_=CL[:, i * TS + TS - 1:i * TS + TS]),
          reads=[("cl", i)], writes=["TOT"])

    def phase_c_tail_job():
        for i in range(NT):
            A("dve", lambda e, i=i: e.tensor_tensor(out=TMP8[:], in0=AOFF[:, i, :], in1=TOT[:], op=ALU.mult),
              reads=["AOFF", "TOT"], writes=["TMP8"])
            A("dve", lambda e, i=i: e.reduce_sum(out=OFF[:, i:i + 1], in_=TMP8[:], axis=AX.X),
              reads=["TMP8"], writes=["OFF"])
        yield
        for i in range(NT):
            r = 0
            A("dve", lambda e, r=r, i=i: e.tensor_scalar(out=CC[:], in0=CL[:, i * TS:(i + 1) * TS], scalar1=OFF[:, i:i + 1],
                                                         scalar2=-1.0, op0=ALU.add, op1=ALU.mult),
              reads=[("cl", i), "OFF"], writes=["cc"])
            A("dve", lambda e, r=r: e.tensor_copy(out=PIE[r][:, 3, :], in_=CC[:]), reads=["cc"], writes=[("pie", r)])
            A("dve", lambda e, r=r: e.tensor_tensor(out=R1[:], in0=CC[:], in1=PIE[r][:, 3, :], op=ALU.subtract),
              reads=["cc", ("pie", r)], writes=["r1"])
            A("dve", lambda e, r=r: e.tensor_copy(out=PIE[r][:, 4, :], in_=R1[:]), reads=["r1"], writes=[("pie", r)])
            yield
            A("dve", lambda e, r=r: e.tensor_tensor(out=CC[:], in0=R1[:], in1=PIE[r][:, 4, :], op=ALU.subtract),
              reads=["r1", ("pie", r)], writes=["cc"])
            A("dve", lambda e, r=r: e.tensor_copy(out=PIE[r][:, 5, :], in_=CC[:]), reads=["cc"], writes=[("pie", r)])
            A("dve", lambda e, r=r: e.tensor_scalar(out=PIE[r][:, 0:3, :], in0=PIE[r][:, 3:6, :], scalar1=-1.0, scalar2=None, op0=ALU.mult),
              reads=[("pie", r)], writes=[("pie", r)])
            dma("sp", cscr[:, :, i * TS:(i + 1) * TS], PIE[r][:], reads=[("pie", r)], writes=["cscr"])
            yield

    cur[0] = phase_base
    XT8 = [alloc("XT8_%d" % i, [128, 8, TS], F32) for i in range(2)]
    SQ0 = [alloc("SQ0_%d" % i, [128, TS], BF16) for i in range(2)]
    RSTD = [alloc("RSTD_%d" % i, [128, TS], F32) for i in range(2)]
    cur[0] = OFFS["PT0"][0]
    XT8.append(alloc("XT8_2", [128, 8, TS], F32))
    assert cur[0] <= OFFS["REC"][0], (cur[0], OFFS["REC"])
    import os
    P0T = int(os.environ.get("P0_TILES", NT))
    SKIP = set(os.environ.get("P0_SKIP", "").split(","))
    for i in range(P0T):
        b = i % 3
        rb = i % 2
        for c in range(8):
            dma("sp", XT8[b][:, c, :], xT[c * 128:(c + 1) * 128, i * TS:(i + 1) * TS],
                writes=[("xt8", b, c)])
        pb = i % 2
        if i >= 2:
            phase_c_front_pe(i - 2)
        for c in range(8):
            R.add("act", lambda e, b=b, c=c: e.activation(out=SQ0[c % 2][:], in_=XT8[b][:, c, :], func=AF.Square),
                  reads=[("xt8", b, c)], writes=[("sq0", c % 2)])
            R.add("pe", lambda e, c=c, pb=pb: e.matmul(bank(pb), lhsT=ONESM, rhs=SQ0[c % 2][:], start=(c == 0), stop=(c == 7)),
                  reads=[("sq0", c % 2), "CM"], writes=[("bank", pb)])
        R.add("act", lambda e, rb=rb, pb=pb: e.activation(out=RSTD[rb][:], in_=bank(pb), func=AF.Ln, bias=EPSC[:, 0:1], scale=1.0),
              reads=[("bank", pb), "EPSC"], writes=[("rstd", rb)])
        R.add("act", lambda e, rb=rb: e.activation(out=RSTD[rb][:], in_=RSTD[rb][:], func=AF.Exp, scale=-0.5),
              reads=[("rstd", rb)], writes=[("rstd", rb)])
        for c in range(8):
            R.add("dve", lambda e, b=b, rb=rb, c=c, i=i: e.scalar_tensor_tensor(
                out=HT[:, c, i * TS:(i + 1) * TS], in0=XT8[b][:, c, :], scalar=GNG[:, c:c + 1], in1=RSTD[rb][:],
                op0=ALU.mult, op1=ALU.mult),
                reads=[("xt8", b, c), ("rstd", rb), "GNG"], writes=[("HT", i, c)])
        if i >= 2:
            phase_c_front_rest(i - 2)
    for i in range(max(P0T - 2, 0), P0T):
        phase_c_front_pe(i)
        phase_c_front_rest(i)
    R.barrier()

    if "HT" in dbg_out:
        for c in range(8):
            dma("sp", dbg_out["HT"][:, c, :], HT[:, c, :], reads=[("HT", i, c) for i in range(NT)])

    if stop_after == "p0":
        R.emit(nc)
        return nc

    HROWS = [slice(0, 80), slice(0, 128)]
    MROWS = [slice(0, 64), slice(64, 128)]
    AUG0 = [64, 0]
    VCOLS = [66, 128]
    DENROW = [slice(64, 65), slice(0, 1)]
    plists = past_lists()

    A("pool", lambda e: e.memset(KT[1][0:64, :], 0.0), writes=[("Kaug", 1)])
    A("pool", lambda e: e.memset(QT[1][0:64, :], 0.0), writes=[("Qaug", 1)])
    A("pool", lambda e: e.memset(QT[0][64:80, :], 0.0), writes=[("Qaug", 0)])
    A("pool", lambda e: e.memset(RH[0][0:64, :], 0.0), writes=["RHc"])
    A("pool", lambda e: e.memset(RL[0][0:64, :], 0.0), writes=["RHc"])
    A("pool", lambda e: e.memset(VT[0][:, :, 64:66], 1.0), writes=[("Vc", 0)])
    A("pool", lambda e: e.memset(VT[1][:, :, 0:2], 1.0), writes=[("Vc", 1)])
    A("pool", lambda e: e.memset(VT[1][:, :, 2:64], 0.0), writes=[("Vc", 1)])
    dma("sp", DMASK[:], dmask_d, writes=["DMASK"])

    bank_rot = [0]

    NROT = [8]

    def next_bank():
        b = bank_rot[0] % NROT[0]
        bank_rot[0] = (b + 1) % NROT[0]
        return b

    free_banks = list(range(8))

    def acquire():
        assert free_banks, "out of PSUM banks"
        return free_banks.pop(0)

    def release(b):
        assert b not in free_banks
        free_banks.append(b)

    rotn = {"sq": (0, 2), "rs": (0, 2), "t": (0, 2), "cs": (0, 2), "rc": (0, 2), "ab": (0, 2)}

    def nxt(k):
        v, n = rotn[k]
        rotn[k] = ((v + 1) % n, n)
        return v

    def proj_feat(W, wkey, i, bk):
        for c in range(8):
            A("pe", lambda e, c=c: e.matmul(bank(bk), lhsT=W[:, c, :], rhs=HT[:, c, i * TS:(i + 1) * TS],
                                            start=(c == 0), stop=(c == 7)),
              reads=[wkey, ("HT", i, c)], writes=[("bank", bk)])

    def qk_job(br, isq, W, wkey, i):
        gcol = br * 2 + (0 if isq else 1)
        dst = QT if isq else KT
        dkey = "Q" if isq else "K"
        cols = slice(i * TS, (i + 1) * TS)
        bk = acquire()
        proj_feat(W, wkey, i, bk)
        yield
        sq = nxt("sq")
        A("act", lambda e: e.activation(out=SQ1[sq][:], in_=bank(bk), func=AF.Square),
          reads=[("bank", bk)], writes=[("sq1", sq)])
        by = acquire()
        A("pe", lambda e: e.matmul(bank(by), lhsT=(BDQ if isq else BD), rhs=SQ1[sq][:], start=True, stop=True),
          reads=[("sq1", sq), "CM"], writes=[("bank", by)])
        if br == 1:
            cs = nxt("cs")
            dma("sp", COST[cs][:], cos_d[:, cols], writes=[("cost", cs)])
            dma("sp", SINT[cs][:], sin_d[:, cols], writes=[("sint", cs)])
        yield
        r = nxt("rs")
        ec = 1 if isq else 0
        A("act", lambda e: e.activation(out=RS[r][:], in_=bank(by), func=AF.Ln, bias=EPSC[:, ec:ec + 1], scale=1.0),
          reads=[("bank", by), "EPSC"], writes=[("rs", r)])
        A("act", lambda e: e.activation(out=RS[r][:], in_=RS[r][:], func=AF.Exp, scale=-0.5),
          reads=[("rs", r)], writes=[("rs", r)])
        release(by)
        if br == 0:
            for hd in range(2):
                mr = MROWS[hd]
                A("dve", lambda e, hd=hd, mr=mr: e.scalar_tensor_tensor(
                    out=dst[hd][mr, cols], in0=bank(bk)[mr, :], scalar=GAINS[mr, gcol:gcol + 1], in1=RS[r][mr, :],
                    op0=ALU.mult, op1=ALU.mult),
                  reads=[("bank", bk), ("rs", r), "GAINS"], writes=[(dkey, hd, i)])
            release(bk)
            return
        rc = nxt("rc")
        ab = nxt("ab")
        A("pool", lambda e: e.tensor_tensor(out=RCT[rc][0][:], in0=RS[r][:], in1=COST[cs][:], op=ALU.mult),
          reads=[("rs", r), ("cost", cs)], writes=[("rct", rc, 0)])
        A("pool", lambda e: e.tensor_tensor(out=RCT[rc][1][:], in0=RS[r][:], in1=SINT[cs][:], op=ALU.mult),
          reads=[("rs", r), ("sint", cs)], writes=[("rct", rc, 1)])
        for k2 in range(2):
            A("dve", lambda e, k2=k2: e.scalar_tensor_tensor(
                out=ABT[ab][k2][:], in0=bank(bk), scalar=GAINS[:, gcol:gcol + 1], in1=RCT[rc][k2][:], op0=ALU.mult, op1=ALU.mult),
              reads=[("bank", bk), ("rct", rc, k2), "GAINS"], writes=[("abt", ab, k2)])
        release(bk)
        yield
        bz = acquire()
        A("pe", lambda e: e.matmul(bank(bz), lhsT=IDENT, rhs=ABT[ab][0][:], start=True, stop=False),
          reads=[("abt", ab, 0), "CM"], writes=[("bank", bz)])
        A("pe", lambda e: e.matmul(bank(bz), lhsT=RT, rhs=ABT[ab][1][:], start=False, stop=True),
          reads=[("abt", ab, 1), "CM"], writes=[("bank", bz)])
        yield
        for hd in range(2):
            mr = MROWS[hd]
            A("act", lambda e, hd=hd, mr=mr: e.activation(out=dst[hd][mr, cols], in_=bank(bz)[mr, :], func=AF.Copy),
              reads=[("bank", bz)], writes=[(dkey, hd, i)])
        if not isq:
            for hd in range(2):
                mr = MROWS[hd]
                A("dve", lambda e, hd=hd, mr=mr: e.reduce_sum(out=KM[mr, 2 * i:2 * i + 2],
                                                             in_=bank(bz)[mr, :].rearrange("p (b l) -> p b l", l=256), axis=AX.X),
                  reads=[("bank", bz)], writes=[("KM", i)])
        release(bz)

    def z_job(i):
        bk = acquire()
        proj_feat(WZ, "WZ", i, bk)
        yield
        t = nxt("t")
        A("act", lambda e: e.activation(out=T1[t][:], in_=bank(bk), func=AF.Exp, scale=-1.0),
          reads=[("bank", bk)], writes=[("t1", t)])
        A("act", lambda e: e.activation(out=T1[t][:], in_=T1[t][:], func=AF.Ln, bias=1.0, scale=1.0),
          reads=[("t1", t)], writes=[("t1", t)])
        A("act", lambda e: e.activation(out=T1[t][:], in_=T1[t][:], func=AF.Exp, scale=-1.0),
          reads=[("t1", t)], writes=[("t1", t)])
        A("dve", lambda e: e.tensor_tensor(out=ZS[:, i * TS:(i + 1) * TS], in0=T1[t][:], in1=bank(bk), op=ALU.mult),
          reads=[("t1", t), ("bank", bk)], writes=[("ZS", i)])
        release(bk)

    def v_job(g):
        bk = acquire()
        for s4 in range(4):
            st = 4 * g + s4
            for c in range(8):
                A("pe", lambda e, c=c, st=st, s4=s4: e.matmul(bank(bk)[:, s4 * 128:(s4 + 1) * 128],
                                                            lhsT=HT[:, c, st * 128:(st + 1) * 128], rhs=WV[:, c, :],
                                                            start=(c == 0), stop=(c == 7)),
                  reads=["WV", ("HT", st // 4, c)], writes=[("bank", bk)])
        yield
        src = bank(bk).rearrange("p (s n) -> p s n", n=128)
        A("act", lambda e: e.activation(out=VT[0][:, 4 * g:4 * g + 4, 0:64], in_=src[:, :, 0:64], func=AF.Copy),
          reads=[("bank", bk)], writes=[("V", 0, g)])
        A("act", lambda e: e.activation(out=VT[1][:, 4 * g:4 * g + 4, 64:128], in_=src[:, :, 64:128], func=AF.Copy),
          reads=[("bank", bk)], writes=[("V", 1, g)])
        release(bk)

    def run_pipeline(jobs):
        pend = list(jobs)
        active = []
        done = set()
        nstep = [0]
        held = []
        if pending:
            for b_ in (6, 7):
                free_banks.remove(b_)
                held.append(b_)
        while pend or active:
            for k, (gnr, after) in enumerate(pend):
                if all(id(a_) in done for a_ in after):
                    active.append(gnr)
                    pend.pop(k)
                    break
            for gnr in reversed(list(active)):
                try:
                    next(gnr)
                except StopIteration:
                    active.remove(gnr)
                    done.add(id(gnr))
            nstep[0] += 1
            if nstep[0] == 3:
                flush_pending()
                while held:
                    release(held.pop())
        assert not held

    def attention_pair(br, hp):
        chunk = br * 4 + hp
        flat = []
        for (j, hd) in [(0, 0), (3, 1), (1, 0), (2, 1), (2, 0), (1, 1), (3, 0), (0, 1)]:
            past = []
            for i in plists[j]:
                past += [dict(ks=i * 4 + s4, mk=None, q0=0) for s4 in range(4)]
            dg = [dict(ks=j * 4 + s4, mk=s4, q0=s4 * 128) for s4 in range(4)]
            g1 = [dict(dg[0], b=0, c=0), dict(dg[1], b=1, c=0), dict(dg[3], b=1, c=384)]
            g2 = [dict(dg[2], b=0, c=0), dict(past[0], b=1, c=0)]
            grs = [g1, g2]
            rest = past[1:]
            for k in range(0, len(rest), 2):
                grs.append([dict(e_, b=bi, c=0) for bi, e_ in enumerate(rest[k:k + 2])])
            for gi, g in enumerate(grs):
                flat.append(dict(hd=hd, j=j, ents=g, first=(gi == 0), last=(gi == len(grs) - 1)))
        ngr = len(flat)

        def qk(n):
            G = flat[n]
            hd, j = G["hd"], G["j"]
            Kt, Qt, rows = KT[hd], QT[hd], HROWS[hd]
            sb = n % 3
            for E in G["ents"]:
                ks, mk, q0 = E["ks"], E["mk"], E["q0"]
                c0 = E["b"] * 512 + E["c"]
                c1 = c0 + (512 - q0)
                kreads = [("K", hd, ks // 4), ("Kaug", hd), ("Q", hd, j), ("Qaug", hd)]
                A("pe", lambda e, ks=ks, c0=c0, c1=c1, mk=mk, q0=q0: e.matmul(
                    PS[sb][:, c0:c1], lhsT=Kt[rows, ks * 128:(ks + 1) * 128], rhs=Qt[rows, j * TS + q0:(j + 1) * TS],
                    start=True, stop=(mk is None)),
                  reads=kreads, writes=[("bank", 2 * sb + E["b"])])
                if mk is not None:
                    A("pe", lambda e, c0=c0, c1=c1, mk=mk, q0=q0: e.matmul(PS[sb][:, c0:c1], lhsT=IDENT, rhs=DMASK[:, mk, q0:512],
                                                                         start=False, stop=True),
                      reads=["CM", "DMASK"], writes=[("bank", 2 * sb + E["b"])])

        def ex(n):
            sb = n % 3
            pb = n % 3
            banks = sorted(set(E["b"] for E in flat[n]["ents"]))
            rngs = sorted((E["b"] * 512 + E["c"], E["b"] * 512 + E["c"] + 512 - E["q0"]) for E in flat[n]["ents"])
            merged = []
            for (r0, r1) in rngs:
                if merged and merged[-1][1] == r0:
                    merged[-1][1] = r1
                else:
                    merged.append([r0, r1])
            for (r0, r1) in merged:
                A("act", lambda e, r0=r0, r1=r1: e.activation(out=PT[pb][:, r0:r1], in_=PS[sb][:, r0:r1], func=AF.Exp),
                  reads=[("bank", 2 * sb + b_) for b_ in banks], writes=[("pt", pb)])

        def pv(n):
            G = flat[n]
            hd, j = G["hd"], G["j"]
            Vt, vc, ob = VT[hd], VCOLS[hd], 6 + hd
            pb = n % 3
            ne = len(G["ents"])
            for e_i, E in enumerate(G["ents"]):
                ks, q0 = E["ks"], E["q0"]
                c0 = E["b"] * 512 + E["c"]
                c1 = c0 + (512 - q0)
                first = G["first"] and e_i == 0
                last = G["last"] and e_i == ne - 1
                A("pe", lambda e, ks=ks, c0=c0, c1=c1, q0=q0, first=first, last=last: e.matmul(
                    bank(ob)[0:vc, q0:512], lhsT=Vt[:, ks, 0:vc], rhs=PT[pb][:, c0:c1], start=first, stop=last,
                    skip_group_check=True),
                  reads=[("pt", pb), ("V", hd, ks // 4), ("Vc", hd)], writes=[("bank", ob)])
            if G["last"]:
                finalize1(hd, j, ob)

        def finalize1(hd, j, ob):
            dr = DENROW[hd]
            mr = MROWS[hd]
            A("dve", lambda e: e.reciprocal(out=REC[hd][dr, :], in_=bank(ob)[dr, :]), reads=[("bank", ob)], writes=[("REC", hd)])
            A("dve", lambda e: e.tensor_copy(out=RH[hd][dr, :], in_=REC[hd][dr, :]), reads=[("REC", hd)], writes=[("RH", hd)])
            A("dve", lambda e: e.tensor_tensor(out=RL[hd][dr, :], in0=REC[hd][dr, :], in1=RH[hd][dr, :], op=ALU.subtract),
              reads=[("REC", hd), ("RH", hd)], writes=[("RL", hd)])
            A("dve", lambda e: e.tensor_tensor(out=YTMP[hd][mr, :], in0=bank(ob)[mr, :], in1=ZS[mr, j * TS:(j + 1) * TS], op=ALU.mult),
              reads=[("bank", ob), ("ZS", j)], writes=[("YTMP", hd)])

            def stage2():
                if hd == 0:
                    bl, br_ = ONESEL[0:65, :], slice(0, 65)
                else:
                    bl, br_ = ONES1[0:1, :], slice(0, 1)
                A("pe", lambda e: e.matmul(bank(ob), lhsT=bl, rhs=RH[hd][br_, :], start=True, stop=False),
                  reads=[("RH", hd), "RHc", "CM"], writes=[("bank", ob)])
                A("pe", lambda e: e.matmul(bank(ob), lhsT=bl, rhs=RL[hd][br_, :], start=False, stop=True),
                  reads=[("RL", hd), "RHc", "CM"], writes=[("bank", ob)])
                A("dve", lambda e: e.tensor_tensor(out=YT[mr, chunk, j * TS:(j + 1) * TS], in0=YTMP[hd][mr, :], in1=bank(ob)[mr, :], op=ALU.mult),
                  reads=[("YTMP", hd), ("bank", ob)], writes=[("YT", chunk, j)])
            pending.append([8, stage2])

        LOOK = 2
        for n in range(min(LOOK, ngr)):
            qk(n)
            ex(n)
        for n in range(ngr):
            if n + LOOK < ngr:
                qk(n + LOOK)
                ex(n + LOOK)
            for it in list(pending):
                it[0] -= 1
                if it[0] <= 0:
                    pending.remove(it)
                    it[1]()
            pv(n)

    pending = []

    def flush_pending():
        while pending:
            pending.pop(0)[1]()

    def gating_job():
        A("dve", lambda e: e.tensor_copy(out=KMH[:], in_=KM[:]), reads=[("KM", i) for i in range(NT)], writes=["KMH"])
        A("dve", lambda e: e.tensor_tensor(out=KMR[:], in0=KM[:], in1=KMH[:], op=ALU.subtract),
          reads=[("KM", i) for i in range(NT)] + ["KMH"], writes=["KMR"])
        A("dve", lambda e: e.tensor_copy(out=KML[:], in_=KMR[:]), reads=["KMR"], writes=["KML"])
        for hd in range(2):
            mr = MROWS[hd]
            A("dve", lambda e, hd=hd, mr=mr: e.tensor_copy(out=KMH2[mr, hd, :], in_=KMH[mr, :]), reads=["KMH", "KMH2c"], writes=[("KMH2", hd)])
            A("dve", lambda e, hd=hd, mr=mr: e.tensor_copy(out=KML2[mr, hd, :], in_=KML[mr, :]), reads=["KML", "KMH2c"], writes=[("KML2", hd)])
        yield
        gb_ = acquire()
        g4 = bank(gb_).rearrange("p (s h j) -> p s h j", h=2, j=16)
        for st in range(16):
            for hd in range(2):
                hr = HROWS[hd]
                A("pe", lambda e, st=st, hd=hd, hr=hr: e.matmul(g4[:, st, hd, :], lhsT=QT[hd][hr, st * 128:(st + 1) * 128], rhs=KMH2[hr, hd, :],
                                                                start=True, stop=False),
                  reads=[("Q", hd, st // 4), ("Qaug", hd), ("KMH2", hd), "KMH2c"], writes=[("bank", gb_)])
                A("pe", lambda e, st=st, hd=hd, hr=hr: e.matmul(g4[:, st, hd, :], lhsT=QT[hd][hr, st * 128:(st + 1) * 128], rhs=KML2[hr, hd, :],
                                                                start=False, stop=True),
                  reads=[("Q", hd, st // 4), ("Qaug", hd), ("KML2", hd), "KMH2c"], writes=[("bank", gb_)])
        yield
        A("dve", lambda e: e.tensor_tensor(out=GM[:], in0=bank(gb_), in1=MTAB[:, 0, :], op=ALU.add),
          reads=[("bank", gb_), "MTAB"], writes=["GM"])
        release(gb_)
        for grp in range(32):
            A("dve", lambda e, grp=grp: e.max(out=T8[:, grp, :], in_=GM[:, grp * 16:(grp + 1) * 16]),
              reads=["GM"], writes=[("T8", grp)])
        yield
        gm3 = GM[:].rearrange("p (g j) -> p g j", j=16)
        sel3 = SEL[:].rearrange("p (g j) -> p g j", j=16)
        A("dve", lambda e: e.tensor_tensor(out=sel3, in0=gm3, in1=T8[:, :, 2:3].to_broadcast([128, 32, 16]), op=ALU.is_ge),
          reads=["GM"] + [("T8", grp) for grp in range(32)], writes=["SEL"])
        A("dve", lambda e: e.tensor_tensor(out=SEL[:], in0=SEL[:], in1=MTAB[:, 1, :], op=ALU.mult), reads=["SEL", "MTAB"], writes=["SEL"])
        A("dve", lambda e: e.tensor_tensor(out=SEL[:], in0=SEL[:], in1=MTAB[:, 2, :], op=ALU.add), reads=["SEL", "MTAB"], writes=["SEL"])
        sel4 = SEL[:].rearrange("p (s h j) -> p s h j", h=2, j=16)
        for hd in range(2):
            a0 = AUG0[hd]
            A("dve", lambda e, hd=hd, a0=a0: e.tensor_scalar(out=PEN[hd][:, :, a0:a0 + 16], in0=sel4[:, :, hd, :], scalar1=-1.0, scalar2=BIG,
                                                             op0=ALU.add, op1=ALU.mult),
              reads=["SEL", "PENc"], writes=[("PEN", hd)])
        yield
        for g in range(4):
            for hd in range(2):
                M = 80 if hd == 0 else 16
                cr = slice(64, 80) if hd == 0 else slice(0, 16)
                tb = acquire()
                for s4 in range(4):
                    st = 4 * g + s4
                    A("pe", lambda e, st=st, s4=s4, hd=hd, M=M, tb=tb: e.matmul(bank(tb)[0:M, s4 * 128:(s4 + 1) * 128], lhsT=PEN[hd][:, st, :], rhs=IDENT,
                                                                              start=True, stop=True),
                      reads=[("PEN", hd), "PENc", "CM"], writes=[("bank", tb)])
                A("dve", lambda e, hd=hd, g=g, cr=cr, tb=tb: e.tensor_copy(out=QT[hd][cr, g * TS:(g + 1) * TS], in_=bank(tb)[cr, :]),
                  reads=[("bank", tb)], writes=[("Qaug", hd)])
                release(tb)
            yield

    def do_pair(br, hp):
        if (br, hp) != pairs[0]:
            load_pair_weights(br, hp)

        def c_rows():
            for hd in range(2):
                h = 2 * hp + hd
                a0 = AUG0[hd]
                dma("sp", KT[hd][a0:a0 + 3, :], cscr[h, 0:3, :], reads=["cscr"], writes=[("Kaug", hd)])
                dma("sp", QT[hd][a0 + 3:a0 + 6, :], cscr[h, 3:6, 0:NOWN * TS], reads=["cscr"], writes=[("Qaug", hd)])
        first = not state0["tail_done"]
        if br == 0 and not first:
            c_rows()
        jobs = []
        if first:
            jobs.append((phase_c_tail_job(), []))
            state0["tail_done"] = True
        if br == 0:
            for i in range(NT):
                jobs.append((qk_job(br, False, WK, "WK", i), []))
                jobs.append((v_job(i), []))
            for i in range(NOWN):
                jobs.append((qk_job(br, True, WQ, "WQ", i), []))
                jobs.append((z_job(i), []))
        else:
            kq = [qk_job(br, False, WK, "WK", i) for i in range(NT)] + [qk_job(br, True, WQ, "WQ", i) for i in range(NOWN)]
            jobs += [(g_, []) for g_ in kq]
            jobs.append((gating_job(), kq))
            for i in range(NT):
                jobs.append((v_job(i), []))
                if i < NOWN:
                    jobs.append((z_job(i), []))
        run_pipeline(jobs)
        if br == 0 and first:
            c_rows()
        if first:
            R.barrier()
            A("pool", lambda e: e.memset(KMH2[:], 0.0), writes=["KMH2c"])
            A("pool", lambda e: e.memset(KML2[:], 0.0), writes=["KMH2c"])
            A("pool", lambda e: e.memset(PEN[0][:, :, 0:64], 0.0), writes=["PENc"])
            dma("sp", MTAB[:], mtab_d, writes=["MTAB"])
        if (br, hp) == pairs[-1] and stop_after is None:
            dma("pool", WG[:, :, 0:1024], w3[:, :, 4096:5120], reads=[], writes=["WG0"] + MOBA_KEYS)
            dma("pool", WG[:, :, 1024:2048], w3[:, :, 5120:6144], reads=[], writes=["WG1"] + MOBA_KEYS)
            dma("pool", WF[:], w_fox.rearrange("(c p) n -> p c n", p=128), writes=["WF", "WQ", "WK", "WZ", "WV"])
        attention_pair(br, hp)

    last_br = None
    state0 = {"tail_done": False}
    if pairs[0][0] != 0:
        run_pipeline([(phase_c_tail_job(), [])])
        state0["tail_done"] = True
    for (br, hp) in pairs:
        if br != last_br:
            for hd in range(2):
                a0 = AUG0[hd]
                if br == 0:
                    dma("sp", KT[hd][a0 + 3:a0 + 16, :], kac_d[3:16, :], writes=[("Kaug", hd)])
                    dma("sp", QT[hd][a0:a0 + 3, :], qac_d[0:3, :], writes=[("Qaug", hd)])
                    dma("sp", QT[hd][a0 + 6:a0 + 16, :], qac_d[6:16, :], writes=[("Qaug", hd)])
                else:
                    dma("sp", KT[hd][a0:a0 + 16, :], kam_d[:, :], writes=[("Kaug", hd)])
            last_br = br
        do_pair(br, hp)
    flush_pending()
    R.barrier()
    if "YT" in dbg_out:
        for c in sorted(set(b_ * 4 + h_ for (b_, h_) in pairs)):
            dma("sp", dbg_out["YT"][:, c, :], YT[:, c, :], reads=[("YT", c, j) for j in range(NOWN)])
    if "KA" in dbg_out:
        dma("sp", dbg_out["KA"][0:80, :], KT[0][0:80, :], reads=[("K", 0, i) for i in range(NT)] + [("Kaug", 0)])
        dma("sp", dbg_out["KB"], KT[1][:], reads=[("K", 1, i) for i in range(NT)] + [("Kaug", 1)])
        dma("sp", dbg_out["QA"][0:80, :], QT[0][0:80, :], reads=[("Q", 0, i) for i in range(NOWN)] + [("Qaug", 0)])
        dma("sp", dbg_out["QB"], QT[1][:], reads=[("Q", 1, i) for i in range(NOWN)] + [("Qaug", 1)])
        dma("sp", dbg_out["VB"], VT[1][:], reads=[("V", 1, g) for g in range(8)] + [("Vc", 1)])
        dma("sp", dbg_out["ZS"], ZS[:], reads=[("ZS", i) for i in range(NOWN)])
    if stop_after == "p1":
        R.emit(nc)
        return nc

    dma("pool", WM[:], w_moba.rearrange("(c p) n -> p c n", p=128), writes=["WM"])
    dma("pool", WO[:], w_out.rearrange("(c p) n -> p c n", p=128), writes=["WO"])
    rr = [0]
    xr = [0]
    for j in range(NOWN):
        for n in range(8):
            r = rr[0]
            rr[0] = 1 - r
            ba, bb, bf_, bm = next_bank(), next_bank(), next_bank(), next_bank()
            for gi, (bk, dst) in enumerate([(ba, SA), (bb, SB)]):
                for c in range(8):
                    A("pe", lambda e, c=c, gi=gi, bk=bk, n=n, j=j: e.matmul(
                        bank(bk), lhsT=WG[:, c, gi * 1024 + n * 128: gi * 1024 + (n + 1) * 128], rhs=HT[:, c, j * TS:(j + 1) * TS],
                        start=(c == 0), stop=(c == 7)),
                      reads=["WG%d" % gi, ("HT", j, c)], writes=[("bank", bk)])
                A("act", lambda e, gi=gi, bk=bk, n=n, dst=dst, r=r: e.activation(
                    out=dst[r][:], in_=bank(bk), func=AF.Sigmoid, bias=BGATE[:, gi * 8 + n: gi * 8 + n + 1], scale=1.0),
                  reads=[("bank", bk), "BGATE"], writes=[("sg", gi, r)])
            for (bk, W, wk, c0) in [(bf_, WF, "WF", 0), (bm, WM, "WM", 4)]:
                for c in range(4):
                    A("pe", lambda e, c=c, bk=bk, W=W, c0=c0, n=n, j=j: e.matmul(
                        bank(bk), lhsT=W[:, c, n * 128:(n + 1) * 128], rhs=YT[:, c0 + c, j * TS:(j + 1) * TS],
                        start=(c == 0), stop=(c == 3)),
                      reads=[wk] + [("YT", c0 + cc, j) for cc in range(4)], writes=[("bank", bk)])
            A("dve", lambda e, r=r, bf_=bf_: e.tensor_tensor(out=TT[r][:], in0=bank(bf_), in1=SA[r][:], op=ALU.mult),
              reads=[("bank", bf_), ("sg", 0, r)], writes=[("tt", r)])
            A("dve", lambda e, r=r, bm=bm: e.tensor_tensor(out=SB[r][:], in0=bank(bm), in1=SB[r][:], op=ALU.mult),
              reads=[("bank", bm), ("sg", 1, r)], writes=[("sg", 1, r)])
            A("dve", lambda e, r=r, n=n: e.tensor_tensor(out=MTT[:, n, :], in0=TT[r][:], in1=SB[r][:], op=ALU.add),
              reads=[("tt", r), ("sg", 1, r)], writes=[("mtt", n)])
        for ts4 in range(4):
            x = xr[0]
            xr[0] = 1 - x
            row0 = (j * 4 + ts4) * 128
            dma("sp", XO[x][:], xo[row0:row0 + 128, :], writes=[("xo", x)])
            for half in range(2):
                bo = next_bank()
                for c in range(8):
                    A("pe", lambda e, c=c, bo=bo, half=half, ts4=ts4: e.matmul(
                        bank(bo), lhsT=MTT[:, c, ts4 * 128:(ts4 + 1) * 128], rhs=WO[:, c, half * 512:(half + 1) * 512],
                        start=(c == 0), stop=(c == 7)),
                      reads=["WO"] + [("mtt", cc) for cc in range(8)], writes=[("bank", bo)])
                A("dve", lambda e, x=x, bo=bo, half=half: e.tensor_tensor(
                    out=OT[x][:, half * 512:(half + 1) * 512], in0=bank(bo), in1=XO[x][:, half * 512:(half + 1) * 512], op=ALU.add),
                  reads=[("bank", bo), ("xo", x)], writes=[("ot", x, half)])
            dma("sp", out_d[row0:row0 + 128, :], OT[x][:], reads=[("ot", x, 0), ("ot", x, 1)], writes=[("outd", row0)])
    R.emit(nc)
    return nc


def make_in_maps(inputs):
    x = np.asarray(inputs["x"], np.float32)
    w_in = np.ascontiguousarray(np.asarray(inputs["w_in"], np.float32)[0])
    w_fox = np.ascontiguousarray(np.asarray(inputs["w_fox"], np.float32)[0])
    w_moba = np.ascontiguousarray(np.asarray(inputs["w_moba"], np.float32)[0])
    w_out = np.ascontiguousarray(np.asarray(inputs["w_out"], np.float32)[0])
    gng = np.ascontiguousarray(np.asarray(inputs["norm_g"], np.float32)[0].reshape(8, 128).T)
    gains = np.ascontiguousarray(np.stack([
        np.tile(np.asarray(inputs["fox_q_g"], np.float32)[0], 2),
        np.tile(np.asarray(inputs["fox_k_g"], np.float32)[0], 2),
        np.tile(np.asarray(inputs["moba_q_g"], np.float32)[0], 2),
        np.tile(np.asarray(inputs["moba_k_g"], np.float32)[0], 2)], axis=1))
    bg = np.asarray(inputs["b_gate"], np.float32)[0]
    bgate = np.ascontiguousarray(bg.reshape(2, 8, 128).transpose(2, 0, 1).reshape(128, 16))
    bf = np.ascontiguousarray(np.asarray(inputs["b_f"], np.float32)[0].reshape(8, 1))
    tabs = [const_tables(p) for p in range(2)]
    maps = []
    for core in range(8):
        b, p = core // 2, core % 2
        pos = storage_pos(p)
        xs = x[b][pos]
        m = dict(xT=np.ascontiguousarray(xs.T), xo=np.ascontiguousarray(xs[:NOWN * TS]),
                 w_in=w_in, w_fox=w_fox, w_moba=w_moba, w_out=w_out, gng=gng, gains=gains,
                 bgate=bgate, bf=bf)
        m.update(tabs[p])
        maps.append(m)
    return maps


def kernel(**inputs):
    maps = make_in_maps(inputs)
    nc = build_program()
    res = run_bass_kernel_spmd(nc, maps, core_ids=list(range(8)))
    out = np.zeros((NBATCH, SEQ, D), np.float32)
    for core in range(8):
        b, p = core // 2, core % 2
        pos = storage_pos(p)
        out[b, pos[:NOWN * TS]] = res.results[core]["out"]
    return out
```

```python
import numpy as np
import ml_dtypes
import concourse.bass as bass
import concourse.mybir as mybir
from concourse.bass_utils import run_bass_kernel_spmd

F32 = mybir.dt.float32
BF16 = mybir.dt.bfloat16
AF = mybir.ActivationFunctionType
ALU = mybir.AluOpType
AX = mybir.AxisListType

D = 1024
SEQ = 4096
NBATCH = 4
TS = 512
NT = 8
NOWN = 4
HD = 64
INW = 6152
BIG = 30000.0
EPS = 1e-6
PI = [[0, 3, 4, 7, 1, 2, 5, 6], [1, 2, 5, 6, 0, 3, 4, 7]]
ROPE_DIM = 16
ROPE_THETA = 500000.0


def past_lists():
    out = []
    for j in range(NOWN):
        s = set()
        for p in range(2):
            for i in range(NT):
                if PI[p][i] < PI[p][j]:
                    s.add(i)
        out.append(sorted(s))
    return out


class Rec:
    ENGS = ("pe", "act", "dve", "pool", "sp")

    def __init__(self):
        self.ops = []
        self.lastw = {}
        self.readers = {}

    def add(self, eng, fn, reads=(), writes=(), dma=False):
        oid = len(self.ops)
        deps = set()
        for k in reads:
            if k in self.lastw:
                deps.add(self.lastw[k])
        for k in writes:
            if k in self.lastw:
                deps.add(self.lastw[k])
            for r in self.readers.get(k, {}).values():
                deps.update(r)
        for k in reads:
            d = self.readers.setdefault(k, {})
            if dma:
                d.setdefault("dma", []).append(oid)
            else:
                d[eng] = [oid]
        for k in writes:
            self.lastw[k] = oid
            self.readers[k] = {}
        self.ops.append(dict(eng=eng, fn=fn, deps=deps, dma=dma, inc=False))
        return oid

    def barrier(self):
        last = {}
        dmas = []
        for oid, op in enumerate(self.ops):
            if op.get("bar"):
                continue
            if op["dma"]:
                dmas.append(oid)
            else:
                last[op["eng"]] = oid
        deps = set(last.values()) | set(dmas)
        sp_id = len(self.ops)
        self.ops.append(dict(eng="sp", fn=(lambda e: e.sem_inc(self._sp_sem, 1)), deps=set(deps), dma=False, inc=True,
                             bar=True, selfinc=True))
        for e in self.ENGS:
            if e != "sp":
                self.ops.append(dict(eng=e, fn=None, deps={sp_id}, dma=False, inc=False, bar=True))

    def emit(self, nc, nsem_sp=24, nsem_pool=12):
        ops = self.ops
        for op in ops:
            for d in op["deps"]:
                if not ops[d]["dma"]:
                    ops[d]["inc"] = True
        cnt = {e: 0 for e in self.ENGS}
        dcount = {"sp": 0, "pool": 0, "act": 0}
        nsem = {"sp": nsem_sp, "pool": nsem_pool, "act": 4}
        for op in ops:
            e = op["eng"]
            if op["dma"]:
                k = dcount[e]
                dcount[e] += 1
                op["dsem"] = (e, k % nsem[e])
                op["dtarget"] = 16 * (k // nsem[e] + 1)
            elif op["inc"]:
                cnt[e] += 1
                op["val"] = cnt[e]
        import contextlib
        with contextlib.ExitStack() as es:
            sems = {e: es.enter_context(nc.semaphore("s_" + e)) for e in ("pe", "act", "dve", "pool", "sp")}
            self._sp_sem = sems["sp"]
            dsems = {}
            for q in ("sp", "pool"):
                if dcount[q]:
                    for i in range(min(nsem[q], dcount[q])):
                        dsems[(q, i)] = es.enter_context(nc.semaphore("d_%s%d" % (q, i)))
            block = es.enter_context(nc.Block())
            handles = {"pe": block.tensor, "act": block.scalar, "dve": block.vector,
                       "pool": block.gpsimd, "sp": block.sync}
            final_d = {}
            for op in ops:
                if op["dma"]:
                    final_d[op["dsem"]] = max(final_d.get(op["dsem"], 0), op["dtarget"])

            def run_engine(e):
                def body(eng):
                    seen = {}

                    def wait(sem_key, sem, val):
                        if seen.get(sem_key, 0) >= val:
                            return
                        seen[sem_key] = val
                        eng.wait_ge(sem, val)

                    for op in ops:
                        if op["eng"] != e:
                            continue
                        for d in sorted(op["deps"]):
                            dop = ops[d]
                            if dop["dma"]:
                                wait(dop["dsem"], dsems[dop["dsem"]], dop["dtarget"])
                            else:
                                if dop["eng"] == "pe" and e == "pe" and not op["dma"]:
                                    continue
                                wait(dop["eng"], sems[dop["eng"]], dop["val"])
                        if op["fn"] is None:
                            continue
                        if op["dma"]:
                            if op["dtarget"] > 16:
                                wait(op["dsem"], dsems[op["dsem"]], op["dtarget"] - 16)
                            inst = op["fn"](eng)
                            inst.then_inc(dsems[op["dsem"]], 16)
                        else:
                            inst = op["fn"](eng)
                            if op["inc"] and not op.get("selfinc"):
                                inst.then_inc(sems[e], 1)
                    if e == "sp":
                        for k, v in final_d.items():
                            wait(k, dsems[k], v)
                        for x in ("pe", "act", "dve", "pool"):
                            if cnt[x]:
                                wait(x, sems[x], cnt[x])
                return body

            for e in self.ENGS:
                handles[e](run_engine(e))


def storage_pos(p):
    return np.concatenate([np.arange(TS) + PI[p][i] * TS for i in range(NT)])


def const_tables(p):
    bf = ml_dtypes.bfloat16
    pos = storage_pos(p)
    t = {}
    tile_of = np.arange(SEQ) // TS
    kac = np.zeros((16, SEQ), np.float32)
    kac[3:6] = 1.0
    for j in range(NOWN):
        kac[6 + j] = np.where(np.array(PI[p])[tile_of] <= PI[p][j], 0.0, -BIG)
    qac = np.zeros((16, NOWN * TS), np.float32)
    qac[0:3] = 1.0
    for j in range(NOWN):
        qac[6 + j] = (tile_of[:NOWN * TS] == j).astype(np.float32)
    kam = np.zeros((16, SEQ), np.float32)
    blk = np.arange(SEQ) // 256
    for j in range(16):
        kam[j] = (blk == j).astype(np.float32)
    t["kac"] = kac.astype(bf)
    t["qac"] = qac.astype(bf)
    t["kam"] = kam.astype(bf)
    act_blk = np.array([PI[p][j // 2] * 2 + j % 2 for j in range(16)])
    mt = np.zeros((3, 16, 2, 16), np.float32)
    for st in range(16):
        own = st // 2
        valid = (act_blk < act_blk[own]).astype(np.float32)
        mt[0, st, :, :] = (valid - 1.0) * BIG
        mt[1, st, :, :] = valid
        mt[2, st, :, own] = 1.0
    t["mtab"] = np.ascontiguousarray(np.broadcast_to(mt.reshape(1, 3, 512), (128, 3, 512))).astype(bf)
    half = ROPE_DIM // 2
    inv_freq = (np.float32(ROPE_THETA) ** (-np.arange(0, half, dtype=np.float32) * np.float32(2.0) / np.float32(ROPE_DIM))).astype(np.float32)
    ang = (pos.astype(np.float32)[:, None] * inv_freq[None, :]).astype(np.float32)
    cos = np.ones((128, SEQ), np.float32)
    sin = np.zeros((128, SEQ), np.float32)
    for r in range(128):
        d = r % HD
        if d < ROPE_DIM:
            cos[r] = np.cos(ang[:, d % half].astype(np.float64)).astype(np.float32)
            sin[r] = np.sin(ang[:, d % half].astype(np.float64)).astype(np.float32)
    t["cos"] = cos
    t["sin"] = sin
    cm = np.zeros((128, 8, 128), np.float32)
    cm[:, 0, :] = np.eye(128)
    s_idx = np.arange(128)[:, None]
    t_idx = np.arange(128)[None, :]
    cm[:, 1, :] = np.where(s_idx > t_idx, -BIG, 0.0)
    cm[:, 2, :] = 1.0 / 1024.0
    bd = (s_idx // HD == t_idx // HD).astype(np.float32)
    cm[:, 3, :] = bd / 64.0
    rt = np.zeros((128, 128), np.float32)
    for m in range(128):
        d = m % HD
        if d < half:
            rt[m + half, m] = -1.0
        elif d < ROPE_DIM:
            rt[m - half, m] = 1.0
    cm[:, 4, :] = rt
    cm[:, 5, :] = bd
    cm[:, 6, :] = 1.0
    cm[64, 7, :] = 1.0
    t["cmat"] = cm.astype(bf)
    dm = np.zeros((128, 4, 512), np.float32)
    for s4 in range(4):
        dm[:, s4, :] = np.where((s4 * 128 + np.arange(128))[:, None] > np.arange(512)[None, :], -BIG, 0.0)
    t["dmask"] = dm.astype(bf)
    ao = np.zeros((8, 8, 8), np.float32)
    for i in range(8):
        for j in range(8):
            ao[:, i, j] = 1.0 if PI[p][j] < PI[p][i] else 0.0
    t["aoff"] = ao
    return t


def build_program(stop_after=None, dbg=()):
    nc = bass.Bass("TRN2", target_bir_lowering=False)
    R = Rec()

    def din(name, shape, dt=F32):
        return nc.dram_tensor(name, list(shape), dt, kind="ExternalInput").ap()

    xT = din("xT", [D, SEQ])
    xo = din("xo", [NOWN * TS, D])
    w_in = din("w_in", [D, INW])
    w_fox = din("w_fox", [512, D])
    w_moba = din("w_moba", [512, D])
    w_out = din("w_out", [D, D])
    gng_d = din("gng", [128, 8])
    gains_d = din("gains", [128, 4])
    bgate_d = din("bgate", [128, 16])
    bf_d = din("bf", [8, 1])
    kac_d = din("kac", [16, SEQ], BF16)
    qac_d = din("qac", [16, NOWN * TS], BF16)
    kam_d = din("kam", [16, SEQ], BF16)
    mtab_d = din("mtab", [128, 3, 512], BF16)
    cos_d = din("cos", [128, SEQ])
    sin_d = din("sin", [128, SEQ])
    cmat_d = din("cmat", [128, 8, 128], BF16)
    aoff_d = din("aoff", [8, 8, 8])
    dmask_d = din("dmask", [128, 4, 512], BF16)
    out_d = nc.dram_tensor("out", [NOWN * TS, D], F32, kind="ExternalOutput").ap()
    cscr = nc.dram_tensor("cscr", [8, 6, SEQ], BF16).ap()
    dbg_out = {}
    for name, shape, dt in dbg:
        dbg_out[name] = nc.dram_tensor(name, list(shape), dt, kind="ExternalOutput").ap()

    cur = [16640]
    OFFS = {}

    def alloc(name, shape, dt):
        sz = int(np.prod(shape[1:])) * (2 if dt == BF16 else 4)
        off = (cur[0] + 31) // 32 * 32
        t = nc.alloc_sbuf_tensor_at(name, list(shape), dt, offset=off)
        cur[0] = off + sz
        OFFS[name] = (off, off + sz)
        return t

    HT = alloc("HT", [128, 8, SEQ], BF16)
    YT = alloc("YT", [128, 8, NOWN * TS], BF16)
    CM = alloc("CM", [128, 8, 128], BF16)
    GNG = alloc("GNG", [128, 8], F32)
    GAINS = alloc("GAINS", [128, 4], F32)
    BGATE = alloc("BGATE", [128, 16], F32)
    BFT = alloc("BFT", [128, 2], F32)
    EPSC = alloc("EPSC", [128, 2], F32)
    phase_base = cur[0]
    IDENT = CM[:, 0, :]
    TRI = CM[:, 1, :]
    ONESM = CM[:, 2, :]
    BD = CM[:, 3, :]
    RT = CM[:, 4, :]
    BDQ = CM[:, 5, :]
    ONES1 = CM[:, 6, :]
    ONESEL = CM[:, 7, :]

    PS = [nc.alloc_psum_tensor("ps%d" % i, [128, 1024], F32) for i in range(4)]

    def bank(b):
        return PS[b // 2][:, (b % 2) * 512:(b % 2 + 1) * 512]

    def dma(q, out, in_, reads=(), writes=()):
        return R.add(q, lambda e: e.dma_start(out=out, in_=in_), reads=reads, writes=writes, dma=True)

    dma("sp", CM[:], cmat_d, writes=["CM"])
    dma("sp", GNG[:], gng_d, writes=["GNG"])
    dma("sp", GAINS[:], gains_d, writes=["GAINS"])
    dma("sp", BGATE[:], bgate_d, writes=["BGATE"])
    dma("sp", BFT[0:8, 0:1], bf_d, writes=["BFT"])
    R.add("dve", lambda e: e.memset(EPSC[:, 0:1], EPS), writes=["EPSC"])
    R.add("dve", lambda e: e.memset(EPSC[:, 1:2], EPS * 64.0), writes=["EPSC"])

    if stop_after == "c0":
        R.add("dve", lambda e: e.memset(HT[:, 0, 0:512], 1.0), writes=[("HT", 0)])
        R.barrier()
        dma("sp", dbg_out["HT"][:, 0, 0:512], HT[:, 0, 0:512], reads=[("HT", 0)])
        R.emit(nc)
        return nc
    cur[0] = phase_base
    KT = [alloc("KA", [128, SEQ], BF16), alloc("KB", [128, SEQ], BF16)]
    QT = [alloc("QA", [128, NOWN * TS], BF16), alloc("QB", [128, NOWN * TS], BF16)]
    VT = [alloc("VA", [128, 32, 66], BF16), alloc("VB", [128, 32, 128], BF16)]
    ZS = alloc("ZS", [128, NOWN * TS], BF16)
    WQ = alloc("WQ", [128, 8, 128], BF16)
    WK = alloc("WK", [128, 8, 128], BF16)
    WZ = alloc("WZ", [128, 8, 128], BF16)
    WV = alloc("WV", [128, 8, 128], BF16)
    PT = [alloc("PT%d" % i, [128, 1024], BF16) for i in range(3)]
    SQ1 = [alloc("SQ1_%d" % i, [128, TS], BF16) for i in range(2)]
    RS = [alloc("RS%d" % i, [128, TS], F32) for i in range(2)]
    T1 = [alloc("T1_%d" % i, [128, TS], F32) for i in range(2)]
    _rec = alloc("REC", [128, TS], F32)
    REC = [_rec, _rec]
    RH = [alloc("RHA", [128, TS], BF16), alloc("RHB", [128, TS], BF16)]
    RL = [alloc("RLA", [128, TS], BF16), alloc("RLB", [128, TS], BF16)]
    _ytmp = alloc("YTMP", [128, TS], F32)
    YTMP = [_ytmp, _ytmp]
    DMASK = alloc("DMASK", [128, 4, 512], BF16)
    moba_base = cur[0]
    ABT = [[alloc("ABT%d_%d" % (i, k), [128, TS], BF16) for k in range(2)] for i in range(2)]
    RCT = [[alloc("RCT%d_%d" % (i, k), [128, TS], F32) for k in range(2)] for i in range(2)]
    COST = [alloc("COST%d" % i, [128, TS], F32) for i in range(2)]
    SINT = [alloc("SINT%d" % i, [128, TS], F32) for i in range(2)]
    MTAB = alloc("MTAB", [128, 3, 512], BF16)
    GM = alloc("GM", [128, 512], F32)
    T8 = alloc("T8", [128, 32, 8], F32)
    SEL = alloc("SEL", [128, 512], F32)
    PEN = [alloc("PENA", [128, 16, 80], BF16), alloc("PENB", [128, 16, 16], BF16)]
    KM = alloc("KM", [128, 16], F32)
    KMR = alloc("KMR", [128, 16], F32)
    KMH = alloc("KMH", [128, 16], BF16)
    KML = alloc("KML", [128, 16], BF16)
    KMH2 = alloc("KMH2", [128, 2, 16], BF16)
    KML2 = alloc("KML2", [128, 2, 16], BF16)
    SB_LIMIT = 16512 + 212863
    print("phase1 sbuf end", cur[0], "moba_base", moba_base, "limit", SB_LIMIT)
    assert cur[0] <= SB_LIMIT, cur[0]
    cur[0] = phase_base
    WO = alloc("WO", [128, 8, D], BF16)
    SA = [alloc("SA%d" % i, [128, TS], F32) for i in range(2)]
    SB = [alloc("SB%d" % i, [128, TS], F32) for i in range(2)]
    TT = [alloc("TT%d" % i, [128, TS], F32) for i in range(2)]
    MTT = alloc("MTT", [128, 8, TS], BF16)
    assert cur[0] <= OFFS["WQ"][0], (cur[0], OFFS["WQ"])
    cur[0] = OFFS["WQ"][0]
    WF = alloc("WF", [128, 4, D], BF16)
    assert cur[0] <= OFFS["WV"][1]
    cur[0] = OFFS["WV"][1]
    WM = alloc("WM", [128, 4, D], BF16)
    XO = [alloc("XO%d" % i, [128, D], F32) for i in range(2)]
    OT = [alloc("OT%d" % i, [128, D], F32) for i in range(2)]
    assert cur[0] <= moba_base, (cur[0], moba_base)
    cur[0] = moba_base
    WG = alloc("WG", [128, 8, 2048], BF16)
    assert cur[0] <= SB_LIMIT, cur[0]
    MOBA_KEYS = ([("abt", a_, k_) for a_ in range(2) for k_ in range(2)] + [("rct", a_, k_) for a_ in range(2) for k_ in range(2)]
                 + [("cost", a_) for a_ in range(2)] + [("sint", a_) for a_ in range(2)] + ["MTAB", "GM", "SEL", "PENc", "KMH", "KMR", "KML", "KMH2c"]
                 + [("T8", g_) for g_ in range(32)] + [("PEN", h_) for h_ in range(2)] + [("KM", i_) for i_ in range(NT)]
                 + [("KMH2", h_) for h_ in range(2)] + [("KML2", h_) for h_ in range(2)])

    w3 = w_in.rearrange("(c p) n -> p c n", p=128)

    def A(eng, fn, reads=(), writes=()):
        return R.add(eng, fn, reads=reads, writes=writes)

    cur[0] = moba_base
    WFL = alloc("WFL", [128, 8, 8], BF16)
    FLE = alloc("FLE", [8, TS], F32)
    CL = alloc("CL", [8, SEQ], F32)
    ONES8 = alloc("ONES8", [8, TS], F32)
    TOT = alloc("TOT", [8, 8], F32)
    OFF = alloc("OFF", [8, 8], F32)
    TMP8 = alloc("TMP8", [8, 8], F32)
    AOFF = alloc("AOFF", [8, 8, 8], F32)
    CC = alloc("CC", [8, TS], F32)
    R1 = alloc("R1", [8, TS], F32)
    _pie = alloc("PIE", [8, 6, TS], BF16)
    PIE = [_pie, _pie]
    assert cur[0] <= SB_LIMIT, cur[0]
    dma("pool", WFL[:], w3[:, :, 6144:6152], writes=["WFL"])
    dma("sp", AOFF[:], aoff_d, writes=["AOFF"])
    A("dve", lambda e: e.memset(ONES8[:], 1.0), writes=["ONES8"])
    A("dve", lambda e: e.tensor_scalar(out=BFT[0:8, 1:2], in0=BFT[0:8, 0:1], scalar1=-1.0, scalar2=None, op0=ALU.mult),
      reads=["BFT"], writes=["NBF"])

    import os as _os
    pairs = [(0, hp) for hp in range(4)] + [(1, hp) for hp in range(4)]
    if _os.environ.get("K_PAIRS"):
        pairs = [tuple(int(v) for v in t.split(":")) for t in _os.environ["K_PAIRS"].split(",")]

    def load_pair_weights(br, hp):
        base = br * 2048
        dma("pool", WK[:], w3[:, :, base + 512 + hp * 128: base + 512 + (hp + 1) * 128], writes=["WK"])
        dma("pool", WV[:], w3[:, :, base + 1024 + hp * 128: base + 1024 + (hp + 1) * 128], writes=["WV"])
        dma("pool", WQ[:], w3[:, :, base + hp * 128: base + (hp + 1) * 128], writes=["WQ"])
        dma("pool", WZ[:], w3[:, :, base + 1536 + hp * 128: base + 1536 + (hp + 1) * 128], writes=["WZ"])
    load_pair_weights(*pairs[0])

    def phase_c_front_pe(i):
        pb = 2 + i % 2
        for c in range(8):
            A("pe", lambda e, c=c: e.matmul(bank(pb)[0:8, :], lhsT=WFL[:, c, :], rhs=HT[:, c, i * TS:(i + 1) * TS],
                                            start=(c == 0), stop=(c == 7)),
              reads=["WFL", ("HT", i)], writes=[("bank", pb)])

    def phase_c_front_rest(i):
        pb = 2 + i % 2
        A("act", lambda e: e.activation(out=FLE[:], in_=bank(pb)[0:8, :], func=AF.Exp, bias=BFT[0:8, 1:2], scale=-1.0),
          reads=[("bank", pb), "NBF"], writes=["fle"])
        A("act", lambda e: e.activation(out=FLE[:], in_=FLE[:], func=AF.Ln, bias=1.0, scale=1.0),
          reads=["fle"], writes=["fle"])
        A("dve", lambda e: e.tensor_tensor_scan(out=CL[:, i * TS:(i + 1) * TS], data0=ONES8[:], data1=FLE[:],
                                                initial=0.0, op0=ALU.mult, op1=ALU.add),
          reads=["fle", "ONES8"], writes=[("cl", i)])
        A("dve", lambda e: e.tensor_copy(out=TOT[:, i:i + 1], in_=CL[:, i * TS + TS - 1:i * TS + TS]),
          reads=[("cl", i)], writes=["TOT"])

    def phase_c_tail_job():
        for i in range(NT):
            A("dve", lambda e, i=i: e.tensor_tensor(out=TMP8[:], in0=AOFF[:, i, :], in1=TOT[:], op=ALU.mult),
              reads=["AOFF", "TOT"], writes=["TMP8"])
            A("dve", lambda e, i=i: e.reduce_sum(out=OFF[:, i:i + 1], in_=TMP8[:], axis=AX.X),
              reads=["TMP8"], writes=["OFF"])
        yield
        for i in range(NT):
            r = 0
            A("dve", lambda e, r=r, i=i: e.tensor_scalar(out=CC[:], in0=CL[:, i * TS:(i + 1) * TS], scalar1=OFF[:, i:i + 1],
                                                         scalar2=-1.0, op0=ALU.add, op1=ALU.mult),
              reads=[("cl", i), "OFF"], writes=["cc"])
            A("dve", lambda e, r=r: e.tensor_copy(out=PIE[r][:, 3, :], in_=CC[:]), reads=["cc"], writes=[("pie", r)])
            A("dve", lambda e, r=r: e.tensor_tensor(out=R1[:], in0=CC[:], in1=PIE[r][:, 3, :], op=ALU.subtract),
              reads=["cc", ("pie", r)], writes=["r1"])
            A("dve", lambda e, r=r: e.tensor_copy(out=PIE[r][:, 4, :], in_=R1[:]), reads=["r1"], writes=[("pie", r)])
            yield
            A("dve", lambda e, r=r: e.tensor_tensor(out=CC[:], in0=R1[:], in1=PIE[r][:, 4, :], op=ALU.subtract),
              reads=["r1", ("pie", r)], writes=["cc"])
            A("dve", lambda e, r=r: e.tensor_copy(out=PIE[r][:, 5, :], in_=CC[:]), reads=["cc"], writes=[("pie", r)])
            A("dve", lambda e, r=r: e.tensor_scalar(out=PIE[r][:, 0:3, :], in0=PIE[r][:, 3:6, :], scalar1=-1.0, scalar2=None, op0=ALU.mult),
              reads=[("pie", r)], writes=[("pie", r)])
            dma("sp", cscr[:, :, i * TS:(i + 1) * TS], PIE[r][:], reads=[("pie", r)], writes=["cscr"])
            yield

    cur[0] = phase_base
    XT8 = [alloc("XT8_%d" % i, [128, 8, TS], F32) for i in range(2)]
    SQ0 = [alloc("SQ0_%d" % i, [128, TS], BF16) for i in range(2)]
    RSTD = [alloc("RSTD_%d" % i, [128, TS], F32) for i in range(2)]
    cur[0] = OFFS["PT0"][0]
    XT8.append(alloc("XT8_2", [128, 8, TS], F32))
    assert cur[0] <= OFFS["REC"][0], (cur[0], OFFS["REC"])
    import os
    P0T = int(os.environ.get("P0_TILES", NT))
    SKIP = set(os.environ.get("P0_SKIP", "").split(","))
    for i in range(P0T):
        b = i % 3
        rb = i % 2
        for c in range(8):
            dma("sp", XT8[b][:, c, :], xT[c * 128:(c + 1) * 128, i * TS:(i + 1) * TS],
                writes=[("xt8", b, c)])
        pb = i % 2
        if i >= 2:
            phase_c_front_pe(i - 2)
        for c in range(8):
            R.add("act", lambda e, b=b, c=c: e.activation(out=SQ0[c % 2][:], in_=XT8[b][:, c, :], func=AF.Square),
                  reads=[("xt8", b, c)], writes=[("sq0", c % 2)])
            R.add("pe", lambda e, c=c, pb=pb: e.matmul(bank(pb), lhsT=ONESM, rhs=SQ0[c % 2][:], start=(c == 0), stop=(c == 7)),
                  reads=[("sq0", c % 2), "CM"], writes=[("bank", pb)])
        R.add("act", lambda e, rb=rb, pb=pb: e.activation(out=RSTD[rb][:], in_=bank(pb), func=AF.Ln, bias=EPSC[:, 0:1], scale=1.0),
              reads=[("bank", pb), "EPSC"], writes=[("rstd", rb)])
        R.add("act", lambda e, rb=rb: e.activation(out=RSTD[rb][:], in_=RSTD[rb][:], func=AF.Exp, scale=-0.5),
              reads=[("rstd", rb)], writes=[("rstd", rb)])
        for c in range(8):
            R.add("dve", lambda e, b=b, rb=rb, c=c, i=i: e.scalar_tensor_tensor(
                out=HT[:, c, i * TS:(i + 1) * TS], in0=XT8[b][:, c, :], scalar=GNG[:, c:c + 1], in1=RSTD[rb][:],
                op0=ALU.mult, op1=ALU.mult),
                reads=[("xt8", b, c), ("rstd", rb), "GNG"], writes=[("HT", i)])
        if i >= 2:
            phase_c_front_rest(i - 2)
    for i in range(max(P0T - 2, 0), P0T):
        phase_c_front_pe(i)
        phase_c_front_rest(i)
    R.barrier()

    if "HT" in dbg_out:
        for c in range(8):
            dma("sp", dbg_out["HT"][:, c, :], HT[:, c, :], reads=[("HT", i) for i in range(NT)])

    if stop_after == "p0":
        R.emit(nc)
        return nc

    HROWS = [slice(0, 80), slice(0, 128)]
    MROWS = [slice(0, 64), slice(64, 128)]
    AUG0 = [64, 0]
    VCOLS = [66, 128]
    DENROW = [slice(64, 65), slice(0, 1)]
    plists = past_lists()

    A("pool", lambda e: e.memset(KT[1][0:64, :], 0.0), writes=[("Kaug", 1)])
    A("pool", lambda e: e.memset(QT[1][0:64, :], 0.0), writes=[("Qaug", 1)])
    A("pool", lambda e: e.memset(QT[0][64:80, :], 0.0), writes=[("Qaug", 0)])
    A("pool", lambda e: e.memset(RH[0][0:64, :], 0.0), writes=["RHc"])
    A("pool", lambda e: e.memset(RL[0][0:64, :], 0.0), writes=["RHc"])
    A("pool", lambda e: e.memset(VT[0][:, :, 64:66], 1.0), writes=[("Vc", 0)])
    A("pool", lambda e: e.memset(VT[1][:, :, 0:2], 1.0), writes=[("Vc", 1)])
    A("pool", lambda e: e.memset(VT[1][:, :, 2:64], 0.0), writes=[("Vc", 1)])
    dma("sp", DMASK[:], dmask_d, writes=["DMASK"])

    bank_rot = [0]

    NROT = [8]

    def next_bank():
        b = bank_rot[0] % NROT[0]
        bank_rot[0] = (b + 1) % NROT[0]
        return b

    free_banks = list(range(8))

    def acquire():
        assert free_banks, "out of PSUM banks"
        return free_banks.pop(0)

    def release(b):
        assert b not in free_banks
        free_banks.append(b)

    rotn = {"sq": (0, 2), "rs": (0, 2), "t": (0, 2), "cs": (0, 2), "rc": (0, 2), "ab": (0, 2)}

    def nxt(k):
        v, n = rotn[k]
        rotn[k] = ((v + 1) % n, n)
        return v

    def proj_feat(W, wkey, i, bk):
        for c in range(8):
            A("pe", lambda e, c=c: e.matmul(bank(bk), lhsT=W[:, c, :], rhs=HT[:, c, i * TS:(i + 1) * TS],
                                            start=(c == 0), stop=(c == 7)),
              reads=[wkey, ("HT", i)], writes=[("bank", bk)])

    def qk_job(br, isq, W, wkey, i):
        gcol = br * 2 + (0 if isq else 1)
        dst = QT if isq else KT
        dkey = "Q" if isq else "K"
        cols = slice(i * TS, (i + 1) * TS)
        bk = acquire()
        proj_feat(W, wkey, i, bk)
        yield
        sq = nxt("sq")
        A("act", lambda e: e.activation(out=SQ1[sq][:], in_=bank(bk), func=AF.Square),
          reads=[("bank", bk)], writes=[("sq1", sq)])
        by = acquire()
        A("pe", lambda e: e.matmul(bank(by), lhsT=(BDQ if isq else BD), rhs=SQ1[sq][:], start=True, stop=True),
          reads=[("sq1", sq), "CM"], writes=[("bank", by)])
        if br == 1:
            cs = nxt("cs")
            dma("sp", COST[cs][:], cos_d[:, cols], writes=[("cost", cs)])
            dma("sp", SINT[cs][:], sin_d[:, cols], writes=[("sint", cs)])
        yield
        r = nxt("rs")
        ec = 1 if isq else 0
        A("act", lambda e: e.activation(out=RS[r][:], in_=bank(by), func=AF.Ln, bias=EPSC[:, ec:ec + 1], scale=1.0),
          reads=[("bank", by), "EPSC"], writes=[("rs", r)])
        A("act", lambda e: e.activation(out=RS[r][:], in_=RS[r][:], func=AF.Exp, scale=-0.5),
          reads=[("rs", r)], writes=[("rs", r)])
        release(by)
        if br == 0:
            for hd in range(2):
                mr = MROWS[hd]
                A("dve", lambda e, hd=hd, mr=mr: e.scalar_tensor_tensor(
                    out=dst[hd][mr, cols], in0=bank(bk)[mr, :], scalar=GAINS[mr, gcol:gcol + 1], in1=RS[r][mr, :],
                    op0=ALU.mult, op1=ALU.mult),
                  reads=[("bank", bk), ("rs", r), "GAINS"], writes=[(dkey, hd, i)])
            release(bk)
            return
        rc = nxt("rc")
        ab = nxt("ab")
        A("pool", lambda e: e.tensor_tensor(out=RCT[rc][0][:], in0=RS[r][:], in1=COST[cs][:], op=ALU.mult),
          reads=[("rs", r), ("cost", cs)], writes=[("rct", rc, 0)])
        A("pool", lambda e: e.tensor_tensor(out=RCT[rc][1][:], in0=RS[r][:], in1=SINT[cs][:], op=ALU.mult),
          reads=[("rs", r), ("sint", cs)], writes=[("rct", rc, 1)])
        for k2 in range(2):
            A("dve", lambda e, k2=k2: e.scalar_tensor_tensor(
                out=ABT[ab][k2][:], in0=bank(bk), scalar=GAINS[:, gcol:gcol + 1], in1=RCT[rc][k2][:], op0=ALU.mult, op1=ALU.mult),
              reads=[("bank", bk), ("rct", rc, k2), "GAINS"], writes=[("abt", ab, k2)])
        release(bk)
        yield
        bz = acquire()
        A("pe", lambda e: e.matmul(bank(bz), lhsT=IDENT, rhs=ABT[ab][0][:], start=True, stop=False),
          reads=[("abt", ab, 0), "CM"], writes=[("bank", bz)])
        A("pe", lambda e: e.matmul(bank(bz), lhsT=RT, rhs=ABT[ab][1][:], start=False, stop=True),
          reads=[("abt", ab, 1), "CM"], writes=[("bank", bz)])
        yield
        for hd in range(2):
            mr = MROWS[hd]
            A("act", lambda e, hd=hd, mr=mr: e.activation(out=dst[hd][mr, cols], in_=bank(bz)[mr, :], func=AF.Copy),
              reads=[("bank", bz)], writes=[(dkey, hd, i)])
        if not isq:
            for hd in range(2):
                mr = MROWS[hd]
                A("dve", lambda e, hd=hd, mr=mr: e.reduce_sum(out=KM[mr, 2 * i:2 * i + 2],
                                                             in_=bank(bz)[mr, :].rearrange("p (b l) -> p b l", l=256), axis=AX.X),
                  reads=[("bank", bz)], writes=[("KM", i)])
        release(bz)

    def z_job(i):
        bk = acquire()
        proj_feat(WZ, "WZ", i, bk)
        yield
        t = nxt("t")
        A("act", lambda e: e.activation(out=T1[t][:], in_=bank(bk), func=AF.Exp, scale=-1.0),
          reads=[("bank", bk)], writes=[("t1", t)])
        A("act", lambda e: e.activation(out=T1[t][:], in_=T1[t][:], func=AF.Ln, bias=1.0, scale=1.0),
          reads=[("t1", t)], writes=[("t1", t)])
        A("act", lambda e: e.activation(out=T1[t][:], in_=T1[t][:], func=AF.Exp, scale=-1.0),
          reads=[("t1", t)], writes=[("t1", t)])
        A("dve", lambda e: e.tensor_tensor(out=ZS[:, i * TS:(i + 1) * TS], in0=T1[t][:], in1=bank(bk), op=ALU.mult),
          reads=[("t1", t), ("bank", bk)], writes=[("ZS", i)])
        release(bk)

    def v_job(g):
        bk = acquire()
        for s4 in range(4):
            st = 4 * g + s4
            for c in range(8):
                A("pe", lambda e, c=c, st=st, s4=s4: e.matmul(bank(bk)[:, s4 * 128:(s4 + 1) * 128],
                                                            lhsT=HT[:, c, st * 128:(st + 1) * 128], rhs=WV[:, c, :],
                                                            start=(c == 0), stop=(c == 7)),
                  reads=["WV", ("HT", st // 4)], writes=[("bank", bk)])
        yield
        src = bank(bk).rearrange("p (s n) -> p s n", n=128)
        A("act", lambda e: e.activation(out=VT[0][:, 4 * g:4 * g + 4, 0:64], in_=src[:, :, 0:64], func=AF.Copy),
          reads=[("bank", bk)], writes=[("V", 0, g)])
        A("act", lambda e: e.activation(out=VT[1][:, 4 * g:4 * g + 4, 64:128], in_=src[:, :, 64:128], func=AF.Copy),
          reads=[("bank", bk)], writes=[("V", 1, g)])
        release(bk)

    def run_pipeline(jobs):
        pend = list(jobs)
        active = []
        done = set()
        nstep = [0]
        held = []
        if pending:
            for b_ in (6, 7):
                free_banks.remove(b_)
                held.append(b_)
        while pend or active:
            for k, (gnr, after) in enumerate(pend):
                if all(id(a_) in done for a_ in after):
                    active.append(gnr)
                    pend.pop(k)
                    break
            for gnr in reversed(list(active)):
                try:
                    next(gnr)
                except StopIteration:
                    active.remove(gnr)
                    done.add(id(gnr))
            nstep[0] += 1
            if nstep[0] == 3:
                flush_pending()
                while held:
                    release(held.pop())
        assert not held

    def attention_pair(br, hp):
        chunk = br * 4 + hp
        flat = []
        for (j, hd) in [(0, 0), (3, 1), (1, 0), (2, 1), (2, 0), (1, 1), (3, 0), (0, 1)]:
            past = []
            for i in plists[j]:
                past += [dict(ks=i * 4 + s4, mk=None, q0=0) for s4 in range(4)]
            dg = [dict(ks=j * 4 + s4, mk=s4, q0=s4 * 128) for s4 in range(4)]
            g1 = [dict(dg[0], b=0, c=0), dict(dg[1], b=1, c=0), dict(dg[3], b=1, c=384)]
            g2 = [dict(dg[2], b=0, c=256), dict(past[0], b=1, c=0)]
            grs = [g1, g2]
            rest = past[1:]
            for k in range(0, len(rest), 2):
                grs.append([dict(e_, b=bi, c=0) for bi, e_ in enumerate(rest[k:k + 2])])
            for gi, g in enumerate(grs):
                flat.append(dict(hd=hd, j=j, ents=g, first=(gi == 0), last=(gi == len(grs) - 1)))
        ngr = len(flat)

        def qk(n):
            G = flat[n]
            hd, j = G["hd"], G["j"]
            Kt, Qt, rows = KT[hd], QT[hd], HROWS[hd]
            sb = n % 3
            for E in G["ents"]:
                ks, mk, q0 = E["ks"], E["mk"], E["q0"]
                c0 = E["b"] * 512 + E["c"]
                c1 = c0 + (512 - q0)
                kreads = [("K", hd, ks // 4), ("Kaug", hd), ("Q", hd, j), ("Qaug", hd)]
                A("pe", lambda e, ks=ks, c0=c0, c1=c1, mk=mk, q0=q0: e.matmul(
                    PS[sb][:, c0:c1], lhsT=Kt[rows, ks * 128:(ks + 1) * 128], rhs=Qt[rows, j * TS + q0:(j + 1) * TS],
                    start=True, stop=(mk is None)),
                  reads=kreads, writes=[("bank", 2 * sb + E["b"])])
                if mk is not None:
                    A("pe", lambda e, c0=c0, c1=c1, mk=mk, q0=q0: e.matmul(PS[sb][:, c0:c1], lhsT=IDENT, rhs=DMASK[:, mk, q0:512],
                                                                         start=False, stop=True),
                      reads=["CM", "DMASK"], writes=[("bank", 2 * sb + E["b"])])

        def ex(n):
            sb = n % 3
            pb = n % 3
            banks = sorted(set(E["b"] for E in flat[n]["ents"]))
            rngs = sorted((E["b"] * 512 + E["c"], E["b"] * 512 + E["c"] + 512 - E["q0"]) for E in flat[n]["ents"])
            merged = []
            for (r0, r1) in rngs:
                if merged and merged[-1][1] == r0:
                    merged[-1][1] = r1
                else:
                    merged.append([r0, r1])
            for (r0, r1) in merged:
                A("act", lambda e, r0=r0, r1=r1: e.activation(out=PT[pb][:, r0:r1], in_=PS[sb][:, r0:r1], func=AF.Exp),
                  reads=[("bank", 2 * sb + b_) for b_ in banks], writes=[("pt", pb)])

        def pv(n):
            G = flat[n]
            hd, j = G["hd"], G["j"]
            Vt, vc, ob = VT[hd], VCOLS[hd], 6 + hd
            pb = n % 3
            ne = len(G["ents"])
            for e_i, E in enumerate(G["ents"]):
                ks, q0 = E["ks"], E["q0"]
                c0 = E["b"] * 512 + E["c"]
                c1 = c0 + (512 - q0)
                first = G["first"] and e_i == 0
                last = G["last"] and e_i == ne - 1
                A("pe", lambda e, ks=ks, c0=c0, c1=c1, q0=q0, first=first, last=last: e.matmul(
                    bank(ob)[0:vc, q0:512], lhsT=Vt[:, ks, 0:vc], rhs=PT[pb][:, c0:c1], start=first, stop=last,
                    skip_group_check=True),
                  reads=[("pt", pb), ("V", hd, ks // 4), ("Vc", hd)], writes=[("bank", ob)])
            if G["last"]:
                finalize1(hd, j, ob)

        def finalize1(hd, j, ob):
            dr = DENROW[hd]
            mr = MROWS[hd]
            A("dve", lambda e: e.reciprocal(out=REC[hd][dr, :], in_=bank(ob)[dr, :]), reads=[("bank", ob)], writes=[("REC", hd)])
            A("dve", lambda e: e.tensor_copy(out=RH[hd][dr, :], in_=REC[hd][dr, :]), reads=[("REC", hd)], writes=[("RH", hd)])
            A("dve", lambda e: e.tensor_tensor(out=RL[hd][dr, :], in0=REC[hd][dr, :], in1=RH[hd][dr, :], op=ALU.subtract),
              reads=[("REC", hd), ("RH", hd)], writes=[("RL", hd)])
            A("dve", lambda e: e.tensor_tensor(out=YTMP[hd][mr, :], in0=bank(ob)[mr, :], in1=ZS[mr, j * TS:(j + 1) * TS], op=ALU.mult),
              reads=[("bank", ob), ("ZS", j)], writes=[("YTMP", hd)])

            def stage2():
                if hd == 0:
                    bl, br_ = ONESEL[0:65, :], slice(0, 65)
                else:
                    bl, br_ = ONES1[0:1, :], slice(0, 1)
                A("pe", lambda e: e.matmul(bank(ob), lhsT=bl, rhs=RH[hd][br_, :], start=True, stop=False),
                  reads=[("RH", hd), "RHc", "CM"], writes=[("bank", ob)])
                A("pe", lambda e: e.matmul(bank(ob), lhsT=bl, rhs=RL[hd][br_, :], start=False, stop=True),
                  reads=[("RL", hd), "RHc", "CM"], writes=[("bank", ob)])
                A("dve", lambda e: e.tensor_tensor(out=YT[mr, chunk, j * TS:(j + 1) * TS], in0=YTMP[hd][mr, :], in1=bank(ob)[mr, :], op=ALU.mult),
                  reads=[("YTMP", hd), ("bank", ob)], writes=[("YT", chunk, j)])
            pending.append([8, stage2])

        LOOK = 2
        for n in range(min(LOOK, ngr)):
            qk(n)
            ex(n)
        for n in range(ngr):
            if n + LOOK < ngr:
                qk(n + LOOK)
                ex(n + LOOK)
            for it in list(pending):
                it[0] -= 1
                if it[0] <= 0:
                    pending.remove(it)
                    it[1]()
            pv(n)

    pending = []

    def flush_pending():
        while pending:
            pending.pop(0)[1]()

    def gating_job():
        A("dve", lambda e: e.tensor_copy(out=KMH[:], in_=KM[:]), reads=[("KM", i) for i in range(NT)], writes=["KMH"])
        A("dve", lambda e: e.tensor_tensor(out=KMR[:], in0=KM[:], in1=KMH[:], op=ALU.subtract),
          reads=[("KM", i) for i in range(NT)] + ["KMH"], writes=["KMR"])
        A("dve", lambda e: e.tensor_copy(out=KML[:], in_=KMR[:]), reads=["KMR"], writes=["KML"])
        for hd in range(2):
            mr = MROWS[hd]
            A("dve", lambda e, hd=hd, mr=mr: e.tensor_copy(out=KMH2[mr, hd, :], in_=KMH[mr, :]), reads=["KMH", "KMH2c"], writes=[("KMH2", hd)])
            A("dve", lambda e, hd=hd, mr=mr: e.tensor_copy(out=KML2[mr, hd, :], in_=KML[mr, :]), reads=["KML", "KMH2c"], writes=[("KML2", hd)])
        yield
        gb_ = acquire()
        g4 = bank(gb_).rearrange("p (s h j) -> p s h j", h=2, j=16)
        for st in range(16):
            for hd in range(2):
                hr = HROWS[hd]
                A("pe", lambda e, st=st, hd=hd, hr=hr: e.matmul(g4[:, st, hd, :], lhsT=QT[hd][hr, st * 128:(st + 1) * 128], rhs=KMH2[hr, hd, :],
                                                                start=True, stop=False),
                  reads=[("Q", hd, st // 4), ("Qaug", hd), ("KMH2", hd), "KMH2c"], writes=[("bank", gb_)])
                A("pe", lambda e, st=st, hd=hd, hr=hr: e.matmul(g4[:, st, hd, :], lhsT=QT[hd][hr, st * 128:(st + 1) * 128], rhs=KML2[hr, hd, :],
                                                                start=False, stop=True),
                  reads=[("Q", hd, st // 4), ("Qaug", hd), ("KML2", hd), "KMH2c"], writes=[("bank", gb_)])
        yield
        A("dve", lambda e: e.tensor_tensor(out=GM[:], in0=bank(gb_), in1=MTAB[:, 0, :], op=ALU.add),
          reads=[("bank", gb_), "MTAB"], writes=["GM"])
        release(gb_)
        for grp in range(32):
            A("dve", lambda e, grp=grp: e.max(out=T8[:, grp, :], in_=GM[:, grp * 16:(grp + 1) * 16]),
              reads=["GM"], writes=[("T8", grp)])
        yield
        gm3 = GM[:].rearrange("p (g j) -> p g j", j=16)
        sel3 = SEL[:].rearrange("p (g j) -> p g j", j=16)
        A("dve", lambda e: e.tensor_tensor(out=sel3, in0=gm3, in1=T8[:, :, 2:3].to_broadcast([128, 32, 16]), op=ALU.is_ge),
          reads=["GM"] + [("T8", grp) for grp in range(32)], writes=["SEL"])
        A("dve", lambda e: e.tensor_tensor(out=SEL[:], in0=SEL[:], in1=MTAB[:, 1, :], op=ALU.mult), reads=["SEL", "MTAB"], writes=["SEL"])
        A("dve", lambda e: e.tensor_tensor(out=SEL[:], in0=SEL[:], in1=MTAB[:, 2, :], op=ALU.add), reads=["SEL", "MTAB"], writes=["SEL"])
        sel4 = SEL[:].rearrange("p (s h j) -> p s h j", h=2, j=16)
        for hd in range(2):
            a0 = AUG0[hd]
            A("dve", lambda e, hd=hd, a0=a0: e.tensor_scalar(out=PEN[hd][:, :, a0:a0 + 16], in0=sel4[:, :, hd, :], scalar1=-1.0, scalar2=BIG,
                                                             op0=ALU.add, op1=ALU.mult),
              reads=["SEL", "PENc"], writes=[("PEN", hd)])
        yield
        for g in range(4):
            for hd in range(2):
                M = 80 if hd == 0 else 16
                cr = slice(64, 80) if hd == 0 else slice(0, 16)
                tb = acquire()
                for s4 in range(4):
                    st = 4 * g + s4
                    A("pe", lambda e, st=st, s4=s4, hd=hd, M=M, tb=tb: e.matmul(bank(tb)[0:M, s4 * 128:(s4 + 1) * 128], lhsT=PEN[hd][:, st, :], rhs=IDENT,
                                                                              start=True, stop=True),
                      reads=[("PEN", hd), "PENc", "CM"], writes=[("bank", tb)])
                A("dve", lambda e, hd=hd, g=g, cr=cr, tb=tb: e.tensor_copy(out=QT[hd][cr, g * TS:(g + 1) * TS], in_=bank(tb)[cr, :]),
                  reads=[("bank", tb)], writes=[("Qaug", hd)])
                release(tb)
            yield

    def do_pair(br, hp):
        if (br, hp) != pairs[0]:
            load_pair_weights(br, hp)

        def c_rows():
            for hd in range(2):
                h = 2 * hp + hd
                a0 = AUG0[hd]
                dma("sp", KT[hd][a0:a0 + 3, :], cscr[h, 0:3, :], reads=["cscr"], writes=[("Kaug", hd)])
                dma("sp", QT[hd][a0 + 3:a0 + 6, :], cscr[h, 3:6, 0:NOWN * TS], reads=["cscr"], writes=[("Qaug", hd)])
        first = not state0["tail_done"]
        if br == 0 and not first:
            c_rows()
        jobs = []
        if first:
            jobs.append((phase_c_tail_job(), []))
            state0["tail_done"] = True
        if br == 0:
            for i in range(NT):
                jobs.append((qk_job(br, False, WK, "WK", i), []))
                jobs.append((v_job(i), []))
            for i in range(NOWN):
                jobs.append((qk_job(br, True, WQ, "WQ", i), []))
                jobs.append((z_job(i), []))
        else:
            kq = [qk_job(br, False, WK, "WK", i) for i in range(NT)] + [qk_job(br, True, WQ, "WQ", i) for i in range(NOWN)]
            jobs += [(g_, []) for g_ in kq]
            jobs.append((gating_job(), kq))
            for i in range(NT):
                jobs.append((v_job(i), []))
                if i < NOWN:
                    jobs.append((z_job(i), []))
        run_pipeline(jobs)
        if br == 0 and first:
            c_rows()
        if first:
            R.barrier()
            A("pool", lambda e: e.memset(KMH2[:], 0.0), writes=["KMH2c"])
            A("pool", lambda e: e.memset(KML2[:], 0.0), writes=["KMH2c"])
            A("pool", lambda e: e.memset(PEN[0][:, :, 0:64], 0.0), writes=["PENc"])
            dma("sp", MTAB[:], mtab_d, writes=["MTAB"])
        if (br, hp) == pairs[-1] and stop_after is None:
            dma("pool", WG[:, :, 0:1024], w3[:, :, 4096:5120], reads=[], writes=["WG0"] + MOBA_KEYS)
            dma("pool", WG[:, :, 1024:2048], w3[:, :, 5120:6144], reads=[], writes=["WG1"] + MOBA_KEYS)
            dma("pool", WF[:], w_fox.rearrange("(c p) n -> p c n", p=128), writes=["WF", "WQ", "WK", "WZ", "WV"])
        attention_pair(br, hp)

    last_br = None
    state0 = {"tail_done": False}
    if pairs[0][0] != 0:
        run_pipeline([(phase_c_tail_job(), [])])
        state0["tail_done"] = True
    for (br, hp) in pairs:
        if br != last_br:
            for hd in range(2):
                a0 = AUG0[hd]
                if br == 0:
                    dma("sp", KT[hd][a0 + 3:a0 + 16, :], kac_d[3:16, :], writes=[("Kaug", hd)])
                    dma("sp", QT[hd][a0:a0 + 3, :], qac_d[0:3, :], writes=[("Qaug", hd)])
                    dma("sp", QT[hd][a0 + 6:a0 + 16, :], qac_d[6:16, :], writes=[("Qaug", hd)])
                else:
                    dma("sp", KT[hd][a0:a0 + 16, :], kam_d[:, :], writes=[("Kaug", hd)])
            last_br = br
        do_pair(br, hp)
    flush_pending()
    R.barrier()
    if "YT" in dbg_out:
        for c in sorted(set(b_ * 4 + h_ for (b_, h_) in pairs)):
            dma("sp", dbg_out["YT"][:, c, :], YT[:, c, :], reads=[("YT", c, j) for j in range(NOWN)])
    if "KA" in dbg_out:
        dma("sp", dbg_out["KA"][0:80, :], KT[0][0:80, :], reads=[("K", 0, i) for i in range(NT)] + [("Kaug", 0)])
        dma("sp", dbg_out["KB"], KT[1][:], reads=[("K", 1, i) for i in range(NT)] + [("Kaug", 1)])
        dma("sp", dbg_out["QA"][0:80, :], QT[0][0:80, :], reads=[("Q", 0, i) for i in range(NOWN)] + [("Qaug", 0)])
        dma("sp", dbg_out["QB"], QT[1][:], reads=[("Q", 1, i) for i in range(NOWN)] + [("Qaug", 1)])
        dma("sp", dbg_out["VB"], VT[1][:], reads=[("V", 1, g) for g in range(8)] + [("Vc", 1)])
        dma("sp", dbg_out["ZS"], ZS[:], reads=[("ZS", i) for i in range(NOWN)])
    if stop_after == "p1":
        R.emit(nc)
        return nc

    dma("pool", WM[:], w_moba.rearrange("(c p) n -> p c n", p=128), writes=["WM"])
    dma("pool", WO[:], w_out.rearrange("(c p) n -> p c n", p=128), writes=["WO"])
    rr = [0]
    xr = [0]
    for j in range(NOWN):
        for n in range(8):
            r = rr[0]
            rr[0] = 1 - r
            ba, bb, bf_, bm = next_bank(), next_bank(), next_bank(), next_bank()
            for gi, (bk, dst) in enumerate([(ba, SA), (bb, SB)]):
                for c in range(8):
                    A("pe", lambda e, c=c, gi=gi, bk=bk, n=n, j=j: e.matmul(
                        bank(bk), lhsT=WG[:, c, gi * 1024 + n * 128: gi * 1024 + (n + 1) * 128], rhs=HT[:, c, j * TS:(j + 1) * TS],
                        start=(c == 0), stop=(c == 7)),
                      reads=["WG%d" % gi, ("HT", j)], writes=[("bank", bk)])
                A("act", lambda e, gi=gi, bk=bk, n=n, dst=dst, r=r: e.activation(
                    out=dst[r][:], in_=bank(bk), func=AF.Sigmoid, bias=BGATE[:, gi * 8 + n: gi * 8 + n + 1], scale=1.0),
                  reads=[("bank", bk), "BGATE"], writes=[("sg", gi, r)])
            for (bk, W, wk, c0) in [(bf_, WF, "WF", 0), (bm, WM, "WM", 4)]:
                for c in range(4):
                    A("pe", lambda e, c=c, bk=bk, W=W, c0=c0, n=n, j=j: e.matmul(
                        bank(bk), lhsT=W[:, c, n * 128:(n + 1) * 128], rhs=YT[:, c0 + c, j * TS:(j + 1) * TS],
                        start=(c == 0), stop=(c == 3)),
                      reads=[wk] + [("YT", c0 + cc, j) for cc in range(4)], writes=[("bank", bk)])
            A("dve", lambda e, r=r, bf_=bf_: e.tensor_tensor(out=TT[r][:], in0=bank(bf_), in1=SA[r][:], op=ALU.mult),
              reads=[("bank", bf_), ("sg", 0, r)], writes=[("tt", r)])
            A("dve", lambda e, r=r, bm=bm: e.tensor_tensor(out=SB[r][:], in0=bank(bm), in1=SB[r][:], op=ALU.mult),
              reads=[("bank", bm), ("sg", 1, r)], writes=[("sg", 1, r)])
            A("dve", lambda e, r=r, n=n: e.tensor_tensor(out=MTT[:, n, :], in0=TT[r][:], in1=SB[r][:], op=ALU.add),
              reads=[("tt", r), ("sg", 1, r)], writes=[("mtt", n)])
        for ts4 in range(4):
            x = xr[0]
            xr[0] = 1 - x
            row0 = (j * 4 + ts4) * 128
            dma("sp", XO[x][:], xo[row0:row0 + 128, :], writes=[("xo", x)])
            for half in range(2):
                bo = next_bank()
                for c in range(8):
                    A("pe", lambda e, c=c, bo=bo, half=half, ts4=ts4: e.matmul(
                        bank(bo), lhsT=MTT[:, c, ts4 * 128:(ts4 + 1) * 128], rhs=WO[:, c, half * 512:(half + 1) * 512],
                        start=(c == 0), stop=(c == 7)),
                      reads=["WO"] + [("mtt", cc) for cc in range(8)], writes=[("bank", bo)])
                A("dve", lambda e, x=x, bo=bo, half=half: e.tensor_tensor(
                    out=OT[x][:, half * 512:(half + 1) * 512], in0=bank(bo), in1=XO[x][:, half * 512:(half + 1) * 512], op=ALU.add),
                  reads=[("bank", bo), ("xo", x)], writes=[("ot", x, half)])
            dma("sp", out_d[row0:row0 + 128, :], OT[x][:], reads=[("ot", x, 0), ("ot", x, 1)], writes=[("outd", row0)])
    R.emit(nc)
    return nc


def make_in_maps(inputs):
    x = np.asarray(inputs["x"], np.float32)
    w_in = np.ascontiguousarray(np.asarray(inputs["w_in"], np.float32)[0])
    w_fox = np.ascontiguousarray(np.asarray(inputs["w_fox"], np.float32)[0])
    w_moba = np.ascontiguousarray(np.asarray(inputs["w_moba"], np.float32)[0])
    w_out = np.ascontiguousarray(np.asarray(inputs["w_out"], np.float32)[0])
    gng = np.ascontiguousarray(np.asarray(inputs["norm_g"], np.float32)[0].reshape(8, 128).T)
    gains = np.ascontiguousarray(np.stack([
        np.tile(np.asarray(inputs["fox_q_g"], np.float32)[0], 2),
        np.tile(np.asarray(inputs["fox_k_g"], np.float32)[0], 2),
        np.tile(np.asarray(inputs["moba_q_g"], np.float32)[0], 2),
        np.tile(np.asarray(inputs["moba_k_g"], np.float32)[0], 2)], axis=1))
    bg = np.asarray(inputs["b_gate"], np.float32)[0]
    bgate = np.ascontiguousarray(bg.reshape(2, 8, 128).transpose(2, 0, 1).reshape(128, 16))
    bf = np.ascontiguousarray(np.asarray(inputs["b_f"], np.float32)[0].reshape(8, 1))
    tabs = [const_tables(p) for p in range(2)]
    maps = []
    for core in range(8):
        b, p = core // 2, core % 2
        pos = storage_pos(p)
        xs = x[b][pos]
        m = dict(xT=np.ascontiguousarray(xs.T), xo=np.ascontiguousarray(xs[:NOWN * TS]),
                 w_in=w_in, w_fox=w_fox, w_moba=w_moba, w_out=w_out, gng=gng, gains=gains,
                 bgate=bgate, bf=bf)
        m.update(tabs[p])
        maps.append(m)
    return maps


def kernel(**inputs):
    maps = make_in_maps(inputs)
    nc = build_program()
    res = run_bass_kernel_spmd(nc, maps, core_ids=list(range(8)))
    out = np.zeros((NBATCH, SEQ, D), np.float32)
    for core in range(8):
        b, p = core // 2, core % 2
        pos = storage_pos(p)
        out[b, pos[:NOWN * TS]] = res.results[core]["out"]
    return out
```

```python
import numpy as np
import ml_dtypes
import concourse.bass as bass
import concourse.mybir as mybir
from concourse.bass_utils import run_bass_kernel_spmd

F32 = mybir.dt.float32
BF16 = mybir.dt.bfloat16
AF = mybir.ActivationFunctionType
ALU = mybir.AluOpType
AX = mybir.AxisListType

D = 1024
SEQ = 4096
NBATCH = 4
TS = 512
NT = 8
NOWN = 4
HD = 64
INW = 6152
BIG = 30000.0
EPS = 1e-6
PI = [[0, 3, 4, 7, 1, 2, 5, 6], [1, 2, 5, 6, 0, 3, 4, 7]]
ROPE_DIM = 16
ROPE_THETA = 500000.0


def past_lists():
    out = []
    for j in range(NOWN):
        s = set()
        for p in range(2):
            for i in range(NT):
                if PI[p][i] < PI[p][j]:
                    s.add(i)
        out.append(sorted(s))
    return out


class Rec:
    ENGS = ("pe", "act", "dve", "pool", "sp")

    def __init__(self):
        self.ops = []
        self.lastw = {}
        self.readers = {}

    def add(self, eng, fn, reads=(), writes=(), dma=False):
        oid = len(self.ops)
        deps = set()
        for k in reads:
            if k in self.lastw:
                deps.add(self.lastw[k])
        for k in writes:
            if k in self.lastw:
                deps.add(self.lastw[k])
            for r in self.readers.get(k, {}).values():
                deps.update(r)
        for k in reads:
            d = self.readers.setdefault(k, {})
            if dma:
                d.setdefault("dma", []).append(oid)
            else:
                d[eng] = [oid]
        for k in writes:
            self.lastw[k] = oid
            self.readers[k] = {}
        self.ops.append(dict(eng=eng, fn=fn, deps=deps, dma=dma, inc=False))
        return oid

    def barrier(self):
        last = {}
        dmas = []
        for oid, op in enumerate(self.ops):
            if op.get("bar"):
                continue
            if op["dma"]:
                dmas.append(oid)
            else:
                last[op["eng"]] = oid
        deps = set(last.values()) | set(dmas)
        sp_id = len(self.ops)
        self.ops.append(dict(eng="sp", fn=(lambda e: e.sem_inc(self._sp_sem, 1)), deps=set(deps), dma=False, inc=True,
                             bar=True, selfinc=True))
        for e in self.ENGS:
            if e != "sp":
                self.ops.append(dict(eng=e, fn=None, deps={sp_id}, dma=False, inc=False, bar=True))

    def emit(self, nc, nsem_sp=24, nsem_pool=12):
        ops = self.ops
        for op in ops:
            for d in op["deps"]:
                if not ops[d]["dma"]:
                    ops[d]["inc"] = True
        cnt = {e: 0 for e in self.ENGS}
        dcount = {"sp": 0, "pool": 0, "act": 0}
        nsem = {"sp": nsem_sp, "pool": nsem_pool, "act": 4}
        for op in ops:
            e = op["eng"]
            if op["dma"]:
                k = dcount[e]
                dcount[e] += 1
                op["dsem"] = (e, k % nsem[e])
                op["dtarget"] = 16 * (k // nsem[e] + 1)
            elif op["inc"]:
                cnt[e] += 1
                op["val"] = cnt[e]
        import contextlib
        with contextlib.ExitStack() as es:
            sems = {e: es.enter_context(nc.semaphore("s_" + e)) for e in ("pe", "act", "dve", "pool", "sp")}
            self._sp_sem = sems["sp"]
            dsems = {}
            for q in ("sp", "pool"):
                if dcount[q]:
                    for i in range(min(nsem[q], dcount[q])):
                        dsems[(q, i)] = es.enter_context(nc.semaphore("d_%s%d" % (q, i)))
            block = es.enter_context(nc.Block())
            handles = {"pe": block.tensor, "act": block.scalar, "dve": block.vector,
                       "pool": block.gpsimd, "sp": block.sync}
            final_d = {}
            for op in ops:
                if op["dma"]:
                    final_d[op["dsem"]] = max(final_d.get(op["dsem"], 0), op["dtarget"])

            def run_engine(e):
                def body(eng):
                    seen = {}

                    def wait(sem_key, sem, val):
                        if seen.get(sem_key, 0) >= val:
                            return
                        seen[sem_key] = val
                        eng.wait_ge(sem, val)

                    for op in ops:
                        if op["eng"] != e:
                            continue
                        for d in sorted(op["deps"]):
                            dop = ops[d]
                            if dop["dma"]:
                                wait(dop["dsem"], dsems[dop["dsem"]], dop["dtarget"])
                            else:
                                if dop["eng"] == "pe" and e == "pe" and not op["dma"]:
                                    continue
                                wait(dop["eng"], sems[dop["eng"]], dop["val"])
                        if op["fn"] is None:
                            continue
                        if op["dma"]:
                            if op["dtarget"] > 16:
                                wait(op["dsem"], dsems[op["dsem"]], op["dtarget"] - 16)
                            inst = op["fn"](eng)
                            inst.then_inc(dsems[op["dsem"]], 16)
                        else:
                            inst = op["fn"](eng)
                            if op["inc"] and not op.get("selfinc"):
                                inst.then_inc(sems[e], 1)
                    if e == "sp":
                        for k, v in final_d.items():
                            wait(k, dsems[k], v)
                        for x in ("pe", "act", "dve", "pool"):
                            if cnt[x]:
                                wait(x, sems[x], cnt[x])
                return body

            for e in self.ENGS:
                handles[e](run_engine(e))


def storage_pos(p):
    return np.concatenate([np.arange(TS) + PI[p][i] * TS for i in range(NT)])


def const_tables(p):
    bf = ml_dtypes.bfloat16
    pos = storage_pos(p)
    t = {}
    tile_of = np.arange(SEQ) // TS
    kac = np.zeros((16, SEQ), np.float32)
    kac[3:6] = 1.0
    for j in range(NOWN):
        kac[6 + j] = np.where(np.array(PI[p])[tile_of] <= PI[p][j], 0.0, -BIG)
    qac = np.zeros((16, NOWN * TS), np.float32)
    qac[0:3] = 1.0
    for j in range(NOWN):
        qac[6 + j] = (tile_of[:NOWN * TS] == j).astype(np.float32)
    kam = np.zeros((16, SEQ), np.float32)
    blk = np.arange(SEQ) // 256
    for j in range(16):
        kam[j] = (blk == j).astype(np.float32)
    t["kac"] = kac.astype(bf)
    t["qac"] = qac.astype(bf)
    t["kam"] = kam.astype(bf)
    act_blk = np.array([PI[p][j // 2] * 2 + j % 2 for j in range(16)])
    mt = np.zeros((3, 16, 2, 16), np.float32)
    for st in range(16):
        own = st // 2
        valid = (act_blk < act_blk[own]).astype(np.float32)
        mt[0, st, :, :] = (valid - 1.0) * BIG
        mt[1, st, :, :] = valid
        mt[2, st, :, own] = 1.0
    t["mtab"] = np.ascontiguousarray(np.broadcast_to(mt.reshape(1, 3, 512), (128, 3, 512))).astype(bf)
    half = ROPE_DIM // 2
    inv_freq = (np.float32(ROPE_THETA) ** (-np.arange(0, half, dtype=np.float32) * np.float32(2.0) / np.float32(ROPE_DIM))).astype(np.float32)
    ang = (pos.astype(np.float32)[:, None] * inv_freq[None, :]).astype(np.float32)
    cos = np.ones((128, SEQ), np.float32)
    sin = np.zeros((128, SEQ), np.float32)
    for r in range(128):
        d = r % HD
        if d < ROPE_DIM:
            cos[r] = np.cos(ang[:, d % half].astype(np.float64)).astype(np.float32)
            sin[r] = np.sin(ang[:, d % half].astype(np.float64)).astype(np.float32)
    t["cos"] = cos
    t["sin"] = sin
    cm = np.zeros((128, 8, 128), np.float32)
    cm[:, 0, :] = np.eye(128)
    s_idx = np.arange(128)[:, None]
    t_idx = np.arange(128)[None, :]
    cm[:, 1, :] = np.where(s_idx > t_idx, -BIG, 0.0)
    cm[:, 2, :] = 1.0 / 1024.0
    bd = (s_idx // HD == t_idx // HD).astype(np.float32)
    cm[:, 3, :] = bd / 64.0
    rt = np.zeros((128, 128), np.float32)
    for m in range(128):
        d = m % HD
        if d < half:
            rt[m + half, m] = -1.0
        elif d < ROPE_DIM:
            rt[m - half, m] = 1.0
    cm[:, 4, :] = rt
    cm[:, 5, :] = bd
    cm[:, 6, :] = 1.0
    cm[64, 7, :] = 1.0
    t["cmat"] = cm.astype(bf)
    dm = np.zeros((128, 4, 512), np.float32)
    for s4 in range(4):
        dm[:, s4, :] = np.where((s4 * 128 + np.arange(128))[:, None] > np.arange(512)[None, :], -BIG, 0.0)
    t["dmask"] = dm.astype(bf)
    ao = np.zeros((8, 8, 8), np.float32)
    for i in range(8):
        for j in range(8):
            ao[:, i, j] = 1.0 if PI[p][j] < PI[p][i] else 0.0
    t["aoff"] = ao
    return t


def build_program(stop_after=None, dbg=()):
    nc = bass.Bass("TRN2", target_bir_lowering=False)
    R = Rec()

    def din(name, shape, dt=F32):
        return nc.dram_tensor(name, list(shape), dt, kind="ExternalInput").ap()

    xT = din("xT", [D, SEQ])
    xo = din("xo", [NOWN * TS, D])
    w_in = din("w_in", [D, INW])
    w_fox = din("w_fox", [512, D])
    w_moba = din("w_moba", [512, D])
    w_out = din("w_out", [D, D])
    gng_d = din("gng", [128, 8])
    gains_d = din("gains", [128, 4])
    bgate_d = din("bgate", [128, 16])
    bf_d = din("bf", [8, 1])
    kac_d = din("kac", [16, SEQ], BF16)
    qac_d = din("qac", [16, NOWN * TS], BF16)
    kam_d = din("kam", [16, SEQ], BF16)
    mtab_d = din("mtab", [128, 3, 512], BF16)
    cos_d = din("cos", [128, SEQ])
    sin_d = din("sin", [128, SEQ])
    cmat_d = din("cmat", [128, 8, 128], BF16)
    aoff_d = din("aoff", [8, 8, 8])
    dmask_d = din("dmask", [128, 4, 512], BF16)
    out_d = nc.dram_tensor("out", [NOWN * TS, D], F32, kind="ExternalOutput").ap()
    cscr = nc.dram_tensor("cscr", [8, 6, SEQ], BF16).ap()
    dbg_out = {}
    for name, shape, dt in dbg:
        dbg_out[name] = nc.dram_tensor(name, list(shape), dt, kind="ExternalOutput").ap()

    cur = [16640]
    OFFS = {}

    def alloc(name, shape, dt):
        sz = int(np.prod(shape[1:])) * (2 if dt == BF16 else 4)
        off = (cur[0] + 31) // 32 * 32
        t = nc.alloc_sbuf_tensor_at(name, list(shape), dt, offset=off)
        cur[0] = off + sz
        OFFS[name] = (off, off + sz)
        return t

    HT = alloc("HT", [128, 8, SEQ], BF16)
    YT = alloc("YT", [128, 8, NOWN * TS], BF16)
    CM = alloc("CM", [128, 8, 128], BF16)
    GNG = alloc("GNG", [128, 8], F32)
    GAINS = alloc("GAINS", [128, 4], F32)
    BGATE = alloc("BGATE", [128, 16], F32)
    BFT = alloc("BFT", [128, 2], F32)
    EPSC = alloc("EPSC", [128, 2], F32)
    phase_base = cur[0]
    IDENT = CM[:, 0, :]
    TRI = CM[:, 1, :]
    ONESM = CM[:, 2, :]
    BD = CM[:, 3, :]
    RT = CM[:, 4, :]
    BDQ = CM[:, 5, :]
    ONES1 = CM[:, 6, :]
    ONESEL = CM[:, 7, :]

    PS = [nc.alloc_psum_tensor("ps%d" % i, [128, 1024], F32) for i in range(4)]

    def bank(b):
        return PS[b // 2][:, (b % 2) * 512:(b % 2 + 1) * 512]

    def dma(q, out, in_, reads=(), writes=()):
        return R.add(q, lambda e: e.dma_start(out=out, in_=in_), reads=reads, writes=writes, dma=True)

    dma("sp", CM[:], cmat_d, writes=["CM"])
    dma("sp", GNG[:], gng_d, writes=["GNG"])
    dma("sp", GAINS[:], gains_d, writes=["GAINS"])
    dma("sp", BGATE[:], bgate_d, writes=["BGATE"])
    dma("sp", BFT[0:8, 0:1], bf_d, writes=["BFT"])
    R.add("dve", lambda e: e.memset(EPSC[:, 0:1], EPS), writes=["EPSC"])
    R.add("dve", lambda e: e.memset(EPSC[:, 1:2], EPS * 64.0), writes=["EPSC"])

    if stop_after == "c0":
        R.add("dve", lambda e: e.memset(HT[:, 0, 0:512], 1.0), writes=[("HT", 0)])
        R.barrier()
        dma("sp", dbg_out["HT"][:, 0, 0:512], HT[:, 0, 0:512], reads=[("HT", 0)])
        R.emit(nc)
        return nc
    cur[0] = phase_base
    KT = [alloc("KA", [128, SEQ], BF16), alloc("KB", [128, SEQ], BF16)]
    QT = [alloc("QA", [128, NOWN * TS], BF16), alloc("QB", [128, NOWN * TS], BF16)]
    VT = [alloc("VA", [128, 32, 66], BF16), alloc("VB", [128, 32, 128], BF16)]
    ZS = alloc("ZS", [128, NOWN * TS], BF16)
    WQ = alloc("WQ", [128, 8, 128], BF16)
    WK = alloc("WK", [128, 8, 128], BF16)
    WZ = alloc("WZ", [128, 8, 128], BF16)
    WV = alloc("WV", [128, 8, 128], BF16)
    PT = [alloc("PT%d" % i, [128, 1024], BF16) for i in range(3)]
    SQ1 = [alloc("SQ1_%d" % i, [128, TS], BF16) for i in range(2)]
    RS = [alloc("RS%d" % i, [128, TS], F32) for i in range(2)]
    T1 = [alloc("T1_%d" % i, [128, TS], F32) for i in range(2)]
    _rec = alloc("REC", [128, TS], F32)
    REC = [_rec, _rec]
    RH = [alloc("RHA", [128, TS], BF16), alloc("RHB", [128, TS], BF16)]
    RL = [alloc("RLA", [128, TS], BF16), alloc("RLB", [128, TS], BF16)]
    _ytmp = alloc("YTMP", [128, TS], F32)
    YTMP = [_ytmp, _ytmp]
    DMASK = alloc("DMASK", [128, 4, 512], BF16)
    moba_base = cur[0]
    ABT = [[alloc("ABT%d_%d" % (i, k), [128, TS], BF16) for k in range(2)] for i in range(2)]
    RCT = [[alloc("RCT%d_%d" % (i, k), [128, TS], F32) for k in range(2)] for i in range(2)]
    COST = [alloc("COST%d" % i, [128, TS], F32) for i in range(2)]
    SINT = [alloc("SINT%d" % i, [128, TS], F32) for i in range(2)]
    MTAB = alloc("MTAB", [128, 3, 512], BF16)
    GM = alloc("GM", [128, 512], F32)
    T8 = alloc("T8", [128, 32, 8], F32)
    SEL = alloc("SEL", [128, 512], F32)
    PEN = [alloc("PENA", [128, 16, 80], BF16), alloc("PENB", [128, 16, 16], BF16)]
    KM = alloc("KM", [128, 16], F32)
    KMR = alloc("KMR", [128, 16], F32)
    KMH = alloc("KMH", [128, 16], BF16)
    KML = alloc("KML", [128, 16], BF16)
    KMH2 = alloc("KMH2", [128, 2, 16], BF16)
    KML2 = alloc("KML2", [128, 2, 16], BF16)
    SB_LIMIT = 16512 + 212863
    print("phase1 sbuf end", cur[0], "moba_base", moba_base, "limit", SB_LIMIT)
    assert cur[0] <= SB_LIMIT, cur[0]
    cur[0] = phase_base
    WO = alloc("WO", [128, 8, D], BF16)
    SA = [alloc("SA%d" % i, [128, TS], F32) for i in range(2)]
    SB = [alloc("SB%d" % i, [128, TS], F32) for i in range(2)]
    TT = [alloc("TT%d" % i, [128, TS], F32) for i in range(2)]
    MTT = alloc("MTT", [128, 8, TS], BF16)
    assert cur[0] <= OFFS["WQ"][0], (cur[0], OFFS["WQ"])
    cur[0] = OFFS["WQ"][0]
    WF = alloc("WF", [128, 4, D], BF16)
    assert cur[0] <= OFFS["WV"][1]
    cur[0] = OFFS["WV"][1]
    WM = alloc("WM", [128, 4, D], BF16)
    XO = [alloc("XO%d" % i, [128, D], F32) for i in range(2)]
    OT = [alloc("OT%d" % i, [128, D], F32) for i in range(2)]
    assert cur[0] <= moba_base, (cur[0], moba_base)
    cur[0] = moba_base
    WG = alloc("WG", [128, 8, 2048], BF16)
    assert cur[0] <= SB_LIMIT, cur[0]
    MOBA_KEYS = ([("abt", a_, k_) for a_ in range(2) for k_ in range(2)] + [("rct", a_, k_) for a_ in range(2) for k_ in range(2)]
                 + [("cost", a_) for a_ in range(2)] + [("sint", a_) for a_ in range(2)] + ["MTAB", "GM", "SEL", "PENc", "KMH", "KMR", "KML", "KMH2c"]
                 + [("T8", g_) for g_ in range(32)] + [("PEN", h_) for h_ in range(2)] + [("KM", i_) for i_ in range(NT)]
                 + [("KMH2", h_) for h_ in range(2)] + [("KML2", h_) for h_ in range(2)])

    w3 = w_in.rearrange("(c p) n -> p c n", p=128)

    def A(eng, fn, reads=(), writes=()):
        return R.add(eng, fn, reads=reads, writes=writes)

    cur[0] = moba_base
    WFL = alloc("WFL", [128, 8, 8], BF16)
    FLE = alloc("FLE", [8, TS], F32)
    CL = alloc("CL", [8, SEQ], F32)
    ONES8 = alloc("ONES8", [8, TS], F32)
    TOT = alloc("TOT", [8, 8], F32)
    OFF = alloc("OFF", [8, 8], F32)
    TMP8 = alloc("TMP8", [8, 8], F32)
    AOFF = alloc("AOFF", [8, 8, 8], F32)
    CC = alloc("CC", [8, TS], F32)
    R1 = alloc("R1", [8, TS], F32)
    _pie = alloc("PIE", [8, 6, TS], BF16)
    PIE = [_pie, _pie]
    assert cur[0] <= SB_LIMIT, cur[0]
    dma("pool", WFL[:], w3[:, :, 6144:6152], writes=["WFL"])
    dma("sp", AOFF[:], aoff_d, writes=["AOFF"])
    A("dve", lambda e: e.memset(ONES8[:], 1.0), writes=["ONES8"])
    A("dve", lambda e: e.tensor_scalar(out=BFT[0:8, 1:2], in0=BFT[0:8, 0:1], scalar1=-1.0, scalar2=None, op0=ALU.mult),
      reads=["BFT"], writes=["NBF"])

    import os as _os
    pairs = [(0, hp) for hp in range(4)] + [(1, hp) for hp in range(4)]
    if _os.environ.get("K_PAIRS"):
        pairs = [tuple(int(v) for v in t.split(":")) for t in _os.environ["K_PAIRS"].split(",")]

    def load_pair_weights(br, hp):
        base = br * 2048
        dma("pool", WK[:], w3[:, :, base + 512 + hp * 128: base + 512 + (hp + 1) * 128], writes=["WK"])
        dma("pool", WV[:], w3[:, :, base + 1024 + hp * 128: base + 1024 + (hp + 1) * 128], writes=["WV"])
        dma("pool", WQ[:], w3[:, :, base + hp * 128: base + (hp + 1) * 128], writes=["WQ"])
        dma("pool", WZ[:], w3[:, :, base + 1536 + hp * 128: base + 1536 + (hp + 1) * 128], writes=["WZ"])
    load_pair_weights(*pairs[0])

    def phase_c_front_pe(i):
        pb = 2 + i % 2
        for c in range(8):
            A("pe", lambda e, c=c: e.matmul(bank(pb)[0:8, :], lhsT=WFL[:, c, :], rhs=HT[:, c, i * TS:(i + 1) * TS],
                                            start=(c == 0), stop=(c == 7)),
              reads=["WFL", ("HT", i, c)], writes=[("bank", pb)])

    def phase_c_front_rest(i):
        pb = 2 + i % 2
        A("act", lambda e: e.activation(out=FLE[:], in_=bank(pb)[0:8, :], func=AF.Exp, bias=BFT[0:8, 1:2], scale=-1.0),
          reads=[("bank", pb), "NBF"], writes=["fle"])
        A("act", lambda e: e.activation(out=FLE[:], in_=FLE[:], func=AF.Ln, bias=1.0, scale=1.0),
          reads=["fle"], writes=["fle"])
        A("dve", lambda e: e.tensor_tensor_scan(out=CL[:, i * TS:(i + 1) * TS], data0=ONES8[:], data1=FLE[:],
                                                initial=0.0, op0=ALU.mult, op1=ALU.add),
          reads=["fle", "ONES8"], writes=[("cl", i)])
        A("dve", lambda e: e.tensor_copy(out=TOT[:, i:i + 1], in_=CL[:, i * TS + TS - 1:i * TS + TS]),
          reads=[("cl", i)], writes=["TOT"])

    def phase_c_tail_job():
        for i in range(NT):
            A("dve", lambda e, i=i: e.tensor_tensor(out=TMP8[:], in0=AOFF[:, i, :], in1=TOT[:], op=ALU.mult),
              reads=["AOFF", "TOT"], writes=["TMP8"])
            A("dve", lambda e, i=i: e.reduce_sum(out=OFF[:, i:i + 1], in_=TMP8[:], axis=AX.X),
              reads=["TMP8"], writes=["OFF"])
        yield
        for i in range(NT):
            r = 0
            A("dve", lambda e, r=r, i=i: e.tensor_scalar(out=CC[:], in0=CL[:, i * TS:(i + 1) * TS], scalar1=OFF[:, i:i + 1],
                                                         scalar2=-1.0, op0=ALU.add, op1=ALU.mult),
              reads=[("cl", i), "OFF"], writes=["cc"])
            A("dve", lambda e, r=r: e.tensor_copy(out=PIE[r][:, 3, :], in_=CC[:]), reads=["cc"], writes=[("pie", r)])
            A("dve", lambda e, r=r: e.tensor_tensor(out=R1[:], in0=CC[:], in1=PIE[r][:, 3, :], op=ALU.subtract),
              reads=["cc", ("pie", r)], writes=["r1"])
            A("dve", lambda e, r=r: e.tensor_copy(out=PIE[r][:, 4, :], in_=R1[:]), reads=["r1"], writes=[("pie", r)])
            yield
            A("dve", lambda e, r=r: e.tensor_tensor(out=CC[:], in0=R1[:], in1=PIE[r][:, 4, :], op=ALU.subtract),
              reads=["r1", ("pie", r)], writes=["cc"])
            A("dve", lambda e, r=r: e.tensor_copy(out=PIE[r][:, 5, :], in_=CC[:]), reads=["cc"], writes=[("pie", r)])
            A("dve", lambda e, r=r: e.tensor_scalar(out=PIE[r][:, 0:3, :], in0=PIE[r][:, 3:6, :], scalar1=-1.0, scalar2=None, op0=ALU.mult),
              reads=[("pie", r)], writes=[("pie", r)])
            dma("sp", cscr[:, :, i * TS:(i + 1) * TS], PIE[r][:], reads=[("pie", r)], writes=["cscr"])
            yield

    cur[0] = phase_base
    XT8 = [alloc("XT8_%d" % i, [128, 8, TS], F32) for i in range(2)]
    SQ0 = [alloc("SQ0_%d" % i, [128, TS], BF16) for i in range(2)]
    RSTD = [alloc("RSTD_%d" % i, [128, TS], F32) for i in range(2)]
    cur[0] = OFFS["PT0"][0]
    XT8.append(alloc("XT8_2", [128, 8, TS], F32))
    assert cur[0] <= OFFS["REC"][0], (cur[0], OFFS["REC"])
    import os
    P0T = int(os.environ.get("P0_TILES", NT))
    SKIP = set(os.environ.get("P0_SKIP", "").split(","))
    for i in range(P0T):
        b = i % 3
        rb = i % 2
        for c in range(8):
            dma("sp", XT8[b][:, c, :], xT[c * 128:(c + 1) * 128, i * TS:(i + 1) * TS],
                writes=[("xt8", b, c)])
        pb = i % 2
        if i >= 2:
            phase_c_front_pe(i - 2)
        for c in range(8):
            R.add("act", lambda e, b=b, c=c: e.activation(out=SQ0[c % 2][:], in_=XT8[b][:, c, :], func=AF.Square),
                  reads=[("xt8", b, c)], writes=[("sq0", c % 2)])
            R.add("pe", lambda e, c=c, pb=pb: e.matmul(bank(pb), lhsT=ONESM, rhs=SQ0[c % 2][:], start=(c == 0), stop=(c == 7)),
                  reads=[("sq0", c % 2), "CM"], writes=[("bank", pb)])
        R.add("act", lambda e, rb=rb, pb=pb: e.activation(out=RSTD[rb][:], in_=bank(pb), func=AF.Ln, bias=EPSC[:, 0:1], scale=1.0),
              reads=[("bank", pb), "EPSC"], writes=[("rstd", rb)])
        R.add("act", lambda e, rb=rb: e.activation(out=RSTD[rb][:], in_=RSTD[rb][:], func=AF.Exp, scale=-0.5),
              reads=[("rstd", rb)], writes=[("rstd", rb)])
        for c in range(8):
            R.add("dve", lambda e, b=b, rb=rb, c=c, i=i: e.scalar_tensor_tensor(
                out=HT[:, c, i * TS:(i + 1) * TS], in0=XT8[b][:, c, :], scalar=GNG[:, c:c + 1], in1=RSTD[rb][:],
                op0=ALU.mult, op1=ALU.mult),
                reads=[("xt8", b, c), ("rstd", rb), "GNG"], writes=[("HT", i, c)])
        if i >= 2:
            phase_c_front_rest(i - 2)
    for i in range(max(P0T - 2, 0), P0T):
        phase_c_front_pe(i)
        phase_c_front_rest(i)
    R.barrier()

    if "HT" in dbg_out:
        for c in range(8):
            dma("sp", dbg_out["HT"][:, c, :], HT[:, c, :], reads=[("HT", i, c) for i in range(NT)])

    if stop_after == "p0":
        R.emit(nc)
        return nc

    HROWS = [slice(0, 80), slice(0, 128)]
    MROWS = [slice(0, 64), slice(64, 128)]
    AUG0 = [64, 0]
    VCOLS = [66, 128]
    DENROW = [slice(64, 65), slice(0, 1)]
    plists = past_lists()

    A("pool", lambda e: e.memset(KT[1][0:64, :], 0.0), writes=[("Kaug", 1)])
    A("pool", lambda e: e.memset(QT[1][0:64, :], 0.0), writes=[("Qaug", 1)])
    A("pool", lambda e: e.memset(QT[0][64:80, :], 0.0), writes=[("Qaug", 0)])
    A("pool", lambda e: e.memset(RH[0][0:64, :], 0.0), writes=["RHc"])
    A("pool", lambda e: e.memset(RL[0][0:64, :], 0.0), writes=["RHc"])
    A("pool", lambda e: e.memset(VT[0][:, :, 64:66], 1.0), writes=[("Vc", 0)])
    A("pool", lambda e: e.memset(VT[1][:, :, 0:2], 1.0), writes=[("Vc", 1)])
    A("pool", lambda e: e.memset(VT[1][:, :, 2:64], 0.0), writes=[("Vc", 1)])
    dma("sp", DMASK[:], dmask_d, writes=["DMASK"])

    bank_rot = [0]

    NROT = [8]

    def next_bank():
        b = bank_rot[0] % NROT[0]
        bank_rot[0] = (b + 1) % NROT[0]
        return b

    free_banks = list(range(8))

    def acquire():
        assert free_banks, "out of PSUM banks"
        return free_banks.pop(0)

    def release(b):
        assert b not in free_banks
        free_banks.append(b)

    rotn = {"sq": (0, 2), "rs": (0, 2), "t": (0, 2), "cs": (0, 2), "rc": (0, 2), "ab": (0, 2)}

    def nxt(k):
        v, n = rotn[k]
        rotn[k] = ((v + 1) % n, n)
        return v

    def proj_feat(W, wkey, i, bk):
        for c in range(8):
            A("pe", lambda e, c=c: e.matmul(bank(bk), lhsT=W[:, c, :], rhs=HT[:, c, i * TS:(i + 1) * TS],
                                            start=(c == 0), stop=(c == 7)),
              reads=[wkey, ("HT", i, c)], writes=[("bank", bk)])

    def qk_job(br, isq, W, wkey, i):
        gcol = br * 2 + (0 if isq else 1)
        dst = QT if isq else KT
        dkey = "Q" if isq else "K"
        cols = slice(i * TS, (i + 1) * TS)
        bk = acquire()
        proj_feat(W, wkey, i, bk)
        yield
        sq = nxt("sq")
        A("act", lambda e: e.activation(out=SQ1[sq][:], in_=bank(bk), func=AF.Square),
          reads=[("bank", bk)], writes=[("sq1", sq)])
        by = acquire()
        A("pe", lambda e: e.matmul(bank(by), lhsT=(BDQ if isq else BD), rhs=SQ1[sq][:], start=True, stop=True),
          reads=[("sq1", sq), "CM"], writes=[("bank", by)])
        if br == 1:
            cs = nxt("cs")
            dma("sp", COST[cs][:], cos_d[:, cols], writes=[("cost", cs)])
            dma("sp", SINT[cs][:], sin_d[:, cols], writes=[("sint", cs)])
        yield
        r = nxt("rs")
        ec = 1 if isq else 0
        A("act", lambda e: e.activation(out=RS[r][:], in_=bank(by), func=AF.Ln, bias=EPSC[:, ec:ec + 1], scale=1.0),
          reads=[("bank", by), "EPSC"], writes=[("rs", r)])
        A("act", lambda e: e.activation(out=RS[r][:], in_=RS[r][:], func=AF.Exp, scale=-0.5),
          reads=[("rs", r)], writes=[("rs", r)])
        release(by)
        if br == 0:
            for hd in range(2):
                mr = MROWS[hd]
                A("dve", lambda e, hd=hd, mr=mr: e.scalar_tensor_tensor(
                    out=dst[hd][mr, cols], in0=bank(bk)[mr, :], scalar=GAINS[mr, gcol:gcol + 1], in1=RS[r][mr, :],
                    op0=ALU.mult, op1=ALU.mult),
                  reads=[("bank", bk), ("rs", r), "GAINS"], writes=[(dkey, hd, i)])
            release(bk)
            return
        rc = nxt("rc")
        ab = nxt("ab")
        A("pool", lambda e: e.tensor_tensor(out=RCT[rc][0][:], in0=RS[r][:], in1=COST[cs][:], op=ALU.mult),
          reads=[("rs", r), ("cost", cs)], writes=[("rct", rc, 0)])
        A("pool", lambda e: e.tensor_tensor(out=RCT[rc][1][:], in0=RS[r][:], in1=SINT[cs][:], op=ALU.mult),
          reads=[("rs", r), ("sint", cs)], writes=[("rct", rc, 1)])
        for k2 in range(2):
            A("dve", lambda e, k2=k2: e.scalar_tensor_tensor(
                out=ABT[ab][k2][:], in0=bank(bk), scalar=GAINS[:, gcol:gcol + 1], in1=RCT[rc][k2][:], op0=ALU.mult, op1=ALU.mult),
              reads=[("bank", bk), ("rct", rc, k2), "GAINS"], writes=[("abt", ab, k2)])
        release(bk)
        yield
        bz = acquire()
        A("pe", lambda e: e.matmul(bank(bz), lhsT=IDENT, rhs=ABT[ab][0][:], start=True, stop=False),
          reads=[("abt", ab, 0), "CM"], writes=[("bank", bz)])
        A("pe", lambda e: e.matmul(bank(bz), lhsT=RT, rhs=ABT[ab][1][:], start=False, stop=True),
          reads=[("abt", ab, 1), "CM"], writes=[("bank", bz)])
        yield
        for hd in range(2):
            mr = MROWS[hd]
            A("act", lambda e, hd=hd, mr=mr: e.activation(out=dst[hd][mr, cols], in_=bank(bz)[mr, :], func=AF.Copy),
              reads=[("bank", bz)], writes=[(dkey, hd, i)])
        if not isq:
            for hd in range(2):
                mr = MROWS[hd]
                A("dve", lambda e, hd=hd, mr=mr: e.reduce_sum(out=KM[mr, 2 * i:2 * i + 2],
                                                             in_=bank(bz)[mr, :].rearrange("p (b l) -> p b l", l=256), axis=AX.X),
                  reads=[("bank", bz)], writes=[("KM", i)])
        release(bz)

    def z_job(i):
        bk = acquire()
        proj_feat(WZ, "WZ", i, bk)
        yield
        t = nxt("t")
        A("act", lambda e: e.activation(out=T1[t][:], in_=bank(bk), func=AF.Exp, scale=-1.0),
          reads=[("bank", bk)], writes=[("t1", t)])
        A("act", lambda e: e.activation(out=T1[t][:], in_=T1[t][:], func=AF.Ln, bias=1.0, scale=1.0),
          reads=[("t1", t)], writes=[("t1", t)])
        A("act", lambda e: e.activation(out=T1[t][:], in_=T1[t][:], func=AF.Exp, scale=-1.0),
          reads=[("t1", t)], writes=[("t1", t)])
        A("dve", lambda e: e.tensor_tensor(out=ZS[:, i * TS:(i + 1) * TS], in0=T1[t][:], in1=bank(bk), op=ALU.mult),
          reads=[("t1", t), ("bank", bk)], writes=[("ZS", i)])
        release(bk)

    def v_job(g):
        bk = acquire()
        for s4 in range(4):
            st = 4 * g + s4
            for c in range(8):
                A("pe", lambda e, c=c, st=st, s4=s4: e.matmul(bank(bk)[:, s4 * 128:(s4 + 1) * 128],
                                                            lhsT=HT[:, c, st * 128:(st + 1) * 128], rhs=WV[:, c, :],
                                                            start=(c == 0), stop=(c == 7)),
                  reads=["WV", ("HT", st // 4, c)], writes=[("bank", bk)])
        yield
        src = bank(bk).rearrange("p (s n) -> p s n", n=128)
        A("act", lambda e: e.activation(out=VT[0][:, 4 * g:4 * g + 4, 0:64], in_=src[:, :, 0:64], func=AF.Copy),
          reads=[("bank", bk)], writes=[("V", 0, g)])
        A("act", lambda e: e.activation(out=VT[1][:, 4 * g:4 * g + 4, 64:128], in_=src[:, :, 64:128], func=AF.Copy),
          reads=[("bank", bk)], writes=[("V", 1, g)])
        release(bk)

    def run_pipeline(jobs):
        pend = list(jobs)
        active = []
        done = set()
        nstep = [0]
        held = []
        if pending:
            for b_ in (6, 7):
                free_banks.remove(b_)
                held.append(b_)
        while pend or active:
            for k, (gnr, after) in enumerate(pend):
                if all(id(a_) in done for a_ in after):
                    active.append(gnr)
                    pend.pop(k)
                    break
            for gnr in reversed(list(active)):
                try:
                    next(gnr)
                except StopIteration:
                    active.remove(gnr)
                    done.add(id(gnr))
            nstep[0] += 1
            if nstep[0] == 3:
                flush_pending()
                while held:
                    release(held.pop())
        assert not held

    def attention_pair(br, hp):
        chunk = br * 4 + hp
        flat = []
        for (j, hd) in [(0, 0), (3, 1), (1, 0), (2, 1), (2, 0), (1, 1), (3, 0), (0, 1)]:
            past = []
            for i in plists[j]:
                past += [dict(ks=i * 4 + s4, mk=None, q0=0) for s4 in range(4)]
            dg = [dict(ks=j * 4 + s4, mk=s4, q0=s4 * 128) for s4 in range(4)]
            g1 = [dict(dg[0], b=0, c=0), dict(dg[1], b=1, c=0), dict(dg[3], b=1, c=384)]
            g2 = [dict(dg[2], b=0, c=256), dict(past[0], b=1, c=0)]
            grs = [g1, g2]
            rest = past[1:]
            for k in range(0, len(rest), 2):
                grs.append([dict(e_, b=bi, c=0) for bi, e_ in enumerate(rest[k:k + 2])])
            for gi, g in enumerate(grs):
                flat.append(dict(hd=hd, j=j, ents=g, first=(gi == 0), last=(gi == len(grs) - 1)))
        ngr = len(flat)

        def qk(n):
            G = flat[n]
            hd, j = G["hd"], G["j"]
            Kt, Qt, rows = KT[hd], QT[hd], HROWS[hd]
            sb = n % 3
            for E in G["ents"]:
                ks, mk, q0 = E["ks"], E["mk"], E["q0"]
                c0 = E["b"] * 512 + E["c"]
                c1 = c0 + (512 - q0)
                kreads = [("K", hd, ks // 4), ("Kaug", hd), ("Q", hd, j), ("Qaug", hd)]
                A("pe", lambda e, ks=ks, c0=c0, c1=c1, mk=mk, q0=q0: e.matmul(
                    PS[sb][:, c0:c1], lhsT=Kt[rows, ks * 128:(ks + 1) * 128], rhs=Qt[rows, j * TS + q0:(j + 1) * TS],
                    start=True, stop=(mk is None)),
                  reads=kreads, writes=[("bank", 2 * sb + E["b"])])
                if mk is not None:
                    A("pe", lambda e, c0=c0, c1=c1, mk=mk, q0=q0: e.matmul(PS[sb][:, c0:c1], lhsT=IDENT, rhs=DMASK[:, mk, q0:512],
                                                                         start=False, stop=True),
                      reads=["CM", "DMASK"], writes=[("bank", 2 * sb + E["b"])])

        def ex(n):
            sb = n % 3
            pb = n % 3
            banks = sorted(set(E["b"] for E in flat[n]["ents"]))
            rngs = sorted((E["b"] * 512 + E["c"], E["b"] * 512 + E["c"] + 512 - E["q0"]) for E in flat[n]["ents"])
            merged = []
            for (r0, r1) in rngs:
                if merged and merged[-1][1] == r0:
                    merged[-1][1] = r1
                else:
                    merged.append([r0, r1])
            for (r0, r1) in merged:
                A("act", lambda e, r0=r0, r1=r1: e.activation(out=PT[pb][:, r0:r1], in_=PS[sb][:, r0:r1], func=AF.Exp),
                  reads=[("bank", 2 * sb + b_) for b_ in banks], writes=[("pt", pb)])

        def pv(n):
            G = flat[n]
            hd, j = G["hd"], G["j"]
            Vt, vc, ob = VT[hd], VCOLS[hd], 6 + hd
            pb = n % 3
            ne = len(G["ents"])
            for e_i, E in enumerate(G["ents"]):
                ks, q0 = E["ks"], E["q0"]
                c0 = E["b"] * 512 + E["c"]
                c1 = c0 + (512 - q0)
                first = G["first"] and e_i == 0
                last = G["last"] and e_i == ne - 1
                A("pe", lambda e, ks=ks, c0=c0, c1=c1, q0=q0, first=first, last=last: e.matmul(
                    bank(ob)[0:vc, q0:512], lhsT=Vt[:, ks, 0:vc], rhs=PT[pb][:, c0:c1], start=first, stop=last,
                    skip_group_check=True),
                  reads=[("pt", pb), ("V", hd, ks // 4), ("Vc", hd)], writes=[("bank", ob)])
            if G["last"]:
                finalize1(hd, j, ob)

        def finalize1(hd, j, ob):
            dr = DENROW[hd]
            mr = MROWS[hd]
            A("dve", lambda e: e.reciprocal(out=REC[hd][dr, :], in_=bank(ob)[dr, :]), reads=[("bank", ob)], writes=[("REC", hd)])
            A("dve", lambda e: e.tensor_copy(out=RH[hd][dr, :], in_=REC[hd][dr, :]), reads=[("REC", hd)], writes=[("RH", hd)])
            A("dve", lambda e: e.tensor_tensor(out=RL[hd][dr, :], in0=REC[hd][dr, :], in1=RH[hd][dr, :], op=ALU.subtract),
              reads=[("REC", hd), ("RH", hd)], writes=[("RL", hd)])
            A("dve", lambda e: e.tensor_tensor(out=YTMP[hd][mr, :], in0=bank(ob)[mr, :], in1=ZS[mr, j * TS:(j + 1) * TS], op=ALU.mult),
              reads=[("bank", ob), ("ZS", j)], writes=[("YTMP", hd)])

            def stage2():
                if hd == 0:
                    bl, br_ = ONESEL[0:65, :], slice(0, 65)
                else:
                    bl, br_ = ONES1[0:1, :], slice(0, 1)
                A("pe", lambda e: e.matmul(bank(ob), lhsT=bl, rhs=RH[hd][br_, :], start=True, stop=False),
                  reads=[("RH", hd), "RHc", "CM"], writes=[("bank", ob)])
                A("pe", lambda e: e.matmul(bank(ob), lhsT=bl, rhs=RL[hd][br_, :], start=False, stop=True),
                  reads=[("RL", hd), "RHc", "CM"], writes=[("bank", ob)])
                A("dve", lambda e: e.tensor_tensor(out=YT[mr, chunk, j * TS:(j + 1) * TS], in0=YTMP[hd][mr, :], in1=bank(ob)[mr, :], op=ALU.mult),
                  reads=[("YTMP", hd), ("bank", ob)], writes=[("YT", chunk, j)])
            pending.append([8, stage2])

        LOOK = 2
        for n in range(min(LOOK, ngr)):
            qk(n)
            ex(n)
        for n in range(ngr):
            if n + LOOK < ngr:
                qk(n + LOOK)
                ex(n + LOOK)
            for it in list(pending):
                it[0] -= 1
                if it[0] <= 0:
                    pending.remove(it)
                    it[1]()
            pv(n)

    pending = []

    def flush_pending():
        while pending:
            pending.pop(0)[1]()

    def gating_job():
        A("dve", lambda e: e.tensor_copy(out=KMH[:], in_=KM[:]), reads=[("KM", i) for i in range(NT)], writes=["KMH"])
        A("dve", lambda e: e.tensor_tensor(out=KMR[:], in0=KM[:], in1=KMH[:], op=ALU.subtract),
          reads=[("KM", i) for i in range(NT)] + ["KMH"], writes=["KMR"])
        A("dve", lambda e: e.tensor_copy(out=KML[:], in_=KMR[:]), reads=["KMR"], writes=["KML"])
        for hd in range(2):
            mr = MROWS[hd]
            A("dve", lambda e, hd=hd, mr=mr: e.tensor_copy(out=KMH2[mr, hd, :], in_=KMH[mr, :]), reads=["KMH", "KMH2c"], writes=[("KMH2", hd)])
            A("dve", lambda e, hd=hd, mr=mr: e.tensor_copy(out=KML2[mr, hd, :], in_=KML[mr, :]), reads=["KML", "KMH2c"], writes=[("KML2", hd)])
        yield
        gb_ = acquire()
        g4 = bank(gb_).rearrange("p (s h j) -> p s h j", h=2, j=16)
        for st in range(16):
            for hd in range(2):
                hr = HROWS[hd]
                A("pe", lambda e, st=st, hd=hd, hr=hr: e.matmul(g4[:, st, hd, :], lhsT=QT[hd][hr, st * 128:(st + 1) * 128], rhs=KMH2[hr, hd, :],
                                                                start=True, stop=False),
                  reads=[("Q", hd, st // 4), ("Qaug", hd), ("KMH2", hd), "KMH2c"], writes=[("bank", gb_)])
                A("pe", lambda e, st=st, hd=hd, hr=hr: e.matmul(g4[:, st, hd, :], lhsT=QT[hd][hr, st * 128:(st + 1) * 128], rhs=KML2[hr, hd, :],
                                                                start=False, stop=True),
                  reads=[("Q", hd, st // 4), ("Qaug", hd), ("KML2", hd), "KMH2c"], writes=[("bank", gb_)])
        yield
        A("dve", lambda e: e.tensor_tensor(out=GM[:], in0=bank(gb_), in1=MTAB[:, 0, :], op=ALU.add),
          reads=[("bank", gb_), "MTAB"], writes=["GM"])
        release(gb_)
        for grp in range(32):
            A("dve", lambda e, grp=grp: e.max(out=T8[:, grp, :], in_=GM[:, grp * 16:(grp + 1) * 16]),
              reads=["GM"], writes=[("T8", grp)])
        yield
        gm3 = GM[:].rearrange("p (g j) -> p g j", j=16)
        sel3 = SEL[:].rearrange("p (g j) -> p g j", j=16)
        A("dve", lambda e: e.tensor_tensor(out=sel3, in0=gm3, in1=T8[:, :, 2:3].to_broadcast([128, 32, 16]), op=ALU.is_ge),
          reads=["GM"] + [("T8", grp) for grp in range(32)], writes=["SEL"])
        A("dve", lambda e: e.tensor_tensor(out=SEL[:], in0=SEL[:], in1=MTAB[:, 1, :], op=ALU.mult), reads=["SEL", "MTAB"], writes=["SEL"])
        A("dve", lambda e: e.tensor_tensor(out=SEL[:], in0=SEL[:], in1=MTAB[:, 2, :], op=ALU.add), reads=["SEL", "MTAB"], writes=["SEL"])
        sel4 = SEL[:].rearrange("p (s h j) -> p s h j", h=2, j=16)
        for hd in range(2):
            a0 = AUG0[hd]
            A("dve", lambda e, hd=hd, a0=a0: e.tensor_scalar(out=PEN[hd][:, :, a0:a0 + 16], in0=sel4[:, :, hd, :], scalar1=-1.0, scalar2=BIG,
                                                             op0=ALU.add, op1=ALU.mult),
              reads=["SEL", "PENc"], writes=[("PEN", hd)])
        yield
        for g in range(4):
            for hd in range(2):
                M = 80 if hd == 0 else 16
                cr = slice(64, 80) if hd == 0 else slice(0, 16)
                tb = acquire()
                for s4 in range(4):
                    st = 4 * g + s4
                    A("pe", lambda e, st=st, s4=s4, hd=hd, M=M, tb=tb: e.matmul(bank(tb)[0:M, s4 * 128:(s4 + 1) * 128], lhsT=PEN[hd][:, st, :], rhs=IDENT,
                                                                              start=True, stop=True),
                      reads=[("PEN", hd), "PENc", "CM"], writes=[("bank", tb)])
                A("dve", lambda e, hd=hd, g=g, cr=cr, tb=tb: e.tensor_copy(out=QT[hd][cr, g * TS:(g + 1) * TS], in_=bank(tb)[cr, :]),
                  reads=[("bank", tb)], writes=[("Qaug", hd)])
                release(tb)
            yield

    def do_pair(br, hp):
        if (br, hp) != pairs[0]:
            load_pair_weights(br, hp)

        def c_rows():
            for hd in range(2):
                h = 2 * hp + hd
                a0 = AUG0[hd]
                dma("sp", KT[hd][a0:a0 + 3, :], cscr[h, 0:3, :], reads=["cscr"], writes=[("Kaug", hd)])
                dma("sp", QT[hd][a0 + 3:a0 + 6, :], cscr[h, 3:6, 0:NOWN * TS], reads=["cscr"], writes=[("Qaug", hd)])
        first = not state0["tail_done"]
        if br == 0 and not first:
            c_rows()
        jobs = []
        if first:
            jobs.append((phase_c_tail_job(), []))
            state0["tail_done"] = True
        if br == 0:
            for i in range(NT):
                jobs.append((qk_job(br, False, WK, "WK", i), []))
                jobs.append((v_job(i), []))
            for i in range(NOWN):
                jobs.append((qk_job(br, True, WQ, "WQ", i), []))
                jobs.append((z_job(i), []))
        else:
            kq = [qk_job(br, False, WK, "WK", i) for i in range(NT)] + [qk_job(br, True, WQ, "WQ", i) for i in range(NOWN)]
            jobs += [(g_, []) for g_ in kq]
            jobs.append((gating_job(), kq))
            for i in range(NT):
                jobs.append((v_job(i), []))
                if i < NOWN:
                    jobs.append((z_job(i), []))
        run_pipeline(jobs)
        if br == 0 and first:
            c_rows()
        if first:
            R.barrier()
            A("pool", lambda e: e.memset(KMH2[:], 0.0), writes=["KMH2c"])
            A("pool", lambda e: e.memset(KML2[:], 0.0), writes=["KMH2c"])
            A("pool", lambda e: e.memset(PEN[0][:, :, 0:64], 0.0), writes=["PENc"])
            dma("sp", MTAB[:], mtab_d, writes=["MTAB"])
        if (br, hp) == pairs[-1] and stop_after is None:
            dma("pool", WG[:, :, 0:1024], w3[:, :, 4096:5120], reads=[], writes=["WG0"] + MOBA_KEYS)
            dma("pool", WG[:, :, 1024:2048], w3[:, :, 5120:6144], reads=[], writes=["WG1"] + MOBA_KEYS)
            dma("pool", WF[:], w_fox.rearrange("(c p) n -> p c n", p=128), writes=["WF", "WQ", "WK", "WZ", "WV"])
        attention_pair(br, hp)

    last_br = None
    state0 = {"tail_done": False}
    if pairs[0][0] != 0:
        run_pipeline([(phase_c_tail_job(), [])])
        state0["tail_done"] = True
    for (br, hp) in pairs:
        if br != last_br:
            for hd in range(2):
                a0 = AUG0[hd]
                if br == 0:
                    dma("sp", KT[hd][a0 + 3:a0 + 16, :], kac_d[3:16, :], writes=[("Kaug", hd)])
                    dma("sp", QT[hd][a0:a0 + 3, :], qac_d[0:3, :], writes=[("Qaug", hd)])
                    dma("sp", QT[hd][a0 + 6:a0 + 16, :], qac_d[6:16, :], writes=[("Qaug", hd)])
                else:
                    dma("sp", KT[hd][a0:a0 + 16, :], kam_d[:, :], writes=[("Kaug", hd)])
            last_br = br
        do_pair(br, hp)
    flush_pending()
    R.barrier()
    if "YT" in dbg_out:
        for c in sorted(set(b_ * 4 + h_ for (b_, h_) in pairs)):
            dma("sp", dbg_out["YT"][:, c, :], YT[:, c, :], reads=[("YT", c, j) for j in range(NOWN)])
    if "KA" in dbg_out:
        dma("sp", dbg_out["KA"][0:80, :], KT[0][0:80, :], reads=[("K", 0, i) for i in range(NT)] + [("Kaug", 0)])
        dma("sp", dbg_out["KB"], KT[1][:], reads=[("K", 1, i) for i in range(NT)] + [("Kaug", 1)])
        dma("sp", dbg_out["QA"][0:80, :], QT[0][0:80, :], reads=[("Q", 0, i) for i in range(NOWN)] + [("Qaug", 0)])
        dma("sp", dbg_out["QB"], QT[1][:], reads=[("Q", 1, i) for i in range(NOWN)] + [("Qaug", 1)])
        dma("sp", dbg_out["VB"], VT[1][:], reads=[("V", 1, g) for g in range(8)] + [("Vc", 1)])
        dma("sp", dbg_out["ZS"], ZS[:], reads=[("ZS", i) for i in range(NOWN)])
    if stop_after == "p1":
        R.emit(nc)
        return nc

    dma("pool", WM[:], w_moba.rearrange("(c p) n -> p c n", p=128), writes=["WM"])
    dma("pool", WO[:], w_out.rearrange("(c p) n -> p c n", p=128), writes=["WO"])
    rr = [0]
    xr = [0]
    for j in range(NOWN):
        for n in range(8):
            r = rr[0]
            rr[0] = 1 - r
            ba, bb, bf_, bm = next_bank(), next_bank(), next_bank(), next_bank()
            for gi, (bk, dst) in enumerate([(ba, SA), (bb, SB)]):
                for c in range(8):
                    A("pe", lambda e, c=c, gi=gi, bk=bk, n=n, j=j: e.matmul(
                        bank(bk), lhsT=WG[:, c, gi * 1024 + n * 128: gi * 1024 + (n + 1) * 128], rhs=HT[:, c, j * TS:(j + 1) * TS],
                        start=(c == 0), stop=(c == 7)),
                      reads=["WG%d" % gi, ("HT", j, c)], writes=[("bank", bk)])
                A("act", lambda e, gi=gi, bk=bk, n=n, dst=dst, r=r: e.activation(
                    out=dst[r][:], in_=bank(bk), func=AF.Sigmoid, bias=BGATE[:, gi * 8 + n: gi * 8 + n + 1], scale=1.0),
                  reads=[("bank", bk), "BGATE"], writes=[("sg", gi, r)])
            for (bk, W, wk, c0) in [(bf_, WF, "WF", 0), (bm, WM, "WM", 4)]:
                for c in range(4):
                    A("pe", lambda e, c=c, bk=bk, W=W, c0=c0, n=n, j=j: e.matmul(
                        bank(bk), lhsT=W[:, c, n * 128:(n + 1) * 128], rhs=YT[:, c0 + c, j * TS:(j + 1) * TS],
                        start=(c == 0), stop=(c == 3)),
                      reads=[wk] + [("YT", c0 + cc, j) for cc in range(4)], writes=[("bank", bk)])
            A("dve", lambda e, r=r, bf_=bf_: e.tensor_tensor(out=TT[r][:], in0=bank(bf_), in1=SA[r][:], op=ALU.mult),
              reads=[("bank", bf_), ("sg", 0, r)], writes=[("tt", r)])
            A("dve", lambda e, r=r, bm=bm: e.tensor_tensor(out=SB[r][:], in0=bank(bm), in1=SB[r][:], op=ALU.mult),
              reads=[("bank", bm), ("sg", 1, r)], writes=[("sg", 1, r)])
            A("dve", lambda e, r=r, n=n: e.tensor_tensor(out=MTT[:, n, :], in0=TT[r][:], in1=SB[r][:], op=ALU.add),
              reads=[("tt", r), ("sg", 1, r)], writes=[("mtt", n)])
        for ts4 in range(4):
            x = xr[0]
            xr[0] = 1 - x
            row0 = (j * 4 + ts4) * 128
            dma("sp", XO[x][:], xo[row0:row0 + 128, :], writes=[("xo", x)])
            for half in range(2):
                bo = next_bank()
                for c in range(8):
                    A("pe", lambda e, c=c, bo=bo, half=half, ts4=ts4: e.matmul(
                        bank(bo), lhsT=MTT[:, c, ts4 * 128:(ts4 + 1) * 128], rhs=WO[:, c, half * 512:(half + 1) * 512],
                        start=(c == 0), stop=(c == 7)),
                      reads=["WO"] + [("mtt", cc) for cc in range(8)], writes=[("bank", bo)])
                A("dve", lambda e, x=x, bo=bo, half=half: e.tensor_tensor(
                    out=OT[x][:, half * 512:(half + 1) * 512], in0=bank(bo), in1=XO[x][:, half * 512:(half + 1) * 512], op=ALU.add),
                  reads=[("bank", bo), ("xo", x)], writes=[("ot", x, half)])
            dma("sp", out_d[row0:row0 + 128, :], OT[x][:], reads=[("ot", x, 0), ("ot", x, 1)], writes=[("outd", row0)])
    R.emit(nc)
    return nc


def make_in_maps(inputs):
    x = np.asarray(inputs["x"], np.float32)
    w_in = np.ascontiguousarray(np.asarray(inputs["w_in"], np.float32)[0])
    w_fox = np.ascontiguousarray(np.asarray(inputs["w_fox"], np.float32)[0])
    w_moba = np.ascontiguousarray(np.asarray(inputs["w_moba"], np.float32)[0])
    w_out = np.ascontiguousarray(np.asarray(inputs["w_out"], np.float32)[0])
    gng = np.ascontiguousarray(np.asarray(inputs["norm_g"], np.float32)[0].reshape(8, 128).T)
    gains = np.ascontiguousarray(np.stack([
        np.tile(np.asarray(inputs["fox_q_g"], np.float32)[0], 2),
        np.tile(np.asarray(inputs["fox_k_g"], np.float32)[0], 2),
        np.tile(np.asarray(inputs["moba_q_g"], np.float32)[0], 2),
        np.tile(np.asarray(inputs["moba_k_g"], np.float32)[0], 2)], axis=1))
    bg = np.asarray(inputs["b_gate"], np.float32)[0]
    bgate = np.ascontiguousarray(bg.reshape(2, 8, 128).transpose(2, 0, 1).reshape(128, 16))
    bf = np.ascontiguousarray(np.asarray(inputs["b_f"], np.float32)[0].reshape(8, 1))
    tabs = [const_tables(p) for p in range(2)]
    maps = []
    for core in range(8):
        b, p = core // 2, core % 2
        pos = storage_pos(p)
        xs = x[b][pos]
        m = dict(xT=np.ascontiguousarray(xs.T), xo=np.ascontiguousarray(xs[:NOWN * TS]),
                 w_in=w_in, w_fox=w_fox, w_moba=w_moba, w_out=w_out, gng=gng, gains=gains,
                 bgate=bgate, bf=bf)
        m.update(tabs[p])
        maps.append(m)
    return maps


def kernel(**inputs):
    maps = make_in_maps(inputs)
    nc = build_program()
    res = run_bass_kernel_spmd(nc, maps, core_ids=list(range(8)))
    out = np.zeros((NBATCH, SEQ, D), np.float32)
    for core in range(8):
        b, p = core // 2, core % 2
        pos = storage_pos(p)
        out[b, pos[:NOWN * TS]] = res.results[core]["out"]
    return out
```
